# Optimizing a Trainium2 kernel written in Bass

```python
import math
import jax, jax.numpy as jnp
from jax import lax
import numpy as np

D_MODEL = 1024
BATCH = 8
SEQ = 2048
DEPTH = 1
DEC_BATCH = 2
DEC_SEQ = 16384
PAST_LEN = 128

D_CONV = 512
CONV_K = 3
N_DIFF_HEADS = 4
DIFF_HEAD_DIM = 64
D_ATTN = N_DIFF_HEADS * 2 * DIFF_HEAD_DIM
ROT_DIM = DIFF_HEAD_DIM // 4
ROPE_THETA = 500000.0
Q_BLOCK = 128
EPS = 1e-6
IN_COLS = 4 * D_CONV + 4 * D_ATTN + 2 * D_MODEL

kernel_name = "hybrid_conv_diffattn_encoder"


def rms_norm(x, g):
    xf = x.astype(jnp.float32)
    y = xf * lax.rsqrt(jnp.mean(xf * xf, axis=-1, keepdims=True) + EPS)
    return (y * g.astype(jnp.float32)).astype(x.dtype)


def rope_partial(x, pos):
    half = ROT_DIM // 2
    inv = ROPE_THETA ** (-jnp.arange(half, dtype=jnp.float32) / half)
    ang = pos.astype(jnp.float32)[:, None] * inv[None, :]
    cos, sin = jnp.cos(ang), jnp.sin(ang)
    xr = x[..., :ROT_DIM].astype(jnp.float32)
    x1, x2 = xr[..., :half], xr[..., half:]
    rot = jnp.concatenate([x1 * cos - x2 * sin, x2 * cos + x1 * sin], axis=-1).astype(x.dtype)
    return jnp.concatenate([rot, x[..., ROT_DIM:]], axis=-1)


def depthwise_conv_centred(u, w):
    pad = (CONV_K - 1) // 2
    return lax.conv_general_dilated(
        u, w[:, None, :].astype(u.dtype), window_strides=(1,), padding=[(pad, pad)],
        dimension_numbers=('NWC', 'WIO', 'NWC'), feature_group_count=u.shape[-1])


def diff_attention(q, k, v, lam):
    B, H, _, S, Dh = q.shape
    nb = S // Q_BLOCK
    scale = Dh ** -0.5
    qb = q.reshape(B, H, 2, nb, Q_BLOCK, Dh).transpose(3, 0, 1, 2, 4, 5)

    def one_block(qblk):
        s = jnp.einsum('bhcqd,bhckd->bhcqk', qblk, k,
                       preferred_element_type=jnp.float32) * scale
        p = jax.nn.softmax(s, axis=-1)
        a = p[:, :, 0] - lam * p[:, :, 1]
        o = jnp.einsum('bhqk,bhkv->bhqv', a.astype(v.dtype), v,
                       preferred_element_type=jnp.float32)
        return o.astype(v.dtype)

    out = lax.map(one_block, qb)
    return out.transpose(1, 2, 0, 3, 4).reshape(B, H, S, 2 * Dh)


def encoder_layer(x, cond, layer_idx, norm_g, w_ada, b_ada, w_in, conv_w, q_norm_g, k_norm_g,
                  lam_q1, lam_k1, lam_q2, lam_k2, subln_g, w_conv_out, w_attn_out, w_out):
    B, S, _ = x.shape
    H, Dh = N_DIFF_HEADS, DIFF_HEAD_DIM
    mod = jax.nn.silu(cond) @ w_ada + b_ada
    shift, scale, gate = jnp.split(mod[:, None, :], 3, axis=-1)
    h = rms_norm(x, norm_g) * (1 + scale) + shift

    proj = h @ w_in
    sizes = [D_CONV] * 4 + [D_ATTN] * 4 + [D_MODEL] * 2
    cuts = np.cumsum(sizes)[:-1].tolist()
    cb, cc, cx, cz, q, k, v, az, ga, gb = jnp.split(proj, cuts, axis=-1)

    y_conv = cb * depthwise_conv_centred(cc * cx, conv_w)
    y_conv = y_conv * jax.nn.silu(cz)
    branch_a = y_conv @ w_conv_out

    pos = jnp.arange(S)
    q = q.reshape(B, S, H, 2, Dh).transpose(0, 2, 3, 1, 4)
    k = k.reshape(B, S, H, 2, Dh).transpose(0, 2, 3, 1, 4)
    v = v.reshape(B, S, H, 2 * Dh).transpose(0, 2, 1, 3)
    q = rope_partial(rms_norm(q, q_norm_g), pos)
    k = rope_partial(rms_norm(k, k_norm_g), pos)
    lam_init = 0.8 - 0.6 * math.exp(-0.3 * layer_idx)
    lam = (jnp.exp(jnp.sum(lam_q1.astype(jnp.float32) * lam_k1.astype(jnp.float32)))
           - jnp.exp(jnp.sum(lam_q2.astype(jnp.float32) * lam_k2.astype(jnp.float32)))
           + lam_init)
    o = diff_attention(q, k, v, lam)
    o = rms_norm(o, subln_g) * (1.0 - lam_init)
    o = o.transpose(0, 2, 1, 3).reshape(B, S, D_ATTN) * jax.nn.silu(az)
    branch_b = o @ w_attn_out

    merged = jax.nn.sigmoid(ga) * branch_a + jax.nn.sigmoid(gb) * branch_b
    return x + gate * (merged @ w_out)


def setup_inputs(seed: int = 0) -> dict:
    key = jax.random.key(seed)
    ks = jax.random.split(key, 20)
    f32 = jnp.float32
    D = D_MODEL
    nrm = lambda k, shape: jax.random.normal(k, shape, f32)
    return {
        "x_prompt": nrm(ks[0], (BATCH, SEQ, D)),
        "x_sample": nrm(ks[1], (DEC_BATCH, DEC_SEQ, D)),
        "c_prompt": nrm(ks[2], (BATCH, D)),
        "c_sample": nrm(ks[3], (DEC_BATCH, D)),
        "norm_g": 1.0 + 0.02 * nrm(ks[4], (DEPTH, D)),
        "w_ada": 0.5 * D ** -0.5 * nrm(ks[5], (DEPTH, D, 3 * D)),
        "b_ada": 0.01 * nrm(ks[6], (DEPTH, 3 * D)),
        "w_in": D ** -0.5 * nrm(ks[7], (DEPTH, D, IN_COLS)),
        "conv_w": CONV_K ** -0.5 * nrm(ks[8], (DEPTH, CONV_K, D_CONV)),
        "q_norm_g": 1.0 + 0.02 * nrm(ks[9], (DEPTH, DIFF_HEAD_DIM)),
        "k_norm_g": 1.0 + 0.02 * nrm(ks[10], (DEPTH, DIFF_HEAD_DIM)),
        "lam_q1": 0.1 * nrm(ks[11], (DEPTH, DIFF_HEAD_DIM)),
        "lam_k1": 0.1 * nrm(ks[12], (DEPTH, DIFF_HEAD_DIM)),
        "lam_q2": 0.1 * nrm(ks[13], (DEPTH, DIFF_HEAD_DIM)),
        "lam_k2": 0.1 * nrm(ks[14], (DEPTH, DIFF_HEAD_DIM)),
        "subln_g": 1.0 + 0.02 * nrm(ks[15], (DEPTH, 2 * DIFF_HEAD_DIM)),
        "w_conv_out": D_CONV ** -0.5 * nrm(ks[16], (DEPTH, D_CONV, D)),
        "w_attn_out": D_ATTN ** -0.5 * nrm(ks[17], (DEPTH, D_ATTN, D)),
        "w_out": D ** -0.5 * nrm(ks[18], (DEPTH, D, D)),
    }


def reference(x_prompt, x_sample, c_prompt, c_sample, norm_g, w_ada, b_ada, w_in, conv_w,
              q_norm_g, k_norm_g, lam_q1, lam_k1, lam_q2, lam_k2, subln_g,
              w_conv_out, w_attn_out, w_out):
    def trunk(x, cond):
        for l in range(DEPTH):
            x = encoder_layer(x, cond, l, norm_g[l], w_ada[l], b_ada[l], w_in[l], conv_w[l],
                              q_norm_g[l], k_norm_g[l], lam_q1[l], lam_k1[l], lam_q2[l],
                              lam_k2[l], subln_g[l], w_conv_out[l], w_attn_out[l], w_out[l])
        return x

    y_prompt = trunk(x_prompt, c_prompt)
    y_sample = trunk(x_sample, c_sample)
    return (y_prompt, y_sample)
```

```python
import math
import contextlib
import numpy as np
import concourse.bass as bass
import concourse.mybir as mybir
from concourse.bass_utils import run_bass_kernel_spmd

F32 = mybir.dt.float32
BF16 = mybir.dt.bfloat16
AF = mybir.ActivationFunctionType
ALU = mybir.AluOpType
AX = mybir.AxisListType

D = 1024
KC = 8
EPS = 1e-6
LAM_INIT = 0.8 - 0.6 * math.exp(-0.3 * 0)
SBUF_LO = 16512
SBUF_HI = 229344
G1 = 512
G3 = 256
QT = 512


class Buf:
    def __init__(self, ap):
        self.ap = ap
        self.wr = {}
        self.rd = {}


class Prog:
    def __init__(self):
        self.q = {e: [] for e in ("sync", "pe", "act", "dve", "pool")}
        self.cnt = {}
        self.waited = {e: {} for e in self.q}

    def wait(self, eng, tok):
        if tok is None:
            return
        s, v = tok
        if self.waited[eng].get(s, 0) >= v:
            return
        self.waited[eng][s] = v
        self.q[eng].append(("w", s, v))

    def _hz(self, reads, writes, extra):
        waits = list(extra)
        for b in reads:
            waits += list(b.wr.items())
        for b in writes:
            waits += list(b.rd.items())
            if not b.rd:
                waits += list(b.wr.items())
        return waits

    def _reg(self, tok, reads, writes, addwrites):
        s, v = tok
        for b in reads:
            b.rd[s] = max(b.rd.get(s, 0), v)
        for b in writes:
            b.wr = {s: v}
            b.rd = {}
        for b in addwrites:
            b.wr[s] = max(b.wr.get(s, 0), v)

    def op(self, eng, fn, reads=(), writes=(), addwrites=(), extra=()):
        for t in self._hz(reads, tuple(writes) + tuple(addwrites), extra):
            self.wait(eng, t)
        s = "s_" + eng
        self.cnt[s] = self.cnt.get(s, 0) + 1
        tok = (s, self.cnt[s])
        self.q[eng].append(("o", fn, tok))
        self._reg(tok, reads, writes, addwrites)
        return tok

    def group(self, eng, fns, reads=(), writes=(), addwrites=(), extra=()):
        for t in self._hz(reads, tuple(writes) + tuple(addwrites), extra):
            self.wait(eng, t)
        for fn in fns[:-1]:
            self.q[eng].append(("o", fn, None))
        s = "s_" + eng
        self.cnt[s] = self.cnt.get(s, 0) + 1
        tok = (s, self.cnt[s])
        self.q[eng].append(("o", fns[-1], tok))
        self._reg(tok, reads, writes, addwrites)
        return tok

    def dma(self, out, in_, sem, reads=(), writes=(), addwrites=(), extra=(), eng="sync"):
        for t in self._hz(reads, tuple(writes) + tuple(addwrites), extra):
            self.wait(eng, t)
        self.cnt[sem] = self.cnt.get(sem, 0) + 16
        tok = (sem, self.cnt[sem])
        self.q[eng].append(("d", out, in_, tok))
        self._reg(tok, reads, writes, addwrites)
        return tok

    def barrier(self):
        toks = [(s, v) for s, v in self.cnt.items()]
        for e in self.q:
            for t in toks:
                self.wait(e, t)

    def replay(self, nc):
        for s, v in self.cnt.items():
            assert v < 60000, (s, v)
        with contextlib.ExitStack() as st:
            sems = {name: st.enter_context(nc.semaphore(name)) for name in sorted(self.cnt)}
            block = st.enter_context(nc.Block())

            def run(items):
                def f(eng):
                    for it in items:
                        if it[0] == "w":
                            eng.wait_ge(sems[it[1]], it[2])
                        elif it[0] == "o":
                            ins = it[1](eng)
                            if it[2] is not None:
                                ins.then_inc(sems[it[2][0]], 1)
                        else:
                            eng.dma_start(out=it[1], in_=it[2]).then_inc(sems[it[3][0]], 16)
                return f

            block.sync(run(self.q["sync"]))
            block.tensor(run(self.q["pe"]))
            block.scalar(run(self.q["act"]))
            block.vector(run(self.q["dve"]))
            block.gpsimd(run(self.q["pool"]))


class Ring:
    def __init__(self, items):
        self.items = items
        self.i = 0

    def get(self):
        b = self.items[self.i % len(self.items)]
        self.i += 1
        return b


def build(jobs):
    nc = bass.Bass("TRN2", target_bir_lowering=False)
    P = Prog()
    uid = [0]

    def din(name, shape, dt=F32):
        return nc.dram_tensor(name, list(shape), dt, kind="ExternalInput").ap()

    w_in = din("w_in", [D, 6144]).rearrange("(k p) c -> p k c", p=128)
    w_ada = din("w_ada", [D, 3072]).rearrange("(k p) c -> p k c", p=128)
    b_ada = din("b_ada", [3072])
    norm_g_d = din("norm_g_l", [128, 8])
    conv_d = din("conv_l", [128, 12])
    qg_d = din("qg", [64])
    kg_d = din("kg", [64])
    lam_d = din("lamv", [256])
    subln_d = din("subln", [128, 1])
    wco_d = din("w_conv_out", [512, D]).rearrange("(k p) c -> p k c", p=128)
    wao_d = din("w_attn_out", [512, D]).rearrange("(k p) c -> p k c", p=128)
    wo_d = din("w_out", [D, D]).rearrange("(k p) c -> p k c", p=128)
    ident_d = din("ident", [128, 128])
    J = []
    for j, cfg in enumerate(jobs):
        S, NQ = cfg["S"], cfg["NQ"]
        ng3 = NQ // G3
        jd = dict(S=S, NQ=NQ, ng3=ng3)
        jd["xf"] = din(f"xf{j}", [S, D])
        jd["xq"] = din(f"xq{j}", [NQ, D])
        jd["halo"] = din(f"halo{j}", [2 * ng3, D])
        jd["hmask"] = din(f"hmask{j}", [128, 2 * ng3])
        jd["ropek"] = din(f"ropek{j}", [S, 16]).rearrange("(t p) c -> p t c", p=128)
        jd["ropeq"] = din(f"ropeq{j}", [NQ, 16]).rearrange("(t p) c -> p t c", p=128)
        jd["cl"] = din(f"cl{j}", [128, 8])
        jd["y"] = nc.dram_tensor(f"y{j}", [NQ, D], F32, kind="ExternalOutput").ap()
        jd["kt_d"] = nc.dram_tensor(f"kt_d{j}", [4, 128, S], BF16).ap()
        jd["v_d"] = nc.dram_tensor(f"v_d{j}", [4, 128, S], BF16).ap()
        jd["qt_d"] = nc.dram_tensor(f"qt_d{j}", [4, 128, NQ], BF16).ap()
        jd["o_d"] = nc.dram_tensor(f"o_d{j}", [4, 128, NQ], BF16).ap()
        J.append(jd)

    class Arena:
        def __init__(self, lo, hi):
            self.lo, self.hi, self.off = lo, hi, lo

        def reset(self):
            self.off = self.lo

        def alloc(self, shape, dt):
            uid[0] += 1
            nb = int(np.prod(shape[1:])) * (4 if dt == F32 else 2)
            off = (self.off + 63) // 64 * 64
            assert off + nb <= self.hi, ("SBUF overflow", shape, off, nb, self.hi)
            self.off = off + nb
            h = nc.alloc_sbuf_tensor_at(f"t{uid[0]}", list(shape), dt, offset=off)
            return Buf(h[:])

    pers = Arena(SBUF_LO, SBUF_LO + 16 * 1024)
    ph = Arena(SBUF_LO + 16 * 1024, SBUF_HI)

    ps_h = nc.alloc_psum_tensor("ps", [128, 4096], F32)
    ps = ps_h[:]
    ps_bf = ps.bitcast(BF16)

    def bank(b, n=512, off=0):
        return ps[:, b * 512 + off: b * 512 + off + n]

    def bank_bf(b, n=1024, off=0):
        return ps_bf[:, b * 1024 + off: b * 1024 + off + n]

    banks = [Buf(bank(b)) for b in range(8)]

    def MM(out, lhsT, rhs, start, stop, tp=None):
        if tp is None:
            return lambda e: e.matmul(out, lhsT=lhsT, rhs=rhs, start=start, stop=stop)
        return lambda e: e.matmul(out, lhsT=lhsT, rhs=rhs, start=start, stop=stop, tile_position=tp)

    def TR(out, in_, ident):
        return lambda e: e.transpose(out, in_, ident)

    def ACT(out, in_, func, scale=1.0, accum=None, bias=None):
        if bias is not None:
            return lambda e: e.activation(out=out, in_=in_, func=func, scale=scale, bias=bias)
        if accum is None:
            return lambda e: e.activation(out=out, in_=in_, func=func, scale=scale)
        return lambda e: e.activation(out=out, in_=in_, func=func, scale=scale, accum_out=accum)

    def TT(out, in0, in1, op):
        return lambda e: e.tensor_tensor(out=out, in0=in0, in1=in1, op=op)

    def TS(out, in0, s1, op0, s2=None, op1=None):
        if op1 is None:
            return lambda e: e.tensor_scalar(out=out, in0=in0, scalar1=s1, scalar2=None, op0=op0)
        return lambda e: e.tensor_scalar(out=out, in0=in0, scalar1=s1, scalar2=s2, op0=op0, op1=op1)

    def STT(out, in0, scalar, in1, op0, op1):
        return lambda e: e.scalar_tensor_tensor(out=out, in0=in0, scalar=scalar, in1=in1, op0=op0, op1=op1)

    def CP(out, in_):
        return lambda e: e.tensor_copy(out=out, in_=in_)

    def RED(out, in_):
        return lambda e: e.tensor_reduce(out=out, in_=in_, axis=AX.X, op=ALU.add)

    def RCP(out, in_):
        return lambda e: e.reciprocal(out=out, in_=in_)

    def MS(ap, val):
        return lambda e: e.memset(ap, val)

    def g3(ap, b):
        return ap.rearrange("p (a b) -> p a b", b=b)

    ident_f = pers.alloc([128, 128], F32)
    ident_b = pers.alloc([128, 128], BF16)
    ones_b = pers.alloc([128, 128], BF16)
    neghalf = pers.alloc([128, 512], F32)
    norm_g = pers.alloc([128, 8], F32)
    conv_l = pers.alloc([128, 12], F32)
    qg_b = pers.alloc([128, 64], F32)
    kg_b = pers.alloc([128, 64], F32)
    lamv = pers.alloc([128, 256], F32)
    subln = pers.alloc([128, 1], F32)
    sgs = pers.alloc([128, 1], F32)
    neg_lam = pers.alloc([128, 1], F32)
    lamt = pers.alloc([128, 128], F32)
    lams = pers.alloc([128, 4], F32)
    ones_f = pers.alloc([128, 128], F32)
    eps_t = pers.alloc([128, 1], F32)

    P.dma(ident_f.ap, ident_d, "cld", writes=[ident_f])
    P.dma(norm_g.ap, norm_g_d, "cld", writes=[norm_g])
    P.dma(conv_l.ap, conv_d, "cld", writes=[conv_l])
    P.dma(qg_b.ap, qg_d.partition_broadcast(128), "cld", writes=[qg_b])
    P.dma(kg_b.ap, kg_d.partition_broadcast(128), "cld", writes=[kg_b])
    P.dma(lamv.ap, lam_d.partition_broadcast(128), "cld", writes=[lamv])
    P.dma(subln.ap, subln_d, "cld", writes=[subln])
    for jd in J:
        jd["Gp"] = pers.alloc([128, 8], F32)
        jd["shiftT"] = pers.alloc([128, 8], F32)
        jd["gate_b"] = pers.alloc([128, D], F32)
        jd["hmask_s"] = pers.alloc([128, 2 * jd["ng3"]], F32)
        jd["rstd_q"] = pers.alloc([128, jd["NQ"] // 128], F32)
        P.dma(jd["hmask_s"].ap, jd["hmask"], "cld", writes=[jd["hmask_s"]])
    P.barrier()
    P.op("pool", MS(ones_b.ap, 1.0), writes=[ones_b])
    P.op("pool", MS(neghalf.ap, -0.5), writes=[neghalf])
    P.op("pool", MS(ones_f.ap, 1.0), writes=[ones_f])
    P.op("pool", MS(eps_t.ap, EPS), writes=[eps_t])
    P.op("dve", TS(conv_l.ap, conv_l.ap, 0.5, ALU.mult), reads=[conv_l], writes=[conv_l])
    P.op("dve", CP(ident_b.ap, ident_f.ap), reads=[ident_f], writes=[ident_b])
    P.op("dve", TT(lamt.ap[:, 0:64], lamv.ap[:, 0:64], lamv.ap[:, 64:128], ALU.mult), reads=[lamv], writes=[lamt])
    P.op("dve", TT(lamt.ap[:, 64:128], lamv.ap[:, 128:192], lamv.ap[:, 192:256], ALU.mult), reads=[lamv], addwrites=[lamt])
    P.op("dve", RED(lams.ap[:, 0:2], g3(lamt.ap, 64)), reads=[lamt], writes=[lams])
    P.op("act", ACT(lams.ap[:, 2:4], lams.ap[:, 0:2], AF.Exp), reads=[lams], addwrites=[lams])
    P.op("dve", STT(neg_lam.ap, lams.ap[:, 3:4], -LAM_INIT, lams.ap[:, 2:3], ALU.add, ALU.subtract), reads=[lams], writes=[neg_lam])
    P.op("dve", TS(sgs.ap, subln.ap, 0.5 * (1.0 - LAM_INIT), ALU.mult), reads=[subln], writes=[sgs])

    ph.reset()
    wada = ph.alloc([128, KC, 3072], F32)
    bada = ph.alloc([128, 3072], F32)
    for kc in range(KC):
        P.dma(wada.ap[:, kc, :], w_ada[:, kc, :], "wld", addwrites=[wada])
    P.dma(bada.ap, b_ada.partition_broadcast(128), "bld", writes=[bada])
    for j, jd in enumerate(J):
        cl = ph.alloc([128, 8], F32)
        ce = ph.alloc([128, 8], F32)
        sc = ph.alloc([128, 8], F32)
        screp = ph.alloc([128, KC, 128], F32)
        modb = ph.alloc([128, 3072], F32)
        scl = ph.alloc([128, 8], F32)
        P.dma(cl.ap, jd["cl"], f"cl{j}", writes=[cl])
        P.op("act", ACT(ce.ap, cl.ap, AF.Exp, scale=-1.0), reads=[cl], writes=[ce])
        P.op("dve", TS(ce.ap, ce.ap, 1.0, ALU.add), reads=[ce], writes=[ce])
        P.op("dve", RCP(ce.ap, ce.ap), reads=[ce], writes=[ce])
        P.op("dve", TT(sc.ap, cl.ap, ce.ap, ALU.mult), reads=[cl, ce], writes=[sc])
        P.op("dve", CP(screp.ap, sc.ap.unsqueeze(2).to_broadcast([128, KC, 128])), reads=[sc], writes=[screp])
        for cg in range(6):
            bk = banks[cg]
            fns = [MM(bank(cg), screp.ap[:, kc, :], wada.ap[:, kc, cg * 512:(cg + 1) * 512], kc == 0, kc == KC - 1) for kc in range(KC)]
            P.group("pe", fns, reads=[screp, wada], writes=[bk])
            P.op("dve", TT(modb.ap[:, cg * 512:(cg + 1) * 512], bank(cg), bada.ap[:, cg * 512:(cg + 1) * 512], ALU.add),
                 reads=[bk, bada], addwrites=[modb])
        P.op("dve", TS(jd["gate_b"].ap, modb.ap[:, 2048:3072], 0.5, ALU.mult), reads=[modb], writes=[jd["gate_b"]])
        for half in range(2):
            fns = [TR(ps[:, 6 * 512 + blk * 128: 6 * 512 + (blk + 1) * 128], modb.ap[:, half * 1024 + blk * 128: half * 1024 + (blk + 1) * 128], ident_f.ap)
                   for blk in range(8)]
            P.group("pe", fns, reads=[modb, ident_f], writes=[banks[6], banks[7]])
            src = g3(ps[:, 6 * 512: 8 * 512], 128)[:, :, 0]
            dst = jd["shiftT"] if half == 0 else scl
            P.op("dve", CP(dst.ap, src), reads=[banks[6], banks[7]], writes=[dst])
        P.op("dve", STT(jd["Gp"].ap, scl.ap, 1.0, norm_g.ap, ALU.add, ALU.mult), reads=[scl, norm_g], writes=[jd["Gp"]])
    P.barrier()

    def make_norm_bufs(G, nxn=2):
        nt = G // 128
        return dict(
            G=G, nt=nt,
            xg=Ring([ph.alloc([128, nt, D], F32) for _ in range(2)]),
            xn=Ring([ph.alloc([128, nt, D], BF16) for _ in range(nxn)]),
            hT=Ring([ph.alloc([128, KC, G], BF16) for _ in range(2)]),
            ss=Ring([ph.alloc([128, 8], F32) for _ in range(2)]),
            rs=Ring([ph.alloc([128, 8], F32) for _ in range(2)]),
            junk=ph.alloc([128, D], BF16),
            xsem=Ring(["xld0", "xld1"]),
            tb=Ring([0, 1]),
        )

    def load_group(nb, x_ap, g):
        G = nb["G"]
        xg = nb["xg"].get()
        P.dma(xg.ap, x_ap[g * G:(g + 1) * G, :].rearrange("(t p) d -> p t d", p=128), nb["xsem"].get(), writes=[xg])
        return xg

    def norm_group(nb, jd, xg, keep=None, pre=None):
        return norm_b(nb, jd, norm_a(nb, jd, xg, keep=keep, pre=pre))

    def norm_a(nb, jd, xg, keep=None, pre=None):
        G, nt = nb["G"], nb["nt"]
        xn, ss = nb["xn"].get(), nb["ss"].get()
        junk = nb["junk"]
        if pre is not None:
            rs, c0 = pre
        else:
            for i in range(nt):
                P.op("act", ACT(junk.ap, xg.ap[:, i, :], AF.Square, accum=ss.ap[:, i:i + 1]),
                     reads=[xg], addwrites=[ss] if i else (), writes=() if i else [ss])
            if keep is not None:
                rs, c0 = keep
                P.op("act", ACT(rs.ap[:, c0:c0 + nt], ss.ap[:, 0:nt], AF.Sqrt, scale=1.0 / D, bias=eps_t.ap[:, 0:1]), reads=[ss, eps_t], addwrites=[rs])
                P.op("dve", RCP(rs.ap[:, c0:c0 + nt], rs.ap[:, c0:c0 + nt]), reads=[rs], addwrites=[rs])
            else:
                rs, c0 = nb["rs"].get(), 0
                P.op("act", ACT(rs.ap[:, 0:nt], ss.ap[:, 0:nt], AF.Sqrt, scale=1.0 / D, bias=eps_t.ap[:, 0:1]), reads=[ss, eps_t], writes=[rs])
                P.op("dve", RCP(rs.ap[:, 0:nt], rs.ap[:, 0:nt]), reads=[rs], writes=[rs])
        for i in range(nt):
            P.op("dve", TS(xn.ap[:, i, :], xg.ap[:, i, :], rs.ap[:, c0 + i:c0 + i + 1], ALU.mult),
                 reads=[xg, rs], addwrites=[xn] if i else (), writes=() if i else [xn])
        return xn

    def norm_b(nb, jd, xn):
        G, nt = nb["G"], nb["nt"]
        hT = nb["hT"].get()
        for kp in range(KC // 2):
            b = nb["tb"].get()
            fns = []
            for k2 in range(2):
                kc = kp * 2 + k2
                for i in range(nt):
                    fns.append(TR(bank_bf(b, 128, k2 * G + i * 128), xn.ap[:, i, kc * 128:(kc + 1) * 128], ident_b.ap))
            P.group("pe", fns, reads=[xn, ident_b], writes=[banks[b]])
            for k2 in range(2):
                kc = kp * 2 + k2
                first = (kc == 0)
                if k2 == 0:
                    P.op("dve", TS(hT.ap[:, kc, :], bank_bf(b, G, k2 * G), jd["Gp"].ap[:, kc:kc + 1], ALU.mult, jd["shiftT"].ap[:, kc:kc + 1], ALU.add),
                         reads=[banks[b], jd["Gp"], jd["shiftT"]], writes=[hT] if first else (), addwrites=() if first else [hT])
                else:
                    P.op("act", ACT(hT.ap[:, kc, :], bank_bf(b, G, k2 * G), AF.Identity, scale=jd["Gp"].ap[:, kc:kc + 1], bias=jd["shiftT"].ap[:, kc:kc + 1]),
                         reads=[banks[b], jd["Gp"], jd["shiftT"]], addwrites=[hT])
        return hT

    ph.reset()
    wqkv = ph.alloc([128, KC, 1536], BF16)
    wst = Ring([ph.alloc([128, 1536], F32) for _ in range(2)])
    wsem = Ring(["wst0", "wst1"])
    for kc in range(KC):
        st_ = wst.get()
        P.dma(st_.ap, w_in[:, kc, 2048:3584], wsem.get(), writes=[st_])
        if kc % 2:
            P.op("act", ACT(wqkv.ap[:, kc, :], st_.ap, AF.Copy), reads=[st_], addwrites=[wqkv])
        else:
            P.op("dve", CP(wqkv.ap[:, kc, :], st_.ap), reads=[st_], addwrites=[wqkv])
    for j, jd in enumerate(J):
        jd["ropek_s"] = ph.alloc([128, jd["S"] // 128, 16], F32)
        jd["ropeq_s"] = ph.alloc([128, jd["NQ"] // 128, 16], F32)
        P.dma(jd["ropek_s"].ap, jd["ropek"], f"rk{j}", writes=[jd["ropek_s"]])
        P.dma(jd["ropeq_s"].ap, jd["ropeq"], f"rq{j}", writes=[jd["ropeq_s"]])
    nb1 = make_norm_bufs(G1)
    sqb = Ring([ph.alloc([128, 512], F32) for _ in range(4)])
    ss8 = Ring([ph.alloc([128, 8], F32) for _ in range(4)])
    r8 = Ring([ph.alloc([128, 8], F32) for _ in range(4)])
    knb = Ring([ph.alloc([128, 512], F32) for _ in range(4)])
    kn2 = Ring([ph.alloc([128, 512], F32) for _ in range(4)])
    kbb = Ring([ph.alloc([128, 512], BF16) for _ in range(4)])
    rt = Ring([ph.alloc([128, 4, 64], F32) for _ in range(4)])
    kts = Ring([ph.alloc([128, 4, G1], BF16) for _ in range(2)])
    vts = Ring([ph.alloc([128, 4, G1], BF16) for _ in range(2)])
    ksem = Ring(["kst0", "kst1"])
    vsem = Ring(["vst0", "vst1"])
    pjk = Ring([2, 4])
    pjv = Ring([3, 5])
    ptr = Ring([6, 7])

    def qk_post(hT, i, col0, gvec, rope_s, T, stage, vstage=None):
        b = pjk.get()
        bk = banks[b]
        if vstage is None:
            fns = [MM(bank(b), hT.ap[:, kc, i * 128:(i + 1) * 128], wqkv.ap[:, kc, col0:col0 + 512], kc == 0, kc == KC - 1) for kc in range(KC)]
            P.group("pe", fns, reads=[hT, wqkv], writes=[bk])
        else:
            bv = pjv.get()
            fns = []
            for kc in range(KC):
                fns.append(MM(bank(b), hT.ap[:, kc, i * 128:(i + 1) * 128], wqkv.ap[:, kc, col0:col0 + 512], kc == 0, kc == KC - 1))
                fns.append(MM(bank(bv), hT.ap[:, kc, i * 128:(i + 1) * 128], wqkv.ap[:, kc, 1024:1536], kc == 0, kc == KC - 1))
            P.group("pe", fns, reads=[hT, wqkv], writes=[bk, banks[bv]])
            P.op("dve", CP(vstage.ap[:, :, i * 128:(i + 1) * 128], g3(bank(bv), 128)), reads=[banks[bv]], addwrites=[vstage])
        sq, s8, rr, kn, k2, kb, r_ = sqb.get(), ss8.get(), r8.get(), knb.get(), kn2.get(), kbb.get(), rt.get()
        P.op("act", ACT(sq.ap, bank(b), AF.Square), reads=[bk], writes=[sq])
        P.op("dve", RED(s8.ap, g3(sq.ap, 64)), reads=[sq], writes=[s8])
        P.op("act", ACT(rr.ap, s8.ap, AF.Sqrt, scale=1.0 / 64, bias=eps_t.ap[:, 0:1]), reads=[s8, eps_t], writes=[rr])
        P.op("dve", RCP(rr.ap, rr.ap), reads=[rr], writes=[rr])
        kn3, k23, kb3 = g3(kn.ap, 64), g3(k2.ap, 64), g3(kb.ap, 64)
        P.op("dve", TT(kn3, g3(bank(b), 64), rr.ap.unsqueeze(2).to_broadcast([128, 8, 64]), ALU.mult), reads=[bk, rr], writes=[kn])
        P.op("dve", TT(k23, kn3, gvec.ap.unsqueeze(1).to_broadcast([128, 8, 64]), ALU.mult), reads=[kn, gvec], writes=[k2])
        P.op("act", ACT(kb3[:, :, 16:64], k23[:, :, 16:64], AF.Copy), reads=[k2], writes=[kb])
        cosb = rope_s.ap[:, T, 0:8].unsqueeze(1).to_broadcast([128, 8, 8])
        sinb = rope_s.ap[:, T, 8:16].unsqueeze(1).to_broadcast([128, 8, 8])
        x1, x2 = k23[:, :, 0:8], k23[:, :, 8:16]
        rq = [g3(r_.ap[:, q, :], 8) for q in range(4)]
        P.op("pool", TT(rq[0], x1, cosb, ALU.mult), reads=[k2, rope_s], writes=[r_])
        P.op("pool", TT(rq[1], x2, sinb, ALU.mult), reads=[k2], addwrites=[r_])
        P.op("pool", TT(rq[2], x2, cosb, ALU.mult), reads=[k2], addwrites=[r_])
        P.op("pool", TT(rq[3], x1, sinb, ALU.mult), reads=[k2], addwrites=[r_])
        P.op("pool", TT(kb3[:, :, 0:8], rq[0], rq[1], ALU.subtract), reads=[r_], addwrites=[kb])
        P.op("pool", TT(kb3[:, :, 8:16], rq[2], rq[3], ALU.add), reads=[r_], addwrites=[kb])

        def part_b():
            tb = ptr.get()
            fns = [TR(bank_bf(tb, 128, h * 128), kb.ap[:, h * 128:(h + 1) * 128], ident_b.ap) for h in range(4)]
            P.group("pe", fns, reads=[kb, ident_b], writes=[banks[tb]])
            P.op("act", ACT(stage.ap[:, :, i * 128:(i + 1) * 128], g3(bank_bf(tb, 512), 128), AF.Copy), reads=[banks[tb]], addwrites=[stage])
        return part_b

    def v_post(hT, i, stage):
        b = pjv.get()
        bk = banks[b]
        fns = [MM(bank(b), hT.ap[:, kc, i * 128:(i + 1) * 128], wqkv.ap[:, kc, 1024:1536], kc == 0, kc == KC - 1) for kc in range(KC)]
        P.group("pe", fns, reads=[hT, wqkv], writes=[bk])
        P.op("dve", CP(stage.ap[:, :, i * 128:(i + 1) * 128], g3(bank(b), 128)), reads=[bk], addwrites=[stage])

    pending = []
    xn_nxt = None
    hT_next = None
    for jd in J:
        S, NQ = jd["S"], jd["NQ"]
        work = [("kv", g) for g in range(S // G1)] + [("q", g) for g in range(NQ // G1)]
        xsrc = {"kv": jd["xf"], "q": jd["xq"]}
        def keep_of(w):
            return (jd["rstd_q"], w[1] * (G1 // 128)) if w[0] == "q" else None

        xg_cur = load_group(nb1, xsrc[work[0][0]], work[0][1])
        hT_next = norm_b(nb1, jd, norm_a(nb1, jd, xg_cur, keep=keep_of(work[0])))
        for wi, (kind, g) in enumerate(work):
            hT = hT_next
            more = wi + 1 < len(work)
            if more:
                xg_nxt = load_group(nb1, xsrc[work[wi + 1][0]], work[wi + 1][1])

            def hook(i, wi=wi, more=more):
                nonlocal hT_next, xn_nxt
                if not more:
                    return
                if i == 0:
                    xn_nxt = norm_a(nb1, jd, xg_nxt, keep=keep_of(work[wi + 1]))
                if i == 2:
                    hT_next = norm_b(nb1, jd, xn_nxt)
            ks = kts.get()
            P.op("pool", MS(ks.ap[:, 0, 0:2], 0.0), writes=[ks])
            if kind == "kv":
                vs = vts.get()
                P.op("pool", MS(vs.ap[:, 0, 0:2], 0.0), writes=[vs])
                for i in range(G1 // 128):
                    pb_ = qk_post(hT, i, 512, kg_b, jd["ropek_s"], g * (G1 // 128) + i, ks)
                    v_post(hT, i, vs)
                    while len(pending) > 1:
                        pending.pop(0)()
                    pending.append(pb_)
                    hook(i)

                def stores(ks=ks, vs=vs, g=g, jd=jd):
                    P.dma(jd["kt_d"][:, :, g * G1:(g + 1) * G1].rearrange("h p s -> p h s"), ks.ap, ksem.get(), reads=[ks])
                    P.dma(jd["v_d"][:, :, g * G1:(g + 1) * G1].rearrange("h p s -> p h s"), vs.ap, vsem.get(), reads=[vs])
                pending.append(stores)
            else:
                for i in range(G1 // 128):
                    pb_ = qk_post(hT, i, 0, qg_b, jd["ropeq_s"], g * (G1 // 128) + i, ks)
                    while len(pending) > 1:
                        pending.pop(0)()
                    pending.append(pb_)
                    hook(i)

                def stores(ks=ks, g=g, jd=jd):
                    P.dma(jd["qt_d"][:, :, g * G1:(g + 1) * G1].rearrange("h p s -> p h s"), ks.ap, ksem.get(), reads=[ks])
                pending.append(stores)
    while pending:
        pending.pop(0)()
    P.barrier()

    ph.reset()
    SMAX = max(jd["S"] for jd in J)
    NQMAX = max(jd["NQ"] for jd in J)
    ktb = Ring([ph.alloc([128, SMAX], BF16) for _ in range(2)])
    vtb = Ring([ph.alloc([128, SMAX], BF16) for _ in range(2)])
    qtb = Ring([ph.alloc([128, NQMAX], BF16) for _ in range(2)])
    kvsem = Ring(["kvl0", "kvl1"])
    Pb = Ring([ph.alloc([128, 1024], BF16) for _ in range(3)])
    Sb = Ring([0, 2])
    rinv = ph.alloc([128, 1024], F32)
    DW = 512
    raccs = Ring([(ph.alloc([128, DW], F32), None) for _ in range(2)])
    osb0 = ph.alloc([128, 512], F32)
    osb1 = ph.alloc([128, 512], F32)
    rsb = ph.alloc([128, 1024], F32)
    deferred = []
    t0b = ph.alloc([128, 512], F32)
    t1b = ph.alloc([128, 512], F32)
    ost = Ring([ph.alloc([128, QT], BF16) for _ in range(2)])
    osem = Ring(["ost0", "ost1"])

    def load_head(jd, h):
        S, NQ = jd["S"], jd["NQ"]
        kt, vt, qt_, sem = ktb.get(), vtb.get(), qtb.get(), kvsem.get()
        nsp = max(1, S // 4096)
        w = S // nsp
        for sp in range(nsp):
            P.dma(kt.ap[:, sp * w:(sp + 1) * w], jd["kt_d"][h, :, sp * w:(sp + 1) * w], sem, writes=[kt] if sp == 0 else (), addwrites=[kt] if sp else ())
        for sp in range(nsp):
            P.dma(vt.ap[:, sp * w:(sp + 1) * w], jd["v_d"][h, :, sp * w:(sp + 1) * w], sem, writes=[vt] if sp == 0 else (), addwrites=[vt] if sp else ())
        tq = P.dma(qt_.ap[:, 0:NQ], jd["qt_d"][h, :, :], sem, writes=[qt_])
        kt.wr = {tq[0]: tq[1]}
        vt.wr = {tq[0]: tq[1]}
        return kt, vt, qt_

    heads = [(jd, h) for jd in J for h in range(4)]
    nxt = load_head(*heads[0])
    for hi, (jd, h) in enumerate(heads):
        kt, vt, qt_ = nxt
        if hi + 1 < len(heads):
            nxt = load_head(*heads[hi + 1])
        S, NQ = jd["S"], jd["NQ"]
        nkt = S // 128
        for qi in range(NQ // QT):
            racc, raccp = raccs.get()

            def qk(u):
                sb = Sb.get()
                fns = [MM(bank(sb), kt.ap[0:64, u * 128:(u + 1) * 128], qt_.ap[0:64, qi * QT:(qi + 1) * QT], True, True, (0, 0)),
                       MM(bank(sb + 1), kt.ap[64:128, u * 128:(u + 1) * 128], qt_.ap[64:128, qi * QT:(qi + 1) * QT], True, True, (64, 0))]
                P.group("pe", fns, reads=[kt, qt_], writes=[banks[sb], banks[sb + 1]])
                pb = Pb.get()
                P.op("act", ACT(pb.ap, ps[:, sb * 512: sb * 512 + 1024], AF.Exp, scale=0.125), reads=[banks[sb], banks[sb + 1]], writes=[pb])
                return pb

            def pv(u, pb):
                vcol = u * 128
                first, last = (u == 0), (u == nkt - 1)
                fns = [MM(bank(4), vt.ap[:, vcol:vcol + 128], pb.ap[:, 0:512], first, last),
                       MM(bank(5), vt.ap[:, vcol:vcol + 128], pb.ap[:, 512:1024], first, last),
                       MM(bank(7, 1024 - DW, DW - 512), ones_b.ap, pb.ap[:, DW:1024], first, last)]
                acc = [banks[4], banks[5], banks[7]]
                if first:
                    P.group("pe", fns, reads=[vt, pb, ones_b], writes=acc)
                    P.op("dve", CP(racc.ap, pb.ap[:, 0:DW]), reads=[pb], writes=[racc])
                else:
                    P.group("pe", fns, reads=[vt, pb, ones_b], addwrites=acc)
                    P.op("dve", TT(racc.ap, racc.ap, pb.ap[:, 0:DW], ALU.add), reads=[pb, racc], writes=[racc])

            pbs = {0: qk(0)}
            if nkt > 1:
                pbs[1] = qk(1)
            for u in range(nkt):
                if u + 2 < nkt:
                    pbs[u + 2] = qk(u + 2)
                pv(u, pbs.pop(u))
                if deferred:
                    deferred.pop(0)()
            while deferred:
                deferred.pop(0)()
            P.op("dve", CP(osb0.ap, bank(4)), reads=[banks[4]], writes=[osb0])
            P.op("dve", CP(osb1.ap, bank(5)), reads=[banks[5]], writes=[osb1])
            P.op("dve", CP(rsb.ap[:, DW:1024], bank(7, 1024 - DW, DW - 512)), reads=[banks[7]], writes=[rsb])

            def tot(racc=racc, raccp=raccp):
                P.group("pe", [MM(bank(6), ones_f.ap, racc.ap[:, 0:512], True, True)], reads=[racc, ones_f], writes=[banks[6]])
                P.op("dve", CP(rsb.ap[:, 0:512], bank(6)), reads=[banks[6]], addwrites=[rsb])
                if DW > 512:
                    P.group("pe", [MM(bank(6, DW - 512), ones_f.ap, racc.ap[:, 512:DW], True, True)], reads=[racc, ones_f], writes=[banks[6]])
                    P.op("dve", CP(rsb.ap[:, 512:DW], bank(6, DW - 512)), reads=[banks[6]], addwrites=[rsb])
            nch = 16 if nkt >= 32 else 8
            cw = 1024 // nch

            def mk_rcp(c):
                def f():
                    P.op("dve", RCP(rinv.ap[:, c * cw:(c + 1) * cw], rsb.ap[:, c * cw:(c + 1) * cw]), reads=[rsb],
                         writes=[rinv] if c == 0 else (), addwrites=[rinv] if c else ())
                return f

            def fin(jd=jd, h=h, qi=qi):
                os_ = ost.get()
                P.op("dve", TT(t0b.ap, osb0.ap, rinv.ap[:, 0:512], ALU.mult), reads=[osb0, rinv], writes=[t0b])
                P.op("dve", TT(t1b.ap, osb1.ap, rinv.ap[:, 512:1024], ALU.mult), reads=[osb1, rinv], writes=[t1b])
                P.op("dve", STT(os_.ap, t1b.ap, neg_lam.ap[:, 0:1], t0b.ap, ALU.mult, ALU.add), reads=[t0b, t1b, neg_lam], writes=[os_])
                P.dma(jd["o_d"][h, :, qi * QT:(qi + 1) * QT], os_.ap, osem.get(), reads=[os_])
            deferred.extend([tot] + [mk_rcp(c) for c in range(nch)] + [fin])
    while deferred:
        deferred.pop(0)()
    P.barrier()

    ph.reset()
    w3 = ph.alloc([128, KC, 4608], BF16)
    wco = ph.alloc([128, 4, D], BF16)
    wao = ph.alloc([128, 4, D], BF16)
    wo = ph.alloc([128, KC, D], BF16)
    wst3 = Ring([ph.alloc([128, 1024], F32) for _ in range(2)])
    wsem3 = Ring(["wst0", "wst1"])
    cast_eng = Ring(["act", "dve"])

    def load_cast(dst_ap, src_ap, n, dstbuf):
        st_ = wst3.get()
        P.dma(st_.ap[:, 0:n], src_ap, wsem3.get(), writes=[st_])
        ce_ = cast_eng.get()
        if ce_ == "act":
            P.op("act", ACT(dst_ap, st_.ap[:, 0:n], AF.Copy), reads=[st_], addwrites=[dstbuf])
        else:
            P.op("dve", CP(dst_ap, st_.ap[:, 0:n]), reads=[st_], addwrites=[dstbuf])

    for kc in range(KC):
        for c0, s0, n in ((0, 0, 1024), (1024, 1024, 1024), (2048, 3584, 1024), (3072, 4608, 1024), (4096, 5632, 512)):
            load_cast(w3.ap[:, kc, c0:c0 + n], w_in[:, kc, s0:s0 + n], n, w3)
        load_cast(wo.ap[:, kc, :], wo_d[:, kc, :], 1024, wo)
    for m in range(4):
        load_cast(wco.ap[:, m, :], wco_d[:, m, :], 1024, wco)
        load_cast(wao.ap[:, m, :], wao_d[:, m, :], 1024, wao)
    nb3 = make_norm_bufs(G3, 1)
    NT3 = G3 // 128
    oin = Ring([ph.alloc([128, 4, G3], BF16) for _ in range(2)])
    oisem = Ring(["oin0", "oin1"])
    yT = ph.alloc([128, 4, G3], BF16)
    ogT = ph.alloc([128, 4, G3], BF16)
    mT = ph.alloc([128, KC, G3], BF16)
    yo = Ring([ph.alloc([128, D], F32) for _ in range(2)])
    yosem = Ring(["yo0", "yo1"])
    tmpf = Ring([ph.alloc([128, G3], F32) for _ in range(10)])
    tmpg = Ring([ph.alloc([128, 512], F32) for _ in range(2)])
    osq = ph.alloc([128, 4, G3], BF16)
    rsa = ph.alloc([128, 4 * G3], F32)
    uext = Ring([ph.alloc([128, G3 + 2], F32) for _ in range(2)])
    pr = Ring([2, 3, 4, 5])
    pr2 = Ring([6, 7])
    xh = yo.items[0]
    xhn = ph.alloc([32, D], BF16)
    hTh = ph.alloc([128, KC, 32], BF16)
    ssh = ph.alloc([32, 2], F32)
    uh = ph.alloc([128, 4, 32], F32)
    cxh = ph.alloc([128, 32], F32)

    def proj(col, hT_buf, hT_ap, n):
        b = pr.get()
        fns = [MM(bank(b, n), w3.ap[:, kc, col:col + 128], hT_ap[:, kc, :], kc == 0, kc == KC - 1) for kc in range(KC)]
        P.group("pe", fns, reads=[hT_buf, w3], writes=[banks[b]])
        return b

    def tanh_half(b, n, dst):
        P.op("act", ACT(dst.ap[:, 0:n], bank(b, n), AF.Tanh, scale=0.5), reads=[banks[b]], writes=[dst])

    def load3(jd, g):
        xg = load_group(nb3, jd["xq"], g)
        oi = oin.get()
        P.dma(oi.ap, jd["o_d"][:, :, g * G3:(g + 1) * G3].rearrange("h p s -> p h s"), oisem.get(), writes=[oi])
        return xg, oi

    for j, jd in enumerate(J):
        NQ, ng3 = jd["NQ"], jd["ng3"]
        nh = 2 * ng3
        P.dma(xh.ap[0:nh, :], jd["halo"], f"hl{j}", writes=[xh])
        P.op("act", ACT(nb3["junk"].ap[0:nh, :], xh.ap[0:nh, :], AF.Square, accum=ssh.ap[0:nh, 0:1]), reads=[xh], writes=[ssh])
        P.op("act", ACT(ssh.ap[0:nh, 1:2], ssh.ap[0:nh, 0:1], AF.Sqrt, scale=1.0 / D, bias=eps_t.ap[0:nh, 0:1]), reads=[ssh, eps_t], writes=[ssh])
        P.op("dve", RCP(ssh.ap[0:nh, 1:2], ssh.ap[0:nh, 1:2]), reads=[ssh], writes=[ssh])
        P.op("dve", TS(xhn.ap[0:nh, :], xh.ap[0:nh, :], ssh.ap[0:nh, 1:2], ALU.mult), reads=[xh, ssh], writes=[xhn])
        fns = [TR(bank_bf(0, nh, kc * 32), xhn.ap[0:nh, kc * 128:(kc + 1) * 128], ident_b.ap[0:nh, 0:nh]) for kc in range(KC)]
        P.group("pe", fns, reads=[xhn, ident_b], writes=[banks[0]])
        for kc in range(KC):
            P.op("dve", TS(hTh.ap[:, kc, 0:nh], bank_bf(0, nh, kc * 32), jd["Gp"].ap[:, kc:kc + 1], ALU.mult, jd["shiftT"].ap[:, kc:kc + 1], ALU.add),
                 reads=[banks[0], jd["Gp"], jd["shiftT"]], writes=[hTh] if kc == 0 else (), addwrites=[hTh] if kc else ())
        for m in range(4):
            bc = proj(512 + m * 128, hTh, hTh.ap[:, :, 0:nh], nh)
            bx = proj(1024 + m * 128, hTh, hTh.ap[:, :, 0:nh], nh)
            P.op("act", ACT(cxh.ap[:, 0:nh], bank(bx, nh), AF.Copy), reads=[banks[bx]], writes=[cxh])
            P.op("dve", TT(uh.ap[:, m, 0:nh], bank(bc, nh), cxh.ap[:, 0:nh], ALU.mult), reads=[banks[bc], cxh], writes=[uh] if m == 0 else (),
                 addwrites=[uh] if m else ())
            P.op("dve", TT(uh.ap[:, m, 0:nh], uh.ap[:, m, 0:nh], jd["hmask_s"].ap[:, 0:nh], ALU.mult), reads=[uh, jd["hmask_s"]], addwrites=[uh])
        cur = load3(jd, 0)
        hT_n3 = norm_group(nb3, jd, cur[0], pre=(jd["rstd_q"], 0))
        for g in range(ng3):
            xg, oi = cur
            hT = hT_n3
            more3 = g + 1 < ng3
            if more3:
                nxt = load3(jd, g + 1)
            P.op("dve", TT(osq.ap, oi.ap, oi.ap, ALU.mult), reads=[oi], writes=[osq])
            P.group("pe", [MM(bank(6 + hh // 2, G3, (hh % 2) * G3), ones_b.ap, osq.ap[:, hh, :], True, True) for hh in range(4)],
                    reads=[osq, ones_b], writes=[banks[6], banks[7]])
            P.op("act", ACT(rsa.ap, ps[:, 6 * 512: 8 * 512], AF.Sqrt, scale=1.0 / 128, bias=eps_t.ap[:, 0:1]), reads=[banks[6], banks[7], eps_t], writes=[rsa])
            P.op("dve", RCP(rsa.ap, rsa.ap), reads=[rsa], writes=[rsa])
            for m in range(4):
                bc = proj(512 + m * 128, hT, hT.ap, G3)
                bx = proj(1024 + m * 128, hT, hT.ap, G3)
                cxs, ue, cv = tmpf.get(), uext.get(), tmpf.get()
                P.op("act", ACT(cxs.ap, bank(bx, G3), AF.Copy), reads=[banks[bx]], writes=[cxs])
                P.op("dve", TT(ue.ap[:, 1:G3 + 1], bank(bc, G3), cxs.ap, ALU.mult), reads=[banks[bc], cxs], writes=[ue])
                P.op("dve", CP(ue.ap[:, 0:1], uh.ap[:, m, 2 * g:2 * g + 1]), reads=[uh], addwrites=[ue])
                P.op("dve", CP(ue.ap[:, G3 + 1:G3 + 2], uh.ap[:, m, 2 * g + 1:2 * g + 2]), reads=[uh], addwrites=[ue])
                P.op("dve", TS(cv.ap, ue.ap[:, 0:G3], conv_l.ap[:, m * 3:m * 3 + 1], ALU.mult), reads=[ue, conv_l], writes=[cv])
                P.op("dve", STT(cv.ap, ue.ap[:, 1:G3 + 1], conv_l.ap[:, m * 3 + 1:m * 3 + 2], cv.ap, ALU.mult, ALU.add), reads=[ue, cv], writes=[cv])
                P.op("dve", STT(cv.ap, ue.ap[:, 2:G3 + 2], conv_l.ap[:, m * 3 + 2:m * 3 + 3], cv.ap, ALU.mult, ALU.add), reads=[ue, cv], writes=[cv])
                bb_ = proj(0 + m * 128, hT, hT.ap, G3)
                bz = proj(1536 + m * 128, hT, hT.ap, G3)
                th, y1 = tmpf.get(), tmpf.get()
                tanh_half(bz, G3, th)
                P.op("dve", STT(th.ap, th.ap, 1.0, bank(bz, G3), ALU.add, ALU.mult), reads=[banks[bz], th], writes=[th])
                P.op("dve", TT(y1.ap, bank(bb_, G3), cv.ap, ALU.mult), reads=[banks[bb_], cv], writes=[y1])
                P.op("dve", TT(yT.ap[:, m, :], y1.ap, th.ap, ALU.mult), reads=[y1, th], writes=[yT] if m == 0 else (), addwrites=[yT] if m else ())
            for h in range(4):
                ba = proj(2048 + h * 128, hT, hT.ap, G3)
                th, on = tmpf.get(), tmpf.get()
                tanh_half(ba, G3, th)
                P.op("dve", STT(th.ap, th.ap, 1.0, bank(ba, G3), ALU.add, ALU.mult), reads=[banks[ba], th], writes=[th])
                P.op("dve", STT(on.ap, oi.ap[:, h, :], sgs.ap[:, 0:1], rsa.ap[:, h * G3:(h + 1) * G3], ALU.mult, ALU.mult), reads=[oi, sgs, rsa], writes=[on])
                P.op("dve", TT(ogT.ap[:, h, :], on.ap, th.ap, ALU.mult), reads=[on, th], writes=[ogT] if h == 0 else (), addwrites=[ogT] if h else ())
            if more3:
                xn_n3 = norm_a(nb3, jd, nxt[0], pre=(jd["rstd_q"], (g + 1) * NT3))
            for f in range(KC):
                bga = proj(2560 + f * 128, hT, hT.ap, G3)
                bgb = proj(3584 + f * 128, hT, hT.ap, G3)
                bA = pr2.get()
                P.group("pe", [MM(bank(bA, G3), wco.ap[:, m, f * 128:(f + 1) * 128], yT.ap[:, m, :], m == 0, m == 3) for m in range(4)],
                        reads=[yT, wco], writes=[banks[bA]])
                bB = pr2.get()
                P.group("pe", [MM(bank(bB, G3), wao.ap[:, h, f * 128:(f + 1) * 128], ogT.ap[:, h, :], h == 0, h == 3) for h in range(4)],
                        reads=[ogT, wao], writes=[banks[bB]])
                sa, sb_, ma, mb = tmpf.get(), tmpf.get(), tmpf.get(), tmpf.get()
                tanh_half(bga, G3, sa)
                tanh_half(bgb, G3, sb_)
                P.op("dve", STT(ma.ap, sa.ap, 1.0, bank(bA, G3), ALU.add, ALU.mult), reads=[banks[bA], sa], writes=[ma])
                P.op("dve", STT(mb.ap, sb_.ap, 1.0, bank(bB, G3), ALU.add, ALU.mult), reads=[banks[bB], sb_], writes=[mb])
                P.op("dve", TT(mT.ap[:, f, :], ma.ap, mb.ap, ALU.add), reads=[ma, mb], writes=[mT] if f == 0 else (), addwrites=[mT] if f else ())
            if more3:
                hT_n3 = norm_b(nb3, jd, xn_n3)
                cur = nxt
            for i in range(NT3):
                yo_ = yo.get()
                for hf in range(2):
                    bo = pr2.get()
                    P.group("pe", [MM(bank(bo), mT.ap[:, f, i * 128:(i + 1) * 128], wo.ap[:, f, hf * 512:(hf + 1) * 512], f == 0, f == KC - 1) for f in range(KC)],
                            reads=[mT, wo], writes=[banks[bo]])
                    tg = tmpg.get()
                    P.op("dve", TT(tg.ap, bank(bo), jd["gate_b"].ap[:, hf * 512:(hf + 1) * 512], ALU.mult), reads=[banks[bo], jd["gate_b"]], writes=[tg])
                    P.op("dve", TT(yo_.ap[:, hf * 512:(hf + 1) * 512], tg.ap, xg.ap[:, i, hf * 512:(hf + 1) * 512], ALU.add),
                         reads=[tg, xg], writes=[yo_] if hf == 0 else (), addwrites=[yo_] if hf else ())
                P.dma(jd["y"][g * G3 + i * 128: g * G3 + (i + 1) * 128, :], yo_.ap, yosem.get(), reads=[yo_])
    P.barrier()
    P.replay(nc)
    return nc


ROT_HALF = 8
ROPE_THETA = 500000.0


def _rope_table(pos):
    inv = (np.float32(ROPE_THETA) ** (-np.arange(ROT_HALF, dtype=np.float32) / np.float32(ROT_HALF))).astype(np.float32)
    ang = pos.astype(np.float32)[:, None] * inv[None, :]
    return np.ascontiguousarray(np.concatenate([np.cos(ang), np.sin(ang)], axis=1).astype(np.float32))


def _job_inputs(j, x_seq, q0, NQ, cvec):
    S = x_seq.shape[0]
    ng3 = NQ // G3
    halo = np.zeros((2 * ng3, D), np.float32)
    hmask = np.zeros((2 * ng3,), np.float32)
    for g in range(ng3):
        for side, idx in ((0, q0 + g * G3 - 1), (1, q0 + (g + 1) * G3)):
            if 0 <= idx < S:
                halo[2 * g + side] = x_seq[idx]
                hmask[2 * g + side] = 1.0
    rope = _rope_table(np.arange(S))
    return {
        f"xf{j}": np.ascontiguousarray(x_seq),
        f"xq{j}": np.ascontiguousarray(x_seq[q0:q0 + NQ]),
        f"halo{j}": halo,
        f"hmask{j}": np.ascontiguousarray(np.broadcast_to(hmask[None, :], (128, 2 * ng3))),
        f"ropek{j}": rope,
        f"ropeq{j}": np.ascontiguousarray(rope[q0:q0 + NQ]),
        f"cl{j}": np.ascontiguousarray(cvec.reshape(8, 128).T),
    }


def _shared_inputs(norm_g, w_ada, b_ada, w_in, conv_w, q_norm_g, k_norm_g, lam_q1, lam_k1, lam_q2, lam_k2, subln_g,
                   w_conv_out, w_attn_out, w_out):
    f = lambda a: np.ascontiguousarray(np.asarray(a, dtype=np.float32))
    return {
        "w_in": f(w_in[0]), "w_ada": f(w_ada[0]), "b_ada": f(b_ada[0]),
        "norm_g_l": f(np.asarray(norm_g[0]).reshape(8, 128).T),
        "conv_l": f(np.asarray(conv_w[0]).reshape(3, 4, 128).transpose(2, 1, 0).reshape(128, 12)),
        "qg": f(q_norm_g[0]), "kg": f(k_norm_g[0]),
        "lamv": f(np.concatenate([np.asarray(lam_q1[0]), np.asarray(lam_k1[0]), np.asarray(lam_q2[0]), np.asarray(lam_k2[0])])),
        "subln": f(np.asarray(subln_g[0]).reshape(128, 1)),
        "w_conv_out": f(w_conv_out[0]), "w_attn_out": f(w_attn_out[0]), "w_out": f(w_out[0]),
        "ident": np.eye(128, dtype=np.float32),
    }


def kernel(x_prompt, x_sample, c_prompt, c_sample, norm_g, w_ada, b_ada, w_in, conv_w, q_norm_g, k_norm_g,
           lam_q1, lam_k1, lam_q2, lam_k2, subln_g, w_conv_out, w_attn_out, w_out):
    x_prompt = np.asarray(x_prompt, dtype=np.float32)
    x_sample = np.asarray(x_sample, dtype=np.float32)
    c_prompt = np.asarray(c_prompt, dtype=np.float32)
    c_sample = np.asarray(c_sample, dtype=np.float32)
    n = 8
    B, S0, _ = x_prompt.shape
    B1, S1, _ = x_sample.shape
    NQ1 = S1 * B1 // n
    per_seq = S1 // NQ1
    nc = build([dict(S=S0, NQ=S0), dict(S=S1, NQ=NQ1)])
    shared = _shared_inputs(norm_g, w_ada, b_ada, w_in, conv_w, q_norm_g, k_norm_g, lam_q1, lam_k1, lam_q2, lam_k2, subln_g,
                            w_conv_out, w_attn_out, w_out)
    in_maps = []
    for c in range(n):
        m = dict(shared)
        m.update(_job_inputs(0, x_prompt[c], 0, S0, c_prompt[c]))
        b, jq = c // per_seq, c % per_seq
        m.update(_job_inputs(1, x_sample[b], jq * NQ1, NQ1, c_sample[b]))
        in_maps.append(m)
    res = run_bass_kernel_spmd(nc, in_maps, core_ids=list(range(n)))
    y_prompt = np.stack([np.asarray(res.results[c]["y0"], dtype=np.float32) for c in range(n)], axis=0)
    y_sample = np.zeros((B1, S1, D), np.float32)
    for c in range(n):
        b, jq = c // per_seq, c % per_seq
        y_sample[b, jq * NQ1:(jq + 1) * NQ1] = np.asarray(res.results[c]["y1"], dtype=np.float32)
    return (y_prompt, y_sample)
```

```python
import math
import contextlib
import numpy as np
import concourse.bass as bass
import concourse.mybir as mybir
from concourse.bass_utils import run_bass_kernel_spmd

F32 = mybir.dt.float32
BF16 = mybir.dt.bfloat16
AF = mybir.ActivationFunctionType
ALU = mybir.AluOpType
AX = mybir.AxisListType

D = 1024
KC = 8
EPS = 1e-6
LAM_INIT = 0.8 - 0.6 * math.exp(-0.3 * 0)
SBUF_LO = 16512
SBUF_HI = 229344
G1 = 512
G3 = 256
QT = 512


class Buf:
    def __init__(self, ap):
        self.ap = ap
        self.wr = {}
        self.rd = {}


class Prog:
    def __init__(self):
        self.q = {e: [] for e in ("sync", "pe", "act", "dve", "pool")}
        self.cnt = {}
        self.waited = {e: {} for e in self.q}

    def wait(self, eng, tok):
        if tok is None:
            return
        s, v = tok
        if self.waited[eng].get(s, 0) >= v:
            return
        self.waited[eng][s] = v
        self.q[eng].append(("w", s, v))

    def _hz(self, reads, writes, extra):
        waits = list(extra)
        for b in reads:
            waits += list(b.wr.items())
        for b in writes:
            waits += list(b.rd.items())
            if not b.rd:
                waits += list(b.wr.items())
        return waits

    def _reg(self, tok, reads, writes, addwrites):
        s, v = tok
        for b in reads:
            b.rd[s] = max(b.rd.get(s, 0), v)
        for b in writes:
            b.wr = {s: v}
            b.rd = {}
        for b in addwrites:
            b.wr[s] = max(b.wr.get(s, 0), v)

    def op(self, eng, fn, reads=(), writes=(), addwrites=(), extra=()):
        for t in self._hz(reads, tuple(writes) + tuple(addwrites), extra):
            self.wait(eng, t)
        s = "s_" + eng
        self.cnt[s] = self.cnt.get(s, 0) + 1
        tok = (s, self.cnt[s])
        self.q[eng].append(("o", fn, tok))
        self._reg(tok, reads, writes, addwrites)
        return tok

    def group(self, eng, fns, reads=(), writes=(), addwrites=(), extra=()):
        for t in self._hz(reads, tuple(writes) + tuple(addwrites), extra):
            self.wait(eng, t)
        for fn in fns[:-1]:
            self.q[eng].append(("o", fn, None))
        s = "s_" + eng
        self.cnt[s] = self.cnt.get(s, 0) + 1
        tok = (s, self.cnt[s])
        self.q[eng].append(("o", fns[-1], tok))
        self._reg(tok, reads, writes, addwrites)
        return tok

    def dma(self, out, in_, sem, reads=(), writes=(), addwrites=(), extra=(), eng="sync"):
        for t in self._hz(reads, tuple(writes) + tuple(addwrites), extra):
            self.wait(eng, t)
        self.cnt[sem] = self.cnt.get(sem, 0) + 16
        tok = (sem, self.cnt[sem])
        self.q[eng].append(("d", out, in_, tok))
        self._reg(tok, reads, writes, addwrites)
        return tok

    def barrier(self):
        toks = [(s, v) for s, v in self.cnt.items()]
        for e in self.q:
            for t in toks:
                self.wait(e, t)

    def replay(self, nc):
        for s, v in self.cnt.items():
            assert v < 60000, (s, v)
        with contextlib.ExitStack() as st:
            sems = {name: st.enter_context(nc.semaphore(name)) for name in sorted(self.cnt)}
            block = st.enter_context(nc.Block())

            def run(items):
                def f(eng):
                    n = len(items)
                    i = 0
                    while i < n:
                        it = items[i]
                        if it[0] == "w":
                            nx = items[i + 1] if i + 1 < n else None
                            if nx is not None and nx[0] == "o" and getattr(nx[1], "embed_ok", False):
                                ins = nx[1](eng)
                                ins._wait_ge(sems[it[1]], it[2])
                                if nx[2] is not None:
                                    ins.then_inc(sems[nx[2][0]], 1)
                                i += 2
                                continue
                            eng.wait_ge(sems[it[1]], it[2])
                        elif it[0] == "o":
                            ins = it[1](eng)
                            if it[2] is not None:
                                ins.then_inc(sems[it[2][0]], 1)
                        else:
                            eng.dma_start(out=it[1], in_=it[2]).then_inc(sems[it[3][0]], 16)
                        i += 1
                return f

            block.sync(run(self.q["sync"]))
            block.tensor(run(self.q["pe"]))
            block.scalar(run(self.q["act"]))
            block.vector(run(self.q["dve"]))
            block.gpsimd(run(self.q["pool"]))


class Ring:
    def __init__(self, items):
        self.items = items
        self.i = 0

    def get(self):
        b = self.items[self.i % len(self.items)]
        self.i += 1
        return b


def build(jobs):
    nc = bass.Bass("TRN2", target_bir_lowering=False)
    P = Prog()
    uid = [0]

    def din(name, shape, dt=F32):
        return nc.dram_tensor(name, list(shape), dt, kind="ExternalInput").ap()

    w_in = din("w_in", [D, 6144]).rearrange("(k p) c -> p k c", p=128)
    w_ada = din("w_ada", [D, 3072]).rearrange("(k p) c -> p k c", p=128)
    b_ada = din("b_ada", [3072])
    norm_g_d = din("norm_g_l", [128, 8])
    conv_d = din("conv_l", [128, 12])
    qg_d = din("qg", [64])
    kg_d = din("kg", [64])
    lam_d = din("lamv", [256])
    subln_d = din("subln", [128, 1])
    wco_d = din("w_conv_out", [512, D]).rearrange("(k p) c -> p k c", p=128)
    wao_d = din("w_attn_out", [512, D]).rearrange("(k p) c -> p k c", p=128)
    wo_d = din("w_out", [D, D]).rearrange("(k p) c -> p k c", p=128)
    ident_d = din("ident", [128, 128])
    J = []
    for j, cfg in enumerate(jobs):
        S, NQ = cfg["S"], cfg["NQ"]
        ng3 = NQ // G3
        jd = dict(S=S, NQ=NQ, ng3=ng3)
        jd["xf"] = din(f"xf{j}", [S, D])
        jd["xq"] = din(f"xq{j}", [NQ, D])
        jd["halo"] = din(f"halo{j}", [2 * ng3, D])
        jd["hmask"] = din(f"hmask{j}", [128, 2 * ng3])
        jd["ropek"] = din(f"ropek{j}", [S, 16]).rearrange("(t p) c -> p t c", p=128)
        jd["ropeq"] = din(f"ropeq{j}", [NQ, 16]).rearrange("(t p) c -> p t c", p=128)
        jd["cl"] = din(f"cl{j}", [128, 8])
        jd["y"] = nc.dram_tensor(f"y{j}", [NQ, D], F32, kind="ExternalOutput").ap()
        jd["kt_d"] = nc.dram_tensor(f"kt_d{j}", [4, 128, S], BF16).ap()
        jd["v_d"] = nc.dram_tensor(f"v_d{j}", [4, 128, S], BF16).ap()
        jd["qt_d"] = nc.dram_tensor(f"qt_d{j}", [4, 128, NQ], BF16).ap()
        jd["o_d"] = nc.dram_tensor(f"o_d{j}", [4, 128, NQ], BF16).ap()
        J.append(jd)

    class Arena:
        def __init__(self, lo, hi):
            self.lo, self.hi, self.off = lo, hi, lo

        def reset(self):
            self.off = self.lo

        def alloc(self, shape, dt):
            uid[0] += 1
            nb = int(np.prod(shape[1:])) * (4 if dt == F32 else 2)
            off = (self.off + 63) // 64 * 64
            assert off + nb <= self.hi, ("SBUF overflow", shape, off, nb, self.hi)
            self.off = off + nb
            h = nc.alloc_sbuf_tensor_at(f"t{uid[0]}", list(shape), dt, offset=off)
            return Buf(h[:])

    pers = Arena(SBUF_LO, SBUF_LO + 16 * 1024)
    ph = Arena(SBUF_LO + 16 * 1024, SBUF_HI)

    ps_h = nc.alloc_psum_tensor("ps", [128, 4096], F32)
    ps = ps_h[:]
    ps_bf = ps.bitcast(BF16)

    def bank(b, n=512, off=0):
        return ps[:, b * 512 + off: b * 512 + off + n]

    def bank_bf(b, n=1024, off=0):
        return ps_bf[:, b * 1024 + off: b * 1024 + off + n]

    banks = [Buf(bank(b)) for b in range(8)]

    def MM(out, lhsT, rhs, start, stop, tp=None):
        if tp is None:
            return lambda e: e.matmul(out, lhsT=lhsT, rhs=rhs, start=start, stop=stop)
        return lambda e: e.matmul(out, lhsT=lhsT, rhs=rhs, start=start, stop=stop, tile_position=tp)

    def TR(out, in_, ident):
        return lambda e: e.transpose(out, in_, ident)

    def EMB(fn):
        fn.embed_ok = True
        return fn

    def ACT(out, in_, func, scale=1.0, accum=None, bias=None):
        if bias is not None:
            return EMB(lambda e: e.activation(out=out, in_=in_, func=func, scale=scale, bias=bias))
        if accum is None:
            return EMB(lambda e: e.activation(out=out, in_=in_, func=func, scale=scale))
        return lambda e: e.activation(out=out, in_=in_, func=func, scale=scale, accum_out=accum)

    def TT(out, in0, in1, op):
        return EMB(lambda e: e.tensor_tensor(out=out, in0=in0, in1=in1, op=op))

    def TS(out, in0, s1, op0, s2=None, op1=None):
        if op1 is None:
            return EMB(lambda e: e.tensor_scalar(out=out, in0=in0, scalar1=s1, scalar2=None, op0=op0))
        return EMB(lambda e: e.tensor_scalar(out=out, in0=in0, scalar1=s1, scalar2=s2, op0=op0, op1=op1))

    def STT(out, in0, scalar, in1, op0, op1):
        return EMB(lambda e: e.scalar_tensor_tensor(out=out, in0=in0, scalar=scalar, in1=in1, op0=op0, op1=op1))

    def CP(out, in_):
        return EMB(lambda e: e.tensor_copy(out=out, in_=in_))

    def RED(out, in_):
        return lambda e: e.tensor_reduce(out=out, in_=in_, axis=AX.X, op=ALU.add)

    def RCP(out, in_):
        return lambda e: e.reciprocal(out=out, in_=in_)

    def MS(ap, val):
        return lambda e: e.memset(ap, val)

    def g3(ap, b):
        return ap.rearrange("p (a b) -> p a b", b=b)

    ident_f = pers.alloc([128, 128], F32)
    ident_b = pers.alloc([128, 128], BF16)
    ones_b = pers.alloc([128, 128], BF16)
    neghalf = pers.alloc([128, 512], F32)
    norm_g = pers.alloc([128, 8], F32)
    conv_l = pers.alloc([128, 12], F32)
    qg_b = pers.alloc([128, 64], F32)
    kg_b = pers.alloc([128, 64], F32)
    lamv = pers.alloc([128, 256], F32)
    subln = pers.alloc([128, 1], F32)
    sgs = pers.alloc([128, 1], F32)
    neg_lam = pers.alloc([128, 1], F32)
    lamt = pers.alloc([128, 128], F32)
    lams = pers.alloc([128, 4], F32)
    ones_f = pers.alloc([128, 128], F32)
    eps_t = pers.alloc([128, 1], F32)

    P.dma(ident_f.ap, ident_d, "cld", writes=[ident_f])
    P.dma(norm_g.ap, norm_g_d, "cld", writes=[norm_g])
    P.dma(conv_l.ap, conv_d, "cld", writes=[conv_l])
    P.dma(qg_b.ap, qg_d.partition_broadcast(128), "cld", writes=[qg_b])
    P.dma(kg_b.ap, kg_d.partition_broadcast(128), "cld", writes=[kg_b])
    P.dma(lamv.ap, lam_d.partition_broadcast(128), "cld", writes=[lamv])
    P.dma(subln.ap, subln_d, "cld", writes=[subln])
    for jd in J:
        jd["Gp"] = pers.alloc([128, 8], F32)
        jd["shiftT"] = pers.alloc([128, 8], F32)
        jd["gate_b"] = pers.alloc([128, D], F32)
        jd["hmask_s"] = pers.alloc([128, 2 * jd["ng3"]], F32)
        jd["rstd_q"] = pers.alloc([128, jd["NQ"] // 128], F32)
        P.dma(jd["hmask_s"].ap, jd["hmask"], "cld", writes=[jd["hmask_s"]])
    P.barrier()
    P.op("pool", MS(ones_b.ap, 1.0), writes=[ones_b])
    P.op("pool", MS(neghalf.ap, -0.5), writes=[neghalf])
    P.op("pool", MS(ones_f.ap, 1.0), writes=[ones_f])
    P.op("pool", MS(eps_t.ap, EPS), writes=[eps_t])
    P.op("dve", TS(conv_l.ap, conv_l.ap, 0.5, ALU.mult), reads=[conv_l], writes=[conv_l])
    P.op("dve", CP(ident_b.ap, ident_f.ap), reads=[ident_f], writes=[ident_b])
    P.op("dve", TT(lamt.ap[:, 0:64], lamv.ap[:, 0:64], lamv.ap[:, 64:128], ALU.mult), reads=[lamv], writes=[lamt])
    P.op("dve", TT(lamt.ap[:, 64:128], lamv.ap[:, 128:192], lamv.ap[:, 192:256], ALU.mult), reads=[lamv], addwrites=[lamt])
    P.op("dve", RED(lams.ap[:, 0:2], g3(lamt.ap, 64)), reads=[lamt], writes=[lams])
    P.op("act", ACT(lams.ap[:, 2:4], lams.ap[:, 0:2], AF.Exp), reads=[lams], addwrites=[lams])
    P.op("dve", STT(neg_lam.ap, lams.ap[:, 3:4], -LAM_INIT, lams.ap[:, 2:3], ALU.add, ALU.subtract), reads=[lams], writes=[neg_lam])
    P.op("dve", TS(sgs.ap, subln.ap, 0.5 * (1.0 - LAM_INIT), ALU.mult), reads=[subln], writes=[sgs])

    ph.reset()
    wada = ph.alloc([128, KC, 3072], F32)
    bada = ph.alloc([128, 3072], F32)
    for kc in range(KC):
        P.dma(wada.ap[:, kc, :], w_ada[:, kc, :], "wld", addwrites=[wada])
    P.dma(bada.ap, b_ada.partition_broadcast(128), "bld", writes=[bada])
    for j, jd in enumerate(J):
        cl = ph.alloc([128, 8], F32)
        ce = ph.alloc([128, 8], F32)
        sc = ph.alloc([128, 8], F32)
        screp = ph.alloc([128, KC, 128], F32)
        modb = ph.alloc([128, 3072], F32)
        scl = ph.alloc([128, 8], F32)
        P.dma(cl.ap, jd["cl"], f"cl{j}", writes=[cl])
        P.op("act", ACT(ce.ap, cl.ap, AF.Exp, scale=-1.0), reads=[cl], writes=[ce])
        P.op("dve", TS(ce.ap, ce.ap, 1.0, ALU.add), reads=[ce], writes=[ce])
        P.op("dve", RCP(ce.ap, ce.ap), reads=[ce], writes=[ce])
        P.op("dve", TT(sc.ap, cl.ap, ce.ap, ALU.mult), reads=[cl, ce], writes=[sc])
        P.op("dve", CP(screp.ap, sc.ap.unsqueeze(2).to_broadcast([128, KC, 128])), reads=[sc], writes=[screp])
        for cg in range(6):
            bk = banks[cg]
            fns = [MM(bank(cg), screp.ap[:, kc, :], wada.ap[:, kc, cg * 512:(cg + 1) * 512], kc == 0, kc == KC - 1) for kc in range(KC)]
            P.group("pe", fns, reads=[screp, wada], writes=[bk])
            P.op("dve", TT(modb.ap[:, cg * 512:(cg + 1) * 512], bank(cg), bada.ap[:, cg * 512:(cg + 1) * 512], ALU.add),
                 reads=[bk, bada], addwrites=[modb])
        P.op("dve", TS(jd["gate_b"].ap, modb.ap[:, 2048:3072], 0.5, ALU.mult), reads=[modb], writes=[jd["gate_b"]])
        for half in range(2):
            fns = [TR(ps[:, 6 * 512 + blk * 128: 6 * 512 + (blk + 1) * 128], modb.ap[:, half * 1024 + blk * 128: half * 1024 + (blk + 1) * 128], ident_f.ap)
                   for blk in range(8)]
            P.group("pe", fns, reads=[modb, ident_f], writes=[banks[6], banks[7]])
            src = g3(ps[:, 6 * 512: 8 * 512], 128)[:, :, 0]
            dst = jd["shiftT"] if half == 0 else scl
            P.op("dve", CP(dst.ap, src), reads=[banks[6], banks[7]], writes=[dst])
        P.op("dve", STT(jd["Gp"].ap, scl.ap, 1.0, norm_g.ap, ALU.add, ALU.mult), reads=[scl, norm_g], writes=[jd["Gp"]])
    P.barrier()

    def make_norm_bufs(G, nxn=2):
        nt = G // 128
        return dict(
            G=G, nt=nt,
            xg=Ring([ph.alloc([128, nt, D], F32) for _ in range(2)]),
            xn=Ring([ph.alloc([128, nt, D], BF16) for _ in range(nxn)]),
            hT=Ring([ph.alloc([128, KC, G], BF16) for _ in range(2)]),
            ss=Ring([ph.alloc([128, 8], F32) for _ in range(2)]),
            rs=Ring([ph.alloc([128, 8], F32) for _ in range(2)]),
            junk=ph.alloc([128, D], BF16),
            xsem=Ring(["xld0", "xld1"]),
            tb=Ring([0, 1]),
        )

    def load_group(nb, x_ap, g):
        G = nb["G"]
        xg = nb["xg"].get()
        P.dma(xg.ap, x_ap[g * G:(g + 1) * G, :].rearrange("(t p) d -> p t d", p=128), nb["xsem"].get(), writes=[xg])
        return xg

    def norm_group(nb, jd, xg, keep=None, pre=None):
        return norm_b(nb, jd, norm_a(nb, jd, xg, keep=keep, pre=pre))

    def norm_a(nb, jd, xg, keep=None, pre=None):
        G, nt = nb["G"], nb["nt"]
        xn, ss = nb["xn"].get(), nb["ss"].get()
        junk = nb["junk"]
        if pre is not None:
            rs, c0 = pre
        else:
            for i in range(nt):
                P.op("act", ACT(junk.ap, xg.ap[:, i, :], AF.Square, accum=ss.ap[:, i:i + 1]),
                     reads=[xg], addwrites=[ss] if i else (), writes=() if i else [ss])
            if keep is not None:
                rs, c0 = keep
                P.op("act", ACT(rs.ap[:, c0:c0 + nt], ss.ap[:, 0:nt], AF.Sqrt, scale=1.0 / D, bias=eps_t.ap[:, 0:1]), reads=[ss, eps_t], addwrites=[rs])
                P.op("dve", RCP(rs.ap[:, c0:c0 + nt], rs.ap[:, c0:c0 + nt]), reads=[rs], addwrites=[rs])
            else:
                rs, c0 = nb["rs"].get(), 0
                P.op("act", ACT(rs.ap[:, 0:nt], ss.ap[:, 0:nt], AF.Sqrt, scale=1.0 / D, bias=eps_t.ap[:, 0:1]), reads=[ss, eps_t], writes=[rs])
                P.op("dve", RCP(rs.ap[:, 0:nt], rs.ap[:, 0:nt]), reads=[rs], writes=[rs])
        for i in range(nt):
            P.op("dve", TS(xn.ap[:, i, :], xg.ap[:, i, :], rs.ap[:, c0 + i:c0 + i + 1], ALU.mult),
                 reads=[xg, rs], addwrites=[xn] if i else (), writes=() if i else [xn])
        return xn

    def norm_b(nb, jd, xn):
        G, nt = nb["G"], nb["nt"]
        hT = nb["hT"].get()
        for kp in range(KC // 2):
            b = nb["tb"].get()
            fns = []
            for k2 in range(2):
                kc = kp * 2 + k2
                for i in range(nt):
                    fns.append(TR(bank_bf(b, 128, k2 * G + i * 128), xn.ap[:, i, kc * 128:(kc + 1) * 128], ident_b.ap))
            P.group("pe", fns, reads=[xn, ident_b], writes=[banks[b]])
            for k2 in range(2):
                kc = kp * 2 + k2
                first = (kc == 0)
                if k2 == 0:
                    P.op("dve", TS(hT.ap[:, kc, :], bank_bf(b, G, k2 * G), jd["Gp"].ap[:, kc:kc + 1], ALU.mult, jd["shiftT"].ap[:, kc:kc + 1], ALU.add),
                         reads=[banks[b], jd["Gp"], jd["shiftT"]], writes=[hT] if first else (), addwrites=() if first else [hT])
                else:
                    P.op("act", ACT(hT.ap[:, kc, :], bank_bf(b, G, k2 * G), AF.Identity, scale=jd["Gp"].ap[:, kc:kc + 1], bias=jd["shiftT"].ap[:, kc:kc + 1]),
                         reads=[banks[b], jd["Gp"], jd["shiftT"]], addwrites=[hT])
        return hT

    ph.reset()
    wqkv = ph.alloc([128, KC, 1536], BF16)
    wst = Ring([ph.alloc([128, 1536], F32) for _ in range(2)])
    wsem = Ring(["wst0", "wst1"])
    for kc in range(KC):
        st_ = wst.get()
        P.dma(st_.ap, w_in[:, kc, 2048:3584], wsem.get(), writes=[st_])
        if kc % 2:
            P.op("act", ACT(wqkv.ap[:, kc, :], st_.ap, AF.Copy), reads=[st_], addwrites=[wqkv])
        else:
            P.op("dve", CP(wqkv.ap[:, kc, :], st_.ap), reads=[st_], addwrites=[wqkv])
    for j, jd in enumerate(J):
        jd["ropek_s"] = ph.alloc([128, jd["S"] // 128, 16], F32)
        jd["ropeq_s"] = ph.alloc([128, jd["NQ"] // 128, 16], F32)
        P.dma(jd["ropek_s"].ap, jd["ropek"], f"rk{j}", writes=[jd["ropek_s"]])
        P.dma(jd["ropeq_s"].ap, jd["ropeq"], f"rq{j}", writes=[jd["ropeq_s"]])
    nb1 = make_norm_bufs(G1)
    sqb = Ring([ph.alloc([128, 512], F32) for _ in range(4)])
    ss8 = Ring([ph.alloc([128, 8], F32) for _ in range(4)])
    r8 = Ring([ph.alloc([128, 8], F32) for _ in range(4)])
    knb = Ring([ph.alloc([128, 512], F32) for _ in range(4)])
    kn2 = Ring([ph.alloc([128, 512], F32) for _ in range(4)])
    kbb = Ring([ph.alloc([128, 512], BF16) for _ in range(8)])
    rt = Ring([ph.alloc([128, 4, 64], F32) for _ in range(4)])
    kts = Ring([ph.alloc([128, 4, G1], BF16) for _ in range(2)])
    vts = Ring([ph.alloc([128, 4, G1], BF16) for _ in range(2)])
    ksem = Ring(["kst0", "kst1"])
    vsem = Ring(["vst0", "vst1"])
    pjk = Ring([2, 4])
    pjv = Ring([3, 5])
    ptr = Ring([6, 7])

    def qk_gen(hT, i, col0, gvec, rope_s, T, stage, out):
        b = pjk.get()
        bk = banks[b]
        fns = [MM(bank(b), hT.ap[:, kc, i * 128:(i + 1) * 128], wqkv.ap[:, kc, col0:col0 + 512], kc == 0, kc == KC - 1) for kc in range(KC)]
        P.group("pe", fns, reads=[hT, wqkv], writes=[bk])
        yield
        sq, s8, rr, kn, k2, kb, r_ = sqb.get(), ss8.get(), r8.get(), knb.get(), kn2.get(), kbb.get(), rt.get()
        P.op("act", ACT(sq.ap, bank(b), AF.Square), reads=[bk], writes=[sq])
        yield
        P.op("dve", RED(s8.ap, g3(sq.ap, 64)), reads=[sq], writes=[s8])
        yield
        P.op("act", ACT(rr.ap, s8.ap, AF.Sqrt, scale=1.0 / 64, bias=eps_t.ap[:, 0:1]), reads=[s8, eps_t], writes=[rr])
        yield
        P.op("dve", RCP(rr.ap, rr.ap), reads=[rr], writes=[rr])
        kn3, k23, kb3 = g3(kn.ap, 64), g3(k2.ap, 64), g3(kb.ap, 64)
        P.op("dve", TT(kn3, g3(bank(b), 64), rr.ap.unsqueeze(2).to_broadcast([128, 8, 64]), ALU.mult), reads=[bk, rr], writes=[kn])
        yield
        P.op("dve", TT(k23, kn3, gvec.ap.unsqueeze(1).to_broadcast([128, 8, 64]), ALU.mult), reads=[kn, gvec], writes=[k2])
        yield
        P.op("act", ACT(kb3[:, :, 16:64], k23[:, :, 16:64], AF.Copy), reads=[k2], writes=[kb])
        cosb = rope_s.ap[:, T, 0:8].unsqueeze(1).to_broadcast([128, 8, 8])
        sinb = rope_s.ap[:, T, 8:16].unsqueeze(1).to_broadcast([128, 8, 8])
        x1, x2 = k23[:, :, 0:8], k23[:, :, 8:16]
        rq = [g3(r_.ap[:, q, :], 8) for q in range(4)]
        P.op("pool", TT(rq[0], x1, cosb, ALU.mult), reads=[k2, rope_s], writes=[r_])
        P.op("pool", TT(rq[1], x2, sinb, ALU.mult), reads=[k2], addwrites=[r_])
        yield
        P.op("pool", TT(rq[2], x2, cosb, ALU.mult), reads=[k2], addwrites=[r_])
        P.op("pool", TT(rq[3], x1, sinb, ALU.mult), reads=[k2], addwrites=[r_])
        yield
        P.op("pool", TT(kb3[:, :, 0:8], rq[0], rq[1], ALU.subtract), reads=[r_], addwrites=[kb])
        P.op("pool", TT(kb3[:, :, 8:16], rq[2], rq[3], ALU.add), reads=[r_], addwrites=[kb])

        def part_b():
            tb = ptr.get()
            fns = [TR(bank_bf(tb, 128, h * 128), kb.ap[:, h * 128:(h + 1) * 128], ident_b.ap) for h in range(4)]
            P.group("pe", fns, reads=[kb, ident_b], writes=[banks[tb]])
            P.op("act", ACT(stage.ap[:, :, i * 128:(i + 1) * 128], g3(bank_bf(tb, 512), 128), AF.Copy), reads=[banks[tb]], addwrites=[stage])
        out.append(part_b)

    def run_pair(gens, after_first=None):
        live = list(gens)
        first = True
        while live:
            nxt_live = []
            for gn in live:
                try:
                    next(gn)
                    nxt_live.append(gn)
                except StopIteration:
                    pass
            live = nxt_live
            if first and after_first is not None:
                after_first()
            first = False

    def v_post(hT, i, stage):
        b = pjv.get()
        bk = banks[b]
        fns = [MM(bank(b), hT.ap[:, kc, i * 128:(i + 1) * 128], wqkv.ap[:, kc, 1024:1536], kc == 0, kc == KC - 1) for kc in range(KC)]
        P.group("pe", fns, reads=[hT, wqkv], writes=[bk])
        P.op("dve", CP(stage.ap[:, :, i * 128:(i + 1) * 128], g3(bank(b), 128)), reads=[bk], addwrites=[stage])

    pending = []
    xn_nxt = None
    hT_next = None
    for jd in J:
        S, NQ = jd["S"], jd["NQ"]
        work = [("kv", g) for g in range(S // G1)] + [("q", g) for g in range(NQ // G1)]
        xsrc = {"kv": jd["xf"], "q": jd["xq"]}
        def keep_of(w):
            return (jd["rstd_q"], w[1] * (G1 // 128)) if w[0] == "q" else None

        xg_cur = load_group(nb1, xsrc[work[0][0]], work[0][1])
        hT_next = norm_b(nb1, jd, norm_a(nb1, jd, xg_cur, keep=keep_of(work[0])))
        for wi, (kind, g) in enumerate(work):
            hT = hT_next
            more = wi + 1 < len(work)
            if more:
                xg_nxt = load_group(nb1, xsrc[work[wi + 1][0]], work[wi + 1][1])

            def hook(i, wi=wi, more=more):
                nonlocal hT_next, xn_nxt
                if not more:
                    return
                if i == 0:
                    xn_nxt = norm_a(nb1, jd, xg_nxt, keep=keep_of(work[wi + 1]))
                if i == 2:
                    hT_next = norm_b(nb1, jd, xn_nxt)
            ks = kts.get()
            P.op("pool", MS(ks.ap[:, 0, 0:2], 0.0), writes=[ks])
            if kind == "kv":
                vs = vts.get()
                P.op("pool", MS(vs.ap[:, 0, 0:2], 0.0), writes=[vs])
                for i0 in range(0, G1 // 128, 2):
                    outs = []
                    gens = [qk_gen(hT, i, 512, kg_b, jd["ropek_s"], g * (G1 // 128) + i, ks, outs) for i in (i0, i0 + 1)]

                    def vproj(i0=i0, hT=hT, vs=vs):
                        v_post(hT, i0, vs)
                        v_post(hT, i0 + 1, vs)
                        if i0 == 0:
                            hook(0)
                    run_pair(gens, after_first=vproj)
                    while pending:
                        pending.pop(0)()
                    pending.extend(outs)
                    if i0 == 0:
                        hook(2)

                def stores(ks=ks, vs=vs, g=g, jd=jd):
                    P.dma(jd["kt_d"][:, :, g * G1:(g + 1) * G1].rearrange("h p s -> p h s"), ks.ap, ksem.get(), reads=[ks])
                    P.dma(jd["v_d"][:, :, g * G1:(g + 1) * G1].rearrange("h p s -> p h s"), vs.ap, vsem.get(), reads=[vs])
                pending.append(stores)
            else:
                for i0 in range(0, G1 // 128, 2):
                    outs = []
                    gens = [qk_gen(hT, i, 0, qg_b, jd["ropeq_s"], g * (G1 // 128) + i, ks, outs) for i in (i0, i0 + 1)]
                    run_pair(gens, after_first=(lambda: hook(0)) if i0 == 0 else None)
                    while pending:
                        pending.pop(0)()
                    pending.extend(outs)
                    if i0 == 0:
                        hook(2)

                def stores(ks=ks, g=g, jd=jd):
                    P.dma(jd["qt_d"][:, :, g * G1:(g + 1) * G1].rearrange("h p s -> p h s"), ks.ap, ksem.get(), reads=[ks])
                pending.append(stores)
    while pending:
        pending.pop(0)()
    P.barrier()

    ph.reset()
    SMAX = max(jd["S"] for jd in J)
    NQMAX = max(jd["NQ"] for jd in J)
    ktb = Ring([ph.alloc([128, SMAX], BF16) for _ in range(2)])
    vtb = Ring([ph.alloc([128, SMAX], BF16) for _ in range(2)])
    qtb = Ring([ph.alloc([128, NQMAX], BF16) for _ in range(2)])
    kvsem = Ring(["kvl0", "kvl1"])
    Pb = Ring([ph.alloc([128, 1024], BF16) for _ in range(3)])
    Sb = Ring([0, 2])
    rinv = ph.alloc([128, 1024], F32)
    DW = 512
    raccs = Ring([(ph.alloc([128, DW], F32), None) for _ in range(2)])
    osb0 = ph.alloc([128, 512], F32)
    osb1 = ph.alloc([128, 512], F32)
    rsb = ph.alloc([128, 1024], F32)
    deferred = []
    t0b = ph.alloc([128, 512], F32)
    t1b = ph.alloc([128, 512], F32)
    ost = Ring([ph.alloc([128, QT], BF16) for _ in range(2)])
    osem = Ring(["ost0", "ost1"])

    def load_head(jd, h):
        S, NQ = jd["S"], jd["NQ"]
        kt, vt, qt_, sem = ktb.get(), vtb.get(), qtb.get(), kvsem.get()
        nsp = max(1, S // 4096)
        w = S // nsp
        for sp in range(nsp):
            P.dma(kt.ap[:, sp * w:(sp + 1) * w], jd["kt_d"][h, :, sp * w:(sp + 1) * w], sem, writes=[kt] if sp == 0 else (), addwrites=[kt] if sp else ())
        for sp in range(nsp):
            P.dma(vt.ap[:, sp * w:(sp + 1) * w], jd["v_d"][h, :, sp * w:(sp + 1) * w], sem, writes=[vt] if sp == 0 else (), addwrites=[vt] if sp else ())
        tq = P.dma(qt_.ap[:, 0:NQ], jd["qt_d"][h, :, :], sem, writes=[qt_])
        kt.wr = {tq[0]: tq[1]}
        vt.wr = {tq[0]: tq[1]}
        return kt, vt, qt_

    heads = [(jd, h) for jd in J for h in range(4)]
    nxt = load_head(*heads[0])
    for hi, (jd, h) in enumerate(heads):
        kt, vt, qt_ = nxt
        if hi + 1 < len(heads):
            nxt = load_head(*heads[hi + 1])
        S, NQ = jd["S"], jd["NQ"]
        nkt = S // 128
        for qi in range(NQ // QT):
            racc, raccp = raccs.get()

            def qk(u):
                sb = Sb.get()
                fns = [MM(bank(sb), kt.ap[0:64, u * 128:(u + 1) * 128], qt_.ap[0:64, qi * QT:(qi + 1) * QT], True, True, (0, 0)),
                       MM(bank(sb + 1), kt.ap[64:128, u * 128:(u + 1) * 128], qt_.ap[64:128, qi * QT:(qi + 1) * QT], True, True, (64, 0))]
                P.group("pe", fns, reads=[kt, qt_], writes=[banks[sb], banks[sb + 1]])
                pb = Pb.get()
                P.op("act", ACT(pb.ap, ps[:, sb * 512: sb * 512 + 1024], AF.Exp, scale=0.125), reads=[banks[sb], banks[sb + 1]], writes=[pb])
                return pb

            def pv(u, pb):
                vcol = u * 128
                first, last = (u == 0), (u == nkt - 1)
                fns = [MM(bank(4), vt.ap[:, vcol:vcol + 128], pb.ap[:, 0:512], first, last),
                       MM(bank(5), vt.ap[:, vcol:vcol + 128], pb.ap[:, 512:1024], first, last),
                       MM(bank(7, 1024 - DW, DW - 512), ones_b.ap, pb.ap[:, DW:1024], first, last)]
                acc = [banks[4], banks[5], banks[7]]
                if first:
                    P.group("pe", fns, reads=[vt, pb, ones_b], writes=acc)
                    P.op("dve", CP(racc.ap, pb.ap[:, 0:DW]), reads=[pb], writes=[racc])
                else:
                    P.group("pe", fns, reads=[vt, pb, ones_b], addwrites=acc)
                    P.op("dve", TT(racc.ap, racc.ap, pb.ap[:, 0:DW], ALU.add), reads=[pb, racc], writes=[racc])

            pbs = {0: qk(0)}
            if nkt > 1:
                pbs[1] = qk(1)
            for u in range(nkt):
                if u + 2 < nkt:
                    pbs[u + 2] = qk(u + 2)
                pv(u, pbs.pop(u))
                if deferred:
                    deferred.pop(0)()
            while deferred:
                deferred.pop(0)()
            P.op("dve", CP(osb0.ap, bank(4)), reads=[banks[4]], writes=[osb0])
            P.op("dve", CP(osb1.ap, bank(5)), reads=[banks[5]], writes=[osb1])
            P.op("dve", CP(rsb.ap[:, DW:1024], bank(7, 1024 - DW, DW - 512)), reads=[banks[7]], writes=[rsb])

            def tot(racc=racc, raccp=raccp):
                P.group("pe", [MM(bank(6), ones_f.ap, racc.ap[:, 0:512], True, True)], reads=[racc, ones_f], writes=[banks[6]])
                P.op("dve", CP(rsb.ap[:, 0:512], bank(6)), reads=[banks[6]], addwrites=[rsb])
                if DW > 512:
                    P.group("pe", [MM(bank(6, DW - 512), ones_f.ap, racc.ap[:, 512:DW], True, True)], reads=[racc, ones_f], writes=[banks[6]])
                    P.op("dve", CP(rsb.ap[:, 512:DW], bank(6, DW - 512)), reads=[banks[6]], addwrites=[rsb])
            nch = 16 if nkt >= 32 else 8
            cw = 1024 // nch

            def mk_rcp(c):
                def f():
                    P.op("dve", RCP(rinv.ap[:, c * cw:(c + 1) * cw], rsb.ap[:, c * cw:(c + 1) * cw]), reads=[rsb],
                         writes=[rinv] if c == 0 else (), addwrites=[rinv] if c else ())
                return f

            def fin(jd=jd, h=h, qi=qi):
                os_ = ost.get()
                P.op("dve", TT(t0b.ap, osb0.ap, rinv.ap[:, 0:512], ALU.mult), reads=[osb0, rinv], writes=[t0b])
                P.op("dve", TT(t1b.ap, osb1.ap, rinv.ap[:, 512:1024], ALU.mult), reads=[osb1, rinv], writes=[t1b])
                P.op("dve", STT(os_.ap, t1b.ap, neg_lam.ap[:, 0:1], t0b.ap, ALU.mult, ALU.add), reads=[t0b, t1b, neg_lam], writes=[os_])
                P.dma(jd["o_d"][h, :, qi * QT:(qi + 1) * QT], os_.ap, osem.get(), reads=[os_])
            deferred.extend([tot] + [mk_rcp(c) for c in range(nch)] + [fin])
    while deferred:
        deferred.pop(0)()
    P.barrier()

    ph.reset()
    w3 = ph.alloc([128, KC, 4608], BF16)
    wco = ph.alloc([128, 4, D], BF16)
    wao = ph.alloc([128, 4, D], BF16)
    wo = ph.alloc([128, KC, D], BF16)
    wst3 = Ring([ph.alloc([128, 1024], F32) for _ in range(2)])
    wsem3 = Ring(["wst0", "wst1"])
    cast_eng = Ring(["act", "dve"])

    def load_cast(dst_ap, src_ap, n, dstbuf):
        st_ = wst3.get()
        P.dma(st_.ap[:, 0:n], src_ap, wsem3.get(), writes=[st_])
        ce_ = cast_eng.get()
        if ce_ == "act":
            P.op("act", ACT(dst_ap, st_.ap[:, 0:n], AF.Copy), reads=[st_], addwrites=[dstbuf])
        else:
            P.op("dve", CP(dst_ap, st_.ap[:, 0:n]), reads=[st_], addwrites=[dstbuf])

    for kc in range(KC):
        for c0, s0, n in ((0, 0, 1024), (1024, 1024, 1024), (2048, 3584, 1024), (3072, 4608, 1024), (4096, 5632, 512)):
            load_cast(w3.ap[:, kc, c0:c0 + n], w_in[:, kc, s0:s0 + n], n, w3)
        load_cast(wo.ap[:, kc, :], wo_d[:, kc, :], 1024, wo)
    for m in range(4):
        load_cast(wco.ap[:, m, :], wco_d[:, m, :], 1024, wco)
        load_cast(wao.ap[:, m, :], wao_d[:, m, :], 1024, wao)
    nb3 = make_norm_bufs(G3, 1)
    NT3 = G3 // 128
    oin = Ring([ph.alloc([128, 4, G3], BF16) for _ in range(2)])
    oisem = Ring(["oin0", "oin1"])
    yT = ph.alloc([128, 4, G3], BF16)
    ogT = ph.alloc([128, 4, G3], BF16)
    mT = ph.alloc([128, KC, G3], BF16)
    yo = Ring([ph.alloc([128, D], F32) for _ in range(2)])
    yosem = Ring(["yo0", "yo1"])
    tmpf = Ring([ph.alloc([128, G3], F32) for _ in range(10)])
    tmpg = Ring([ph.alloc([128, 512], F32) for _ in range(2)])
    osq = ph.alloc([128, 4, G3], BF16)
    rsa = ph.alloc([128, 4 * G3], F32)
    uext = Ring([ph.alloc([128, G3 + 2], F32) for _ in range(2)])
    pr = Ring([2, 3, 4, 5])
    pr2 = Ring([6, 7])
    xh = yo.items[0]
    xhn = ph.alloc([32, D], BF16)
    hTh = ph.alloc([128, KC, 32], BF16)
    ssh = ph.alloc([32, 2], F32)
    uh = ph.alloc([128, 4, 32], F32)
    cxh = ph.alloc([128, 32], F32)

    def proj(col, hT_buf, hT_ap, n):
        b = pr.get()
        fns = [MM(bank(b, n), w3.ap[:, kc, col:col + 128], hT_ap[:, kc, :], kc == 0, kc == KC - 1) for kc in range(KC)]
        P.group("pe", fns, reads=[hT_buf, w3], writes=[banks[b]])
        return b

    def tanh_half(b, n, dst):
        P.op("act", ACT(dst.ap[:, 0:n], bank(b, n), AF.Tanh, scale=0.5), reads=[banks[b]], writes=[dst])

    def load3(jd, g):
        xg = load_group(nb3, jd["xq"], g)
        oi = oin.get()
        P.dma(oi.ap, jd["o_d"][:, :, g * G3:(g + 1) * G3].rearrange("h p s -> p h s"), oisem.get(), writes=[oi])
        return xg, oi

    for j, jd in enumerate(J):
        NQ, ng3 = jd["NQ"], jd["ng3"]
        nh = 2 * ng3
        P.dma(xh.ap[0:nh, :], jd["halo"], f"hl{j}", writes=[xh])
        P.op("act", ACT(nb3["junk"].ap[0:nh, :], xh.ap[0:nh, :], AF.Square, accum=ssh.ap[0:nh, 0:1]), reads=[xh], writes=[ssh])
        P.op("act", ACT(ssh.ap[0:nh, 1:2], ssh.ap[0:nh, 0:1], AF.Sqrt, scale=1.0 / D, bias=eps_t.ap[0:nh, 0:1]), reads=[ssh, eps_t], writes=[ssh])
        P.op("dve", RCP(ssh.ap[0:nh, 1:2], ssh.ap[0:nh, 1:2]), reads=[ssh], writes=[ssh])
        P.op("dve", TS(xhn.ap[0:nh, :], xh.ap[0:nh, :], ssh.ap[0:nh, 1:2], ALU.mult), reads=[xh, ssh], writes=[xhn])
        fns = [TR(bank_bf(0, nh, kc * 32), xhn.ap[0:nh, kc * 128:(kc + 1) * 128], ident_b.ap[0:nh, 0:nh]) for kc in range(KC)]
        P.group("pe", fns, reads=[xhn, ident_b], writes=[banks[0]])
        for kc in range(KC):
            P.op("dve", TS(hTh.ap[:, kc, 0:nh], bank_bf(0, nh, kc * 32), jd["Gp"].ap[:, kc:kc + 1], ALU.mult, jd["shiftT"].ap[:, kc:kc + 1], ALU.add),
                 reads=[banks[0], jd["Gp"], jd["shiftT"]], writes=[hTh] if kc == 0 else (), addwrites=[hTh] if kc else ())
        for m in range(4):
            bc = proj(512 + m * 128, hTh, hTh.ap[:, :, 0:nh], nh)
            bx = proj(1024 + m * 128, hTh, hTh.ap[:, :, 0:nh], nh)
            P.op("act", ACT(cxh.ap[:, 0:nh], bank(bx, nh), AF.Copy), reads=[banks[bx]], writes=[cxh])
            P.op("dve", TT(uh.ap[:, m, 0:nh], bank(bc, nh), cxh.ap[:, 0:nh], ALU.mult), reads=[banks[bc], cxh], writes=[uh] if m == 0 else (),
                 addwrites=[uh] if m else ())
            P.op("dve", TT(uh.ap[:, m, 0:nh], uh.ap[:, m, 0:nh], jd["hmask_s"].ap[:, 0:nh], ALU.mult), reads=[uh, jd["hmask_s"]], addwrites=[uh])
        cur = load3(jd, 0)
        hT_n3 = norm_group(nb3, jd, cur[0], pre=(jd["rstd_q"], 0))
        for g in range(ng3):
            xg, oi = cur
            hT = hT_n3
            more3 = g + 1 < ng3
            if more3:
                nxt = load3(jd, g + 1)
            P.op("dve", TT(osq.ap, oi.ap, oi.ap, ALU.mult), reads=[oi], writes=[osq])
            P.group("pe", [MM(bank(6 + hh // 2, G3, (hh % 2) * G3), ones_b.ap, osq.ap[:, hh, :], True, True) for hh in range(4)],
                    reads=[osq, ones_b], writes=[banks[6], banks[7]])
            P.op("act", ACT(rsa.ap, ps[:, 6 * 512: 8 * 512], AF.Sqrt, scale=1.0 / 128, bias=eps_t.ap[:, 0:1]), reads=[banks[6], banks[7], eps_t], writes=[rsa])
            P.op("dve", RCP(rsa.ap, rsa.ap), reads=[rsa], writes=[rsa])
            for m in range(4):
                bc = proj(512 + m * 128, hT, hT.ap, G3)
                bx = proj(1024 + m * 128, hT, hT.ap, G3)
                cxs, ue, cv = tmpf.get(), uext.get(), tmpf.get()
                P.op("act", ACT(cxs.ap, bank(bx, G3), AF.Copy), reads=[banks[bx]], writes=[cxs])
                P.op("dve", TT(ue.ap[:, 1:G3 + 1], bank(bc, G3), cxs.ap, ALU.mult), reads=[banks[bc], cxs], writes=[ue])
                P.op("dve", CP(ue.ap[:, 0:1], uh.ap[:, m, 2 * g:2 * g + 1]), reads=[uh], addwrites=[ue])
                P.op("dve", CP(ue.ap[:, G3 + 1:G3 + 2], uh.ap[:, m, 2 * g + 1:2 * g + 2]), reads=[uh], addwrites=[ue])
                P.op("dve", TS(cv.ap, ue.ap[:, 0:G3], conv_l.ap[:, m * 3:m * 3 + 1], ALU.mult), reads=[ue, conv_l], writes=[cv])
                P.op("dve", STT(cv.ap, ue.ap[:, 1:G3 + 1], conv_l.ap[:, m * 3 + 1:m * 3 + 2], cv.ap, ALU.mult, ALU.add), reads=[ue, cv], writes=[cv])
                P.op("dve", STT(cv.ap, ue.ap[:, 2:G3 + 2], conv_l.ap[:, m * 3 + 2:m * 3 + 3], cv.ap, ALU.mult, ALU.add), reads=[ue, cv], writes=[cv])
                bb_ = proj(0 + m * 128, hT, hT.ap, G3)
                bz = proj(1536 + m * 128, hT, hT.ap, G3)
                th, y1 = tmpf.get(), tmpf.get()
                tanh_half(bz, G3, th)
                P.op("dve", STT(th.ap, th.ap, 1.0, bank(bz, G3), ALU.add, ALU.mult), reads=[banks[bz], th], writes=[th])
                P.op("dve", TT(y1.ap, bank(bb_, G3), cv.ap, ALU.mult), reads=[banks[bb_], cv], writes=[y1])
                P.op("dve", TT(yT.ap[:, m, :], y1.ap, th.ap, ALU.mult), reads=[y1, th], writes=[yT] if m == 0 else (), addwrites=[yT] if m else ())
            for h in range(4):
                ba = proj(2048 + h * 128, hT, hT.ap, G3)
                th, on = tmpf.get(), tmpf.get()
                tanh_half(ba, G3, th)
                P.op("dve", STT(th.ap, th.ap, 1.0, bank(ba, G3), ALU.add, ALU.mult), reads=[banks[ba], th], writes=[th])
                P.op("dve", STT(on.ap, oi.ap[:, h, :], sgs.ap[:, 0:1], rsa.ap[:, h * G3:(h + 1) * G3], ALU.mult, ALU.mult), reads=[oi, sgs, rsa], writes=[on])
                P.op("dve", TT(ogT.ap[:, h, :], on.ap, th.ap, ALU.mult), reads=[on, th], writes=[ogT] if h == 0 else (), addwrites=[ogT] if h else ())
            if more3:
                xn_n3 = norm_a(nb3, jd, nxt[0], pre=(jd["rstd_q"], (g + 1) * NT3))
            for f in range(KC):
                bga = proj(2560 + f * 128, hT, hT.ap, G3)
                bgb = proj(3584 + f * 128, hT, hT.ap, G3)
                bA = pr2.get()
                P.group("pe", [MM(bank(bA, G3), wco.ap[:, m, f * 128:(f + 1) * 128], yT.ap[:, m, :], m == 0, m == 3) for m in range(4)],
                        reads=[yT, wco], writes=[banks[bA]])
                bB = pr2.get()
                P.group("pe", [MM(bank(bB, G3), wao.ap[:, h, f * 128:(f + 1) * 128], ogT.ap[:, h, :], h == 0, h == 3) for h in range(4)],
                        reads=[ogT, wao], writes=[banks[bB]])
                sa, sb_, ma, mb = tmpf.get(), tmpf.get(), tmpf.get(), tmpf.get()
                tanh_half(bga, G3, sa)
                tanh_half(bgb, G3, sb_)
                P.op("dve", STT(ma.ap, sa.ap, 1.0, bank(bA, G3), ALU.add, ALU.mult), reads=[banks[bA], sa], writes=[ma])
                P.op("dve", STT(mb.ap, sb_.ap, 1.0, bank(bB, G3), ALU.add, ALU.mult), reads=[banks[bB], sb_], writes=[mb])
                P.op("dve", TT(mT.ap[:, f, :], ma.ap, mb.ap, ALU.add), reads=[ma, mb], writes=[mT] if f == 0 else (), addwrites=[mT] if f else ())
            if more3:
                hT_n3 = norm_b(nb3, jd, xn_n3)
                cur = nxt
            for i in range(NT3):
                yo_ = yo.get()
                for hf in range(2):
                    bo = pr2.get()
                    P.group("pe", [MM(bank(bo), mT.ap[:, f, i * 128:(i + 1) * 128], wo.ap[:, f, hf * 512:(hf + 1) * 512], f == 0, f == KC - 1) for f in range(KC)],
                            reads=[mT, wo], writes=[banks[bo]])
                    tg = tmpg.get()
                    P.op("dve", TT(tg.ap, bank(bo), jd["gate_b"].ap[:, hf * 512:(hf + 1) * 512], ALU.mult), reads=[banks[bo], jd["gate_b"]], writes=[tg])
                    P.op("dve", TT(yo_.ap[:, hf * 512:(hf + 1) * 512], tg.ap, xg.ap[:, i, hf * 512:(hf + 1) * 512], ALU.add),
                         reads=[tg, xg], writes=[yo_] if hf == 0 else (), addwrites=[yo_] if hf else ())
                P.dma(jd["y"][g * G3 + i * 128: g * G3 + (i + 1) * 128, :], yo_.ap, yosem.get(), reads=[yo_])
    P.barrier()
    P.replay(nc)
    return nc


ROT_HALF = 8
ROPE_THETA = 500000.0


def _rope_table(pos):
    inv = (np.float32(ROPE_THETA) ** (-np.arange(ROT_HALF, dtype=np.float32) / np.float32(ROT_HALF))).astype(np.float32)
    ang = pos.astype(np.float32)[:, None] * inv[None, :]
    return np.ascontiguousarray(np.concatenate([np.cos(ang), np.sin(ang)], axis=1).astype(np.float32))


def _job_inputs(j, x_seq, q0, NQ, cvec):
    S = x_seq.shape[0]
    ng3 = NQ // G3
    halo = np.zeros((2 * ng3, D), np.float32)
    hmask = np.zeros((2 * ng3,), np.float32)
    for g in range(ng3):
        for side, idx in ((0, q0 + g * G3 - 1), (1, q0 + (g + 1) * G3)):
            if 0 <= idx < S:
                halo[2 * g + side] = x_seq[idx]
                hmask[2 * g + side] = 1.0
    rope = _rope_table(np.arange(S))
    return {
        f"xf{j}": np.ascontiguousarray(x_seq),
        f"xq{j}": np.ascontiguousarray(x_seq[q0:q0 + NQ]),
        f"halo{j}": halo,
        f"hmask{j}": np.ascontiguousarray(np.broadcast_to(hmask[None, :], (128, 2 * ng3))),
        f"ropek{j}": rope,
        f"ropeq{j}": np.ascontiguousarray(rope[q0:q0 + NQ]),
        f"cl{j}": np.ascontiguousarray(cvec.reshape(8, 128).T),
    }


def _shared_inputs(norm_g, w_ada, b_ada, w_in, conv_w, q_norm_g, k_norm_g, lam_q1, lam_k1, lam_q2, lam_k2, subln_g,
                   w_conv_out, w_attn_out, w_out):
    f = lambda a: np.ascontiguousarray(np.asarray(a, dtype=np.float32))
    return {
        "w_in": f(w_in[0]), "w_ada": f(w_ada[0]), "b_ada": f(b_ada[0]),
        "norm_g_l": f(np.asarray(norm_g[0]).reshape(8, 128).T),
        "conv_l": f(np.asarray(conv_w[0]).reshape(3, 4, 128).transpose(2, 1, 0).reshape(128, 12)),
        "qg": f(q_norm_g[0]), "kg": f(k_norm_g[0]),
        "lamv": f(np.concatenate([np.asarray(lam_q1[0]), np.asarray(lam_k1[0]), np.asarray(lam_q2[0]), np.asarray(lam_k2[0])])),
        "subln": f(np.asarray(subln_g[0]).reshape(128, 1)),
        "w_conv_out": f(w_conv_out[0]), "w_attn_out": f(w_attn_out[0]), "w_out": f(w_out[0]),
        "ident": np.eye(128, dtype=np.float32),
    }


def kernel(x_prompt, x_sample, c_prompt, c_sample, norm_g, w_ada, b_ada, w_in, conv_w, q_norm_g, k_norm_g,
           lam_q1, lam_k1, lam_q2, lam_k2, subln_g, w_conv_out, w_attn_out, w_out):
    x_prompt = np.asarray(x_prompt, dtype=np.float32)
    x_sample = np.asarray(x_sample, dtype=np.float32)
    c_prompt = np.asarray(c_prompt, dtype=np.float32)
    c_sample = np.asarray(c_sample, dtype=np.float32)
    n = 8
    B, S0, _ = x_prompt.shape
    B1, S1, _ = x_sample.shape
    NQ1 = S1 * B1 // n
    per_seq = S1 // NQ1
    nc = build([dict(S=S0, NQ=S0), dict(S=S1, NQ=NQ1)])
    shared = _shared_inputs(norm_g, w_ada, b_ada, w_in, conv_w, q_norm_g, k_norm_g, lam_q1, lam_k1, lam_q2, lam_k2, subln_g,
                            w_conv_out, w_attn_out, w_out)
    in_maps = []
    for c in range(n):
        m = dict(shared)
        m.update(_job_inputs(0, x_prompt[c], 0, S0, c_prompt[c]))
        b, jq = c // per_seq, c % per_seq
        m.update(_job_inputs(1, x_sample[b], jq * NQ1, NQ1, c_sample[b]))
        in_maps.append(m)
    res = run_bass_kernel_spmd(nc, in_maps, core_ids=list(range(n)))
    y_prompt = np.stack([np.asarray(res.results[c]["y0"], dtype=np.float32) for c in range(n)], axis=0)
    y_sample = np.zeros((B1, S1, D), np.float32)
    for c in range(n):
        b, jq = c // per_seq, c % per_seq
        y_sample[b, jq * NQ1:(jq + 1) * NQ1] = np.asarray(res.results[c]["y1"], dtype=np.float32)
    return (y_prompt, y_sample)
```

```python
import math
import contextlib
import numpy as np
import concourse.bass as bass
import concourse.mybir as mybir
from concourse.bass_utils import run_bass_kernel_spmd

F32 = mybir.dt.float32
BF16 = mybir.dt.bfloat16
AF = mybir.ActivationFunctionType
ALU = mybir.AluOpType
AX = mybir.AxisListType

D = 1024
KC = 8
EPS = 1e-6
LAM_INIT = 0.8 - 0.6 * math.exp(-0.3 * 0)
SBUF_LO = 16512
SBUF_HI = 229344
G1 = 512
G3 = 256
QT = 512


class Buf:
    def __init__(self, ap):
        self.ap = ap
        self.wr = {}
        self.rd = {}


class Prog:
    def __init__(self):
        self.q = {e: [] for e in ("sync", "pe", "act", "dve", "pool")}
        self.cnt = {}
        self.waited = {e: {} for e in self.q}

    def wait(self, eng, tok):
        if tok is None:
            return
        s, v = tok
        if self.waited[eng].get(s, 0) >= v:
            return
        self.waited[eng][s] = v
        self.q[eng].append(("w", s, v))

    def _hz(self, reads, writes, extra):
        waits = list(extra)
        for b in reads:
            waits += list(b.wr.items())
        for b in writes:
            waits += list(b.rd.items())
            if not b.rd:
                waits += list(b.wr.items())
        return waits

    def _reg(self, tok, reads, writes, addwrites):
        s, v = tok
        for b in reads:
            b.rd[s] = max(b.rd.get(s, 0), v)
        for b in writes:
            b.wr = {s: v}
            b.rd = {}
        for b in addwrites:
            b.wr[s] = max(b.wr.get(s, 0), v)

    def op(self, eng, fn, reads=(), writes=(), addwrites=(), extra=()):
        for t in self._hz(reads, tuple(writes) + tuple(addwrites), extra):
            self.wait(eng, t)
        s = "s_" + eng
        self.cnt[s] = self.cnt.get(s, 0) + 1
        tok = (s, self.cnt[s])
        self.q[eng].append(("o", fn, tok))
        self._reg(tok, reads, writes, addwrites)
        return tok

    def group(self, eng, fns, reads=(), writes=(), addwrites=(), extra=()):
        for t in self._hz(reads, tuple(writes) + tuple(addwrites), extra):
            self.wait(eng, t)
        for fn in fns[:-1]:
            self.q[eng].append(("o", fn, None))
        s = "s_" + eng
        self.cnt[s] = self.cnt.get(s, 0) + 1
        tok = (s, self.cnt[s])
        self.q[eng].append(("o", fns[-1], tok))
        self._reg(tok, reads, writes, addwrites)
        return tok

    def dma(self, out, in_, sem, reads=(), writes=(), addwrites=(), extra=(), eng="sync"):
        for t in self._hz(reads, tuple(writes) + tuple(addwrites), extra):
            self.wait(eng, t)
        self.cnt[sem] = self.cnt.get(sem, 0) + 16
        tok = (sem, self.cnt[sem])
        self.q[eng].append(("d", out, in_, tok))
        self._reg(tok, reads, writes, addwrites)
        return tok

    def barrier(self):
        toks = [(s, v) for s, v in self.cnt.items()]
        for e in self.q:
            for t in toks:
                self.wait(e, t)

    def replay(self, nc):
        for s, v in self.cnt.items():
            assert v < 60000, (s, v)
        with contextlib.ExitStack() as st:
            sems = {name: st.enter_context(nc.semaphore(name)) for name in sorted(self.cnt)}
            block = st.enter_context(nc.Block())

            def run(items):
                def f(eng):
                    n = len(items)
                    i = 0
                    while i < n:
                        it = items[i]
                        if it[0] == "w":
                            nx = items[i + 1] if i + 1 < n else None
                            if nx is not None and nx[0] == "o" and getattr(nx[1], "embed_ok", False):
                                ins = nx[1](eng)
                                ins._wait_ge(sems[it[1]], it[2])
                                if nx[2] is not None:
                                    ins.then_inc(sems[nx[2][0]], 1)
                                i += 2
                                continue
                            eng.wait_ge(sems[it[1]], it[2])
                        elif it[0] == "o":
                            ins = it[1](eng)
                            if it[2] is not None:
                                ins.then_inc(sems[it[2][0]], 1)
                        else:
                            eng.dma_start(out=it[1], in_=it[2]).then_inc(sems[it[3][0]], 16)
                        i += 1
                return f

            block.sync(run(self.q["sync"]))
            block.tensor(run(self.q["pe"]))
            block.scalar(run(self.q["act"]))
            block.vector(run(self.q["dve"]))
            block.gpsimd(run(self.q["pool"]))


class Ring:
    def __init__(self, items):
        self.items = items
        self.i = 0

    def get(self):
        b = self.items[self.i % len(self.items)]
        self.i += 1
        return b


def build(jobs):
    nc = bass.Bass("TRN2", target_bir_lowering=False)
    P = Prog()
    uid = [0]

    def din(name, shape, dt=F32):
        return nc.dram_tensor(name, list(shape), dt, kind="ExternalInput").ap()

    w_in = din("w_in", [D, 6144]).rearrange("(k p) c -> p k c", p=128)
    w_ada = din("w_ada", [D, 3072]).rearrange("(k p) c -> p k c", p=128)
    b_ada = din("b_ada", [3072])
    norm_g_d = din("norm_g_l", [128, 8])
    conv_d = din("conv_l", [128, 12])
    qg_d = din("qg", [64])
    kg_d = din("kg", [64])
    lam_d = din("lamv", [256])
    subln_d = din("subln", [128, 1])
    wco_d = din("w_conv_out", [512, D]).rearrange("(k p) c -> p k c", p=128)
    wao_d = din("w_attn_out", [512, D]).rearrange("(k p) c -> p k c", p=128)
    wo_d = din("w_out", [D, D]).rearrange("(k p) c -> p k c", p=128)
    ident_d = din("ident", [128, 128])
    J = []
    for j, cfg in enumerate(jobs):
        S, NQ = cfg["S"], cfg["NQ"]
        ng3 = NQ // G3
        jd = dict(S=S, NQ=NQ, ng3=ng3)
        jd["xf"] = din(f"xf{j}", [S, D])
        jd["xq"] = din(f"xq{j}", [NQ, D])
        jd["halo"] = din(f"halo{j}", [2 * ng3, D])
        jd["hmask"] = din(f"hmask{j}", [128, 2 * ng3])
        jd["ropek"] = din(f"ropek{j}", [S, 16]).rearrange("(t p) c -> p t c", p=128)
        jd["ropeq"] = din(f"ropeq{j}", [NQ, 16]).rearrange("(t p) c -> p t c", p=128)
        jd["cl"] = din(f"cl{j}", [128, 8])
        jd["y"] = nc.dram_tensor(f"y{j}", [NQ, D], F32, kind="ExternalOutput").ap()
        jd["kt_d"] = nc.dram_tensor(f"kt_d{j}", [4, 128, S], BF16).ap()
        jd["v_d"] = nc.dram_tensor(f"v_d{j}", [4, 128, S], BF16).ap()
        jd["qt_d"] = nc.dram_tensor(f"qt_d{j}", [4, 128, NQ], BF16).ap()
        jd["o_d"] = nc.dram_tensor(f"o_d{j}", [4, 128, NQ], BF16).ap()
        J.append(jd)

    class Arena:
        def __init__(self, lo, hi):
            self.lo, self.hi, self.off = lo, hi, lo

        def reset(self):
            self.off = self.lo

        def alloc(self, shape, dt):
            uid[0] += 1
            nb = int(np.prod(shape[1:])) * (4 if dt == F32 else 2)
            off = (self.off + 63) // 64 * 64
            assert off + nb <= self.hi, ("SBUF overflow", shape, off, nb, self.hi)
            self.off = off + nb
            h = nc.alloc_sbuf_tensor_at(f"t{uid[0]}", list(shape), dt, offset=off)
            return Buf(h[:])

    pers = Arena(SBUF_LO, SBUF_LO + 16 * 1024)
    ph = Arena(SBUF_LO + 16 * 1024, SBUF_HI)

    ps_h = nc.alloc_psum_tensor("ps", [128, 4096], F32)
    ps = ps_h[:]
    ps_bf = ps.bitcast(BF16)

    def bank(b, n=512, off=0):
        return ps[:, b * 512 + off: b * 512 + off + n]

    def bank_bf(b, n=1024, off=0):
        return ps_bf[:, b * 1024 + off: b * 1024 + off + n]

    banks = [Buf(bank(b)) for b in range(8)]

    def MM(out, lhsT, rhs, start, stop, tp=None):
        if tp is None:
            return lambda e: e.matmul(out, lhsT=lhsT, rhs=rhs, start=start, stop=stop)
        return lambda e: e.matmul(out, lhsT=lhsT, rhs=rhs, start=start, stop=stop, tile_position=tp)

    def TR(out, in_, ident):
        return lambda e: e.transpose(out, in_, ident)

    def EMB(fn):
        fn.embed_ok = True
        return fn

    def ACT(out, in_, func, scale=1.0, accum=None, bias=None):
        if bias is not None:
            return EMB(lambda e: e.activation(out=out, in_=in_, func=func, scale=scale, bias=bias))
        if accum is None:
            return EMB(lambda e: e.activation(out=out, in_=in_, func=func, scale=scale))
        return lambda e: e.activation(out=out, in_=in_, func=func, scale=scale, accum_out=accum)

    def TT(out, in0, in1, op):
        return EMB(lambda e: e.tensor_tensor(out=out, in0=in0, in1=in1, op=op))

    def TS(out, in0, s1, op0, s2=None, op1=None):
        if op1 is None:
            return EMB(lambda e: e.tensor_scalar(out=out, in0=in0, scalar1=s1, scalar2=None, op0=op0))
        return EMB(lambda e: e.tensor_scalar(out=out, in0=in0, scalar1=s1, scalar2=s2, op0=op0, op1=op1))

    def STT(out, in0, scalar, in1, op0, op1):
        return EMB(lambda e: e.scalar_tensor_tensor(out=out, in0=in0, scalar=scalar, in1=in1, op0=op0, op1=op1))

    def CP(out, in_):
        return EMB(lambda e: e.tensor_copy(out=out, in_=in_))

    def RED(out, in_):
        return lambda e: e.tensor_reduce(out=out, in_=in_, axis=AX.X, op=ALU.add)

    def RCP(out, in_):
        return lambda e: e.reciprocal(out=out, in_=in_)

    def MS(ap, val):
        return lambda e: e.memset(ap, val)

    def g3(ap, b):
        return ap.rearrange("p (a b) -> p a b", b=b)

    ident_f = pers.alloc([128, 128], F32)
    ident_b = pers.alloc([128, 128], BF16)
    ones_b = pers.alloc([128, 128], BF16)
    neghalf = pers.alloc([128, 512], F32)
    norm_g = pers.alloc([128, 8], F32)
    conv_l = pers.alloc([128, 12], F32)
    qg_b = pers.alloc([128, 64], F32)
    kg_b = pers.alloc([128, 64], F32)
    lamv = pers.alloc([128, 256], F32)
    subln = pers.alloc([128, 1], F32)
    sgs = pers.alloc([128, 1], F32)
    neg_lam = pers.alloc([128, 1], F32)
    lamt = pers.alloc([128, 128], F32)
    lams = pers.alloc([128, 4], F32)
    ones_f = pers.alloc([128, 128], F32)
    eps_t = pers.alloc([128, 1], F32)

    P.dma(ident_f.ap, ident_d, "cld", writes=[ident_f])
    P.dma(norm_g.ap, norm_g_d, "cld", writes=[norm_g])
    P.dma(conv_l.ap, conv_d, "cld", writes=[conv_l])
    P.dma(qg_b.ap, qg_d.partition_broadcast(128), "cld", writes=[qg_b])
    P.dma(kg_b.ap, kg_d.partition_broadcast(128), "cld", writes=[kg_b])
    P.dma(lamv.ap, lam_d.partition_broadcast(128), "cld", writes=[lamv])
    P.dma(subln.ap, subln_d, "cld", writes=[subln])
    for jd in J:
        jd["Gp"] = pers.alloc([128, 8], F32)
        jd["shiftT"] = pers.alloc([128, 8], F32)
        jd["gate_b"] = pers.alloc([128, D], F32)
        jd["hmask_s"] = pers.alloc([128, 2 * jd["ng3"]], F32)
        jd["rstd_q"] = pers.alloc([128, jd["NQ"] // 128], F32)
        P.dma(jd["hmask_s"].ap, jd["hmask"], "cld", writes=[jd["hmask_s"]])
    P.barrier()
    P.op("pool", MS(ones_b.ap, 1.0), writes=[ones_b])
    P.op("pool", MS(neghalf.ap, -0.5), writes=[neghalf])
    P.op("pool", MS(ones_f.ap, 1.0), writes=[ones_f])
    P.op("pool", MS(eps_t.ap, EPS), writes=[eps_t])
    P.op("dve", TS(conv_l.ap, conv_l.ap, 0.5, ALU.mult), reads=[conv_l], writes=[conv_l])
    P.op("dve", CP(ident_b.ap, ident_f.ap), reads=[ident_f], writes=[ident_b])
    P.op("dve", TT(lamt.ap[:, 0:64], lamv.ap[:, 0:64], lamv.ap[:, 64:128], ALU.mult), reads=[lamv], writes=[lamt])
    P.op("dve", TT(lamt.ap[:, 64:128], lamv.ap[:, 128:192], lamv.ap[:, 192:256], ALU.mult), reads=[lamv], addwrites=[lamt])
    P.op("dve", RED(lams.ap[:, 0:2], g3(lamt.ap, 64)), reads=[lamt], writes=[lams])
    P.op("act", ACT(lams.ap[:, 2:4], lams.ap[:, 0:2], AF.Exp), reads=[lams], addwrites=[lams])
    P.op("dve", STT(neg_lam.ap, lams.ap[:, 3:4], -LAM_INIT, lams.ap[:, 2:3], ALU.add, ALU.subtract), reads=[lams], writes=[neg_lam])
    P.op("dve", TS(sgs.ap, subln.ap, 0.5 * (1.0 - LAM_INIT), ALU.mult), reads=[subln], writes=[sgs])

    ph.reset()
    wada = ph.alloc([128, KC, 3072], F32)
    bada = ph.alloc([128, 3072], F32)
    for kc in range(KC):
        P.dma(wada.ap[:, kc, :], w_ada[:, kc, :], "wld", addwrites=[wada])
    P.dma(bada.ap, b_ada.partition_broadcast(128), "bld", writes=[bada])
    for j, jd in enumerate(J):
        cl = ph.alloc([128, 8], F32)
        ce = ph.alloc([128, 8], F32)
        sc = ph.alloc([128, 8], F32)
        screp = ph.alloc([128, KC, 128], F32)
        modb = ph.alloc([128, 3072], F32)
        scl = ph.alloc([128, 8], F32)
        P.dma(cl.ap, jd["cl"], f"cl{j}", writes=[cl])
        P.op("act", ACT(ce.ap, cl.ap, AF.Exp, scale=-1.0), reads=[cl], writes=[ce])
        P.op("dve", TS(ce.ap, ce.ap, 1.0, ALU.add), reads=[ce], writes=[ce])
        P.op("dve", RCP(ce.ap, ce.ap), reads=[ce], writes=[ce])
        P.op("dve", TT(sc.ap, cl.ap, ce.ap, ALU.mult), reads=[cl, ce], writes=[sc])
        P.op("dve", CP(screp.ap, sc.ap.unsqueeze(2).to_broadcast([128, KC, 128])), reads=[sc], writes=[screp])
        for cg in range(6):
            bk = banks[cg]
            fns = [MM(bank(cg), screp.ap[:, kc, :], wada.ap[:, kc, cg * 512:(cg + 1) * 512], kc == 0, kc == KC - 1) for kc in range(KC)]
            P.group("pe", fns, reads=[screp, wada], writes=[bk])
            P.op("dve", TT(modb.ap[:, cg * 512:(cg + 1) * 512], bank(cg), bada.ap[:, cg * 512:(cg + 1) * 512], ALU.add),
                 reads=[bk, bada], addwrites=[modb])
        P.op("dve", TS(jd["gate_b"].ap, modb.ap[:, 2048:3072], 0.5, ALU.mult), reads=[modb], writes=[jd["gate_b"]])
        for half in range(2):
            fns = [TR(ps[:, 6 * 512 + blk * 128: 6 * 512 + (blk + 1) * 128], modb.ap[:, half * 1024 + blk * 128: half * 1024 + (blk + 1) * 128], ident_f.ap)
                   for blk in range(8)]
            P.group("pe", fns, reads=[modb, ident_f], writes=[banks[6], banks[7]])
            src = g3(ps[:, 6 * 512: 8 * 512], 128)[:, :, 0]
            dst = jd["shiftT"] if half == 0 else scl
            P.op("dve", CP(dst.ap, src), reads=[banks[6], banks[7]], writes=[dst])
        P.op("dve", STT(jd["Gp"].ap, scl.ap, 1.0, norm_g.ap, ALU.add, ALU.mult), reads=[scl, norm_g], writes=[jd["Gp"]])
    P.barrier()

    def make_norm_bufs(G, nxn=2):
        nt = G // 128
        return dict(
            G=G, nt=nt,
            xg=Ring([ph.alloc([128, nt, D], F32) for _ in range(2)]),
            xn=Ring([ph.alloc([128, nt, D], BF16) for _ in range(nxn)]),
            hT=Ring([ph.alloc([128, KC, G], BF16) for _ in range(2)]),
            ss=Ring([ph.alloc([128, 8], F32) for _ in range(2)]),
            rs=Ring([ph.alloc([128, 8], F32) for _ in range(2)]),
            junk=ph.alloc([128, D], BF16),
            xsem=Ring(["xld0", "xld1"]),
            tb=Ring([0, 1]),
        )

    def load_group(nb, x_ap, g):
        G = nb["G"]
        xg = nb["xg"].get()
        P.dma(xg.ap, x_ap[g * G:(g + 1) * G, :].rearrange("(t p) d -> p t d", p=128), nb["xsem"].get(), writes=[xg])
        return xg

    def norm_group(nb, jd, xg, keep=None, pre=None):
        return norm_b(nb, jd, norm_a(nb, jd, xg, keep=keep, pre=pre))

    def norm_a(nb, jd, xg, keep=None, pre=None):
        G, nt = nb["G"], nb["nt"]
        xn, ss = nb["xn"].get(), nb["ss"].get()
        junk = nb["junk"]
        if pre is not None:
            rs, c0 = pre
        else:
            for i in range(nt):
                P.op("act", ACT(junk.ap, xg.ap[:, i, :], AF.Square, accum=ss.ap[:, i:i + 1]),
                     reads=[xg], addwrites=[ss] if i else (), writes=() if i else [ss])
            if keep is not None:
                rs, c0 = keep
                P.op("act", ACT(rs.ap[:, c0:c0 + nt], ss.ap[:, 0:nt], AF.Sqrt, scale=1.0 / D, bias=eps_t.ap[:, 0:1]), reads=[ss, eps_t], addwrites=[rs])
                P.op("dve", RCP(rs.ap[:, c0:c0 + nt], rs.ap[:, c0:c0 + nt]), reads=[rs], addwrites=[rs])
            else:
                rs, c0 = nb["rs"].get(), 0
                P.op("act", ACT(rs.ap[:, 0:nt], ss.ap[:, 0:nt], AF.Sqrt, scale=1.0 / D, bias=eps_t.ap[:, 0:1]), reads=[ss, eps_t], writes=[rs])
                P.op("dve", RCP(rs.ap[:, 0:nt], rs.ap[:, 0:nt]), reads=[rs], writes=[rs])
        for i in range(nt):
            P.op("dve", TS(xn.ap[:, i, :], xg.ap[:, i, :], rs.ap[:, c0 + i:c0 + i + 1], ALU.mult),
                 reads=[xg, rs], addwrites=[xn] if i else (), writes=() if i else [xn])
        return xn

    def norm_b(nb, jd, xn):
        G, nt = nb["G"], nb["nt"]
        hT = nb["hT"].get()
        for kp in range(KC // 2):
            b = nb["tb"].get()
            fns = []
            for k2 in range(2):
                kc = kp * 2 + k2
                for i in range(nt):
                    fns.append(TR(bank_bf(b, 128, k2 * G + i * 128), xn.ap[:, i, kc * 128:(kc + 1) * 128], ident_b.ap))
            P.group("pe", fns, reads=[xn, ident_b], writes=[banks[b]])
            for k2 in range(2):
                kc = kp * 2 + k2
                first = (kc == 0)
                if k2 == 0:
                    P.op("dve", TS(hT.ap[:, kc, :], bank_bf(b, G, k2 * G), jd["Gp"].ap[:, kc:kc + 1], ALU.mult, jd["shiftT"].ap[:, kc:kc + 1], ALU.add),
                         reads=[banks[b], jd["Gp"], jd["shiftT"]], writes=[hT] if first else (), addwrites=() if first else [hT])
                else:
                    P.op("act", ACT(hT.ap[:, kc, :], bank_bf(b, G, k2 * G), AF.Identity, scale=jd["Gp"].ap[:, kc:kc + 1], bias=jd["shiftT"].ap[:, kc:kc + 1]),
                         reads=[banks[b], jd["Gp"], jd["shiftT"]], addwrites=[hT])
        return hT

    ph.reset()
    wqkv = ph.alloc([128, KC, 1536], BF16)
    wst = Ring([ph.alloc([128, 1536], F32) for _ in range(2)])
    wsem = Ring(["wst0", "wst1"])
    for kc in range(KC):
        st_ = wst.get()
        P.dma(st_.ap, w_in[:, kc, 2048:3584], wsem.get(), writes=[st_])
        if kc % 2:
            P.op("act", ACT(wqkv.ap[:, kc, :], st_.ap, AF.Copy), reads=[st_], addwrites=[wqkv])
        else:
            P.op("dve", CP(wqkv.ap[:, kc, :], st_.ap), reads=[st_], addwrites=[wqkv])
    for j, jd in enumerate(J):
        jd["ropek_s"] = ph.alloc([128, jd["S"] // 128, 16], F32)
        jd["ropeq_s"] = ph.alloc([128, jd["NQ"] // 128, 16], F32)
        P.dma(jd["ropek_s"].ap, jd["ropek"], f"rk{j}", writes=[jd["ropek_s"]])
        P.dma(jd["ropeq_s"].ap, jd["ropeq"], f"rq{j}", writes=[jd["ropeq_s"]])
    nb1 = make_norm_bufs(G1)
    sqb = Ring([ph.alloc([128, 512], F32) for _ in range(4)])
    ss8 = Ring([ph.alloc([128, 8], F32) for _ in range(4)])
    r8 = Ring([ph.alloc([128, 8], F32) for _ in range(4)])
    knb = Ring([ph.alloc([128, 512], F32) for _ in range(4)])
    kn2 = Ring([ph.alloc([128, 512], F32) for _ in range(4)])
    kbb = Ring([ph.alloc([128, 512], BF16) for _ in range(8)])
    rt = Ring([ph.alloc([128, 4, 64], F32) for _ in range(4)])
    kts = Ring([ph.alloc([128, 4, G1], BF16) for _ in range(2)])
    vts = Ring([ph.alloc([128, 4, G1], BF16) for _ in range(2)])
    ksem = Ring(["kst0", "kst1"])
    vsem = Ring(["vst0", "vst1"])
    pjk = Ring([2, 4])
    pjv = Ring([3, 5])
    ptr = Ring([6, 7])

    def qk_gen(hT, i, col0, gvec, rope_s, T, stage, out):
        b = pjk.get()
        bk = banks[b]
        fns = [MM(bank(b), hT.ap[:, kc, i * 128:(i + 1) * 128], wqkv.ap[:, kc, col0:col0 + 512], kc == 0, kc == KC - 1) for kc in range(KC)]
        P.group("pe", fns, reads=[hT, wqkv], writes=[bk])
        yield
        sq, s8, rr, kn, k2, kb, r_ = sqb.get(), ss8.get(), r8.get(), knb.get(), kn2.get(), kbb.get(), rt.get()
        P.op("act", ACT(sq.ap, bank(b), AF.Square), reads=[bk], writes=[sq])
        yield
        P.op("dve", RED(s8.ap, g3(sq.ap, 64)), reads=[sq], writes=[s8])
        yield
        P.op("act", ACT(rr.ap, s8.ap, AF.Sqrt, scale=1.0 / 64, bias=eps_t.ap[:, 0:1]), reads=[s8, eps_t], writes=[rr])
        yield
        P.op("dve", RCP(rr.ap, rr.ap), reads=[rr], writes=[rr])
        kn3, k23, kb3 = g3(kn.ap, 64), g3(k2.ap, 64), g3(kb.ap, 64)
        P.op("dve", TT(kn3, g3(bank(b), 64), rr.ap.unsqueeze(2).to_broadcast([128, 8, 64]), ALU.mult), reads=[bk, rr], writes=[kn])
        yield
        P.op("dve", TT(k23, kn3, gvec.ap.unsqueeze(1).to_broadcast([128, 8, 64]), ALU.mult), reads=[kn, gvec], writes=[k2])
        yield
        P.op("act", ACT(kb3[:, :, 16:64], k23[:, :, 16:64], AF.Copy), reads=[k2], writes=[kb])
        cosb = rope_s.ap[:, T, 0:8].unsqueeze(1).to_broadcast([128, 8, 8])
        sinb = rope_s.ap[:, T, 8:16].unsqueeze(1).to_broadcast([128, 8, 8])
        x1, x2 = k23[:, :, 0:8], k23[:, :, 8:16]
        rq = [g3(r_.ap[:, q, :], 8) for q in range(4)]
        P.op("pool", TT(rq[0], x1, cosb, ALU.mult), reads=[k2, rope_s], writes=[r_])
        P.op("pool", TT(rq[1], x2, sinb, ALU.mult), reads=[k2], addwrites=[r_])
        yield
        P.op("pool", TT(rq[2], x2, cosb, ALU.mult), reads=[k2], addwrites=[r_])
        P.op("pool", TT(rq[3], x1, sinb, ALU.mult), reads=[k2], addwrites=[r_])
        yield
        P.op("pool", TT(kb3[:, :, 0:8], rq[0], rq[1], ALU.subtract), reads=[r_], addwrites=[kb])
        P.op("pool", TT(kb3[:, :, 8:16], rq[2], rq[3], ALU.add), reads=[r_], addwrites=[kb])

        def part_b():
            tb = ptr.get()
            fns = [TR(bank_bf(tb, 128, h * 128), kb.ap[:, h * 128:(h + 1) * 128], ident_b.ap) for h in range(4)]
            P.group("pe", fns, reads=[kb, ident_b], writes=[banks[tb]])
            P.op("act", ACT(stage.ap[:, :, i * 128:(i + 1) * 128], g3(bank_bf(tb, 512), 128), AF.Copy), reads=[banks[tb]], addwrites=[stage])
        out.append(part_b)

    def run_pair(gens, after_first=None):
        live = list(gens)
        first = True
        while live:
            nxt_live = []
            for gn in live:
                try:
                    next(gn)
                    nxt_live.append(gn)
                except StopIteration:
                    pass
            live = nxt_live
            if first and after_first is not None:
                after_first()
            first = False

    def v_post(hT, i, stage):
        b = pjv.get()
        bk = banks[b]
        fns = [MM(bank(b), hT.ap[:, kc, i * 128:(i + 1) * 128], wqkv.ap[:, kc, 1024:1536], kc == 0, kc == KC - 1) for kc in range(KC)]
        P.group("pe", fns, reads=[hT, wqkv], writes=[bk])
        P.op("dve", CP(stage.ap[:, :, i * 128:(i + 1) * 128], g3(bank(b), 128)), reads=[bk], addwrites=[stage])

    pending = []
    xn_nxt = None
    hT_next = None
    for jd in J:
        S, NQ = jd["S"], jd["NQ"]
        work = [("kv", g) for g in range(S // G1)] + [("q", g) for g in range(NQ // G1)]
        xsrc = {"kv": jd["xf"], "q": jd["xq"]}
        def keep_of(w):
            return (jd["rstd_q"], w[1] * (G1 // 128)) if w[0] == "q" else None

        xg_cur = load_group(nb1, xsrc[work[0][0]], work[0][1])
        hT_next = norm_b(nb1, jd, norm_a(nb1, jd, xg_cur, keep=keep_of(work[0])))
        for wi, (kind, g) in enumerate(work):
            hT = hT_next
            more = wi + 1 < len(work)
            if more:
                xg_nxt = load_group(nb1, xsrc[work[wi + 1][0]], work[wi + 1][1])

            def hook(i, wi=wi, more=more):
                nonlocal hT_next, xn_nxt
                if not more:
                    return
                if i == 0:
                    xn_nxt = norm_a(nb1, jd, xg_nxt, keep=keep_of(work[wi + 1]))
                if i == 2:
                    hT_next = norm_b(nb1, jd, xn_nxt)
            ks = kts.get()
            P.op("pool", MS(ks.ap[:, 0, 0:2], 0.0), writes=[ks])
            if kind == "kv":
                vs = vts.get()
                P.op("pool", MS(vs.ap[:, 0, 0:2], 0.0), writes=[vs])
                for i0 in range(0, G1 // 128, 2):
                    outs = []
                    gens = [qk_gen(hT, i, 512, kg_b, jd["ropek_s"], g * (G1 // 128) + i, ks, outs) for i in (i0, i0 + 1)]

                    def vproj(i0=i0, hT=hT, vs=vs):
                        v_post(hT, i0, vs)
                        v_post(hT, i0 + 1, vs)
                        if i0 == 0:
                            hook(0)
                    run_pair(gens, after_first=vproj)
                    while pending:
                        pending.pop(0)()
                    pending.extend(outs)
                    if i0 == 0:
                        hook(2)

                def stores(ks=ks, vs=vs, g=g, jd=jd):
                    P.dma(jd["kt_d"][:, :, g * G1:(g + 1) * G1].rearrange("h p s -> p h s"), ks.ap, ksem.get(), reads=[ks])
                    P.dma(jd["v_d"][:, :, g * G1:(g + 1) * G1].rearrange("h p s -> p h s"), vs.ap, vsem.get(), reads=[vs])
                pending.append(stores)
            else:
                for i0 in range(0, G1 // 128, 2):
                    outs = []
                    gens = [qk_gen(hT, i, 0, qg_b, jd["ropeq_s"], g * (G1 // 128) + i, ks, outs) for i in (i0, i0 + 1)]
                    run_pair(gens, after_first=(lambda: hook(0)) if i0 == 0 else None)
                    while pending:
                        pending.pop(0)()
                    pending.extend(outs)
                    if i0 == 0:
                        hook(2)

                def stores(ks=ks, g=g, jd=jd):
                    P.dma(jd["qt_d"][:, :, g * G1:(g + 1) * G1].rearrange("h p s -> p h s"), ks.ap, ksem.get(), reads=[ks])
                pending.append(stores)
    while pending:
        pending.pop(0)()
    P.barrier()

    ph.reset()
    SMAX = max(jd["S"] for jd in J)
    NQMAX = max(jd["NQ"] for jd in J)
    ktb = Ring([ph.alloc([128, SMAX], BF16) for _ in range(2)])
    vtb = Ring([ph.alloc([128, SMAX], BF16) for _ in range(2)])
    qtb = Ring([ph.alloc([128, NQMAX], BF16) for _ in range(2)])
    kvsem = Ring(["kvl0", "kvl1"])
    Pb = Ring([ph.alloc([128, 1024], BF16) for _ in range(3)])
    Sb = Ring([0, 2])
    rinv = ph.alloc([128, 1024], F32)
    DW = 512
    raccs = Ring([(ph.alloc([128, DW], F32), None) for _ in range(2)])
    osb0 = ph.alloc([128, 512], F32)
    osb1 = ph.alloc([128, 512], F32)
    rsb = ph.alloc([128, 1024], F32)
    deferred = []
    t0b = ph.alloc([128, 512], F32)
    t1b = ph.alloc([128, 512], F32)
    ost = Ring([ph.alloc([128, QT], BF16) for _ in range(2)])
    osem = Ring(["ost0", "ost1"])

    def load_head(jd, h):
        S, NQ = jd["S"], jd["NQ"]
        kt, vt, qt_, sem = ktb.get(), vtb.get(), qtb.get(), kvsem.get()
        nsp = max(1, S // 4096)
        w = S // nsp
        for sp in range(nsp):
            P.dma(kt.ap[:, sp * w:(sp + 1) * w], jd["kt_d"][h, :, sp * w:(sp + 1) * w], sem, writes=[kt] if sp == 0 else (), addwrites=[kt] if sp else ())
        for sp in range(nsp):
            P.dma(vt.ap[:, sp * w:(sp + 1) * w], jd["v_d"][h, :, sp * w:(sp + 1) * w], sem, writes=[vt] if sp == 0 else (), addwrites=[vt] if sp else ())
        tq = P.dma(qt_.ap[:, 0:NQ], jd["qt_d"][h, :, :], sem, writes=[qt_])
        kt.wr = {tq[0]: tq[1]}
        vt.wr = {tq[0]: tq[1]}
        return kt, vt, qt_

    heads = [(jd, h) for jd in J for h in range(4)]
    nxt = load_head(*heads[0])
    for hi, (jd, h) in enumerate(heads):
        kt, vt, qt_ = nxt
        if hi + 1 < len(heads):
            nxt = load_head(*heads[hi + 1])
        S, NQ = jd["S"], jd["NQ"]
        nkt = S // 128
        for qi in range(NQ // QT):
            racc, raccp = raccs.get()

            def qk(u):
                sb = Sb.get()
                fns = [MM(bank(sb), kt.ap[0:64, u * 128:(u + 1) * 128], qt_.ap[0:64, qi * QT:(qi + 1) * QT], True, True, (0, 0)),
                       MM(bank(sb + 1), kt.ap[64:128, u * 128:(u + 1) * 128], qt_.ap[64:128, qi * QT:(qi + 1) * QT], True, True, (64, 0))]
                if u >= 2:
                    fns[0].embed_ok = True
                P.group("pe", fns, reads=[kt, qt_], writes=[banks[sb], banks[sb + 1]])
                pb = Pb.get()
                P.op("act", ACT(pb.ap, ps[:, sb * 512: sb * 512 + 1024], AF.Exp, scale=0.125), reads=[banks[sb], banks[sb + 1]], writes=[pb])
                return pb

            def pv(u, pb):
                vcol = u * 128
                first, last = (u == 0), (u == nkt - 1)
                fns = [MM(bank(4), vt.ap[:, vcol:vcol + 128], pb.ap[:, 0:512], first, last),
                       MM(bank(5), vt.ap[:, vcol:vcol + 128], pb.ap[:, 512:1024], first, last),
                       MM(bank(7, 1024 - DW, DW - 512), ones_b.ap, pb.ap[:, DW:1024], first, last)]
                acc = [banks[4], banks[5], banks[7]]
                if first:
                    P.group("pe", fns, reads=[vt, pb, ones_b], writes=acc)
                    P.op("dve", CP(racc.ap, pb.ap[:, 0:DW]), reads=[pb], writes=[racc])
                else:
                    fns[0].embed_ok = True
                    P.group("pe", fns, reads=[vt, pb, ones_b], addwrites=acc)
                    P.op("dve", TT(racc.ap, racc.ap, pb.ap[:, 0:DW], ALU.add), reads=[pb, racc], writes=[racc])

            pbs = {0: qk(0)}
            if nkt > 1:
                pbs[1] = qk(1)
            for u in range(nkt):
                if u + 2 < nkt:
                    pbs[u + 2] = qk(u + 2)
                pv(u, pbs.pop(u))
                if deferred:
                    deferred.pop(0)()
            while deferred:
                deferred.pop(0)()
            P.op("dve", CP(osb0.ap, bank(4)), reads=[banks[4]], writes=[osb0])
            P.op("dve", CP(osb1.ap, bank(5)), reads=[banks[5]], writes=[osb1])
            P.op("dve", CP(rsb.ap[:, DW:1024], bank(7, 1024 - DW, DW - 512)), reads=[banks[7]], writes=[rsb])

            def tot(racc=racc, raccp=raccp):
                P.group("pe", [MM(bank(6), ones_f.ap, racc.ap[:, 0:512], True, True)], reads=[racc, ones_f], writes=[banks[6]])
                P.op("dve", CP(rsb.ap[:, 0:512], bank(6)), reads=[banks[6]], addwrites=[rsb])
                if DW > 512:
                    P.group("pe", [MM(bank(6, DW - 512), ones_f.ap, racc.ap[:, 512:DW], True, True)], reads=[racc, ones_f], writes=[banks[6]])
                    P.op("dve", CP(rsb.ap[:, 512:DW], bank(6, DW - 512)), reads=[banks[6]], addwrites=[rsb])
            nch = 16 if nkt >= 32 else 8
            cw = 1024 // nch

            def mk_rcp(c):
                def f():
                    P.op("dve", RCP(rinv.ap[:, c * cw:(c + 1) * cw], rsb.ap[:, c * cw:(c + 1) * cw]), reads=[rsb],
                         writes=[rinv] if c == 0 else (), addwrites=[rinv] if c else ())
                return f

            def fin(jd=jd, h=h, qi=qi):
                os_ = ost.get()
                P.op("dve", TT(t0b.ap, osb0.ap, rinv.ap[:, 0:512], ALU.mult), reads=[osb0, rinv], writes=[t0b])
                P.op("dve", TT(t1b.ap, osb1.ap, rinv.ap[:, 512:1024], ALU.mult), reads=[osb1, rinv], writes=[t1b])
                P.op("dve", STT(os_.ap, t1b.ap, neg_lam.ap[:, 0:1], t0b.ap, ALU.mult, ALU.add), reads=[t0b, t1b, neg_lam], writes=[os_])
                P.dma(jd["o_d"][h, :, qi * QT:(qi + 1) * QT], os_.ap, osem.get(), reads=[os_])
            deferred.extend([tot] + [mk_rcp(c) for c in range(nch)] + [fin])
    while deferred:
        deferred.pop(0)()
    P.barrier()

    ph.reset()
    w3 = ph.alloc([128, KC, 4608], BF16)
    wco = ph.alloc([128, 4, D], BF16)
    wao = ph.alloc([128, 4, D], BF16)
    wo = ph.alloc([128, KC, D], BF16)
    wst3 = Ring([ph.alloc([128, 1024], F32) for _ in range(2)])
    wsem3 = Ring(["wst0", "wst1"])
    cast_eng = Ring(["act", "dve"])

    def load_cast(dst_ap, src_ap, n, dstbuf):
        st_ = wst3.get()
        P.dma(st_.ap[:, 0:n], src_ap, wsem3.get(), writes=[st_])
        ce_ = cast_eng.get()
        if ce_ == "act":
            P.op("act", ACT(dst_ap, st_.ap[:, 0:n], AF.Copy), reads=[st_], addwrites=[dstbuf])
        else:
            P.op("dve", CP(dst_ap, st_.ap[:, 0:n]), reads=[st_], addwrites=[dstbuf])

    for kc in range(KC):
        for c0, s0, n in ((0, 0, 1024), (1024, 1024, 1024), (2048, 3584, 1024), (3072, 4608, 1024), (4096, 5632, 512)):
            load_cast(w3.ap[:, kc, c0:c0 + n], w_in[:, kc, s0:s0 + n], n, w3)
        load_cast(wo.ap[:, kc, :], wo_d[:, kc, :], 1024, wo)
    for m in range(4):
        load_cast(wco.ap[:, m, :], wco_d[:, m, :], 1024, wco)
        load_cast(wao.ap[:, m, :], wao_d[:, m, :], 1024, wao)
    nb3 = make_norm_bufs(G3, 1)
    NT3 = G3 // 128
    oin = Ring([ph.alloc([128, 4, G3], BF16) for _ in range(2)])
    oisem = Ring(["oin0", "oin1"])
    yT = ph.alloc([128, 4, G3], BF16)
    ogT = ph.alloc([128, 4, G3], BF16)
    mT = ph.alloc([128, KC, G3], BF16)
    yo = Ring([ph.alloc([128, D], F32) for _ in range(2)])
    yosem = Ring(["yo0", "yo1"])
    tmpf = Ring([ph.alloc([128, G3], F32) for _ in range(10)])
    tmpg = Ring([ph.alloc([128, 512], F32) for _ in range(2)])
    osq = ph.alloc([128, 4, G3], BF16)
    rsa = ph.alloc([128, 4 * G3], F32)
    uext = Ring([ph.alloc([128, G3 + 2], F32) for _ in range(2)])
    pr = Ring([2, 3, 4, 5])
    pr2 = Ring([6, 7])
    xh = yo.items[0]
    xhn = ph.alloc([32, D], BF16)
    hTh = ph.alloc([128, KC, 32], BF16)
    ssh = ph.alloc([32, 2], F32)
    uh = ph.alloc([128, 4, 32], F32)
    cxh = ph.alloc([128, 32], F32)

    def proj(col, hT_buf, hT_ap, n):
        b = pr.get()
        fns = [MM(bank(b, n), w3.ap[:, kc, col:col + 128], hT_ap[:, kc, :], kc == 0, kc == KC - 1) for kc in range(KC)]
        P.group("pe", fns, reads=[hT_buf, w3], writes=[banks[b]])
        return b

    def tanh_half(b, n, dst):
        P.op("act", ACT(dst.ap[:, 0:n], bank(b, n), AF.Tanh, scale=0.5), reads=[banks[b]], writes=[dst])

    def load3(jd, g):
        xg = load_group(nb3, jd["xq"], g)
        oi = oin.get()
        P.dma(oi.ap, jd["o_d"][:, :, g * G3:(g + 1) * G3].rearrange("h p s -> p h s"), oisem.get(), writes=[oi])
        return xg, oi

    for j, jd in enumerate(J):
        NQ, ng3 = jd["NQ"], jd["ng3"]
        nh = 2 * ng3
        P.dma(xh.ap[0:nh, :], jd["halo"], f"hl{j}", writes=[xh])
        P.op("act", ACT(nb3["junk"].ap[0:nh, :], xh.ap[0:nh, :], AF.Square, accum=ssh.ap[0:nh, 0:1]), reads=[xh], writes=[ssh])
        P.op("act", ACT(ssh.ap[0:nh, 1:2], ssh.ap[0:nh, 0:1], AF.Sqrt, scale=1.0 / D, bias=eps_t.ap[0:nh, 0:1]), reads=[ssh, eps_t], writes=[ssh])
        P.op("dve", RCP(ssh.ap[0:nh, 1:2], ssh.ap[0:nh, 1:2]), reads=[ssh], writes=[ssh])
        P.op("dve", TS(xhn.ap[0:nh, :], xh.ap[0:nh, :], ssh.ap[0:nh, 1:2], ALU.mult), reads=[xh, ssh], writes=[xhn])
        fns = [TR(bank_bf(0, nh, kc * 32), xhn.ap[0:nh, kc * 128:(kc + 1) * 128], ident_b.ap[0:nh, 0:nh]) for kc in range(KC)]
        P.group("pe", fns, reads=[xhn, ident_b], writes=[banks[0]])
        for kc in range(KC):
            P.op("dve", TS(hTh.ap[:, kc, 0:nh], bank_bf(0, nh, kc * 32), jd["Gp"].ap[:, kc:kc + 1], ALU.mult, jd["shiftT"].ap[:, kc:kc + 1], ALU.add),
                 reads=[banks[0], jd["Gp"], jd["shiftT"]], writes=[hTh] if kc == 0 else (), addwrites=[hTh] if kc else ())
        for m in range(4):
            bc = proj(512 + m * 128, hTh, hTh.ap[:, :, 0:nh], nh)
            bx = proj(1024 + m * 128, hTh, hTh.ap[:, :, 0:nh], nh)
            P.op("act", ACT(cxh.ap[:, 0:nh], bank(bx, nh), AF.Copy), reads=[banks[bx]], writes=[cxh])
            P.op("dve", TT(uh.ap[:, m, 0:nh], bank(bc, nh), cxh.ap[:, 0:nh], ALU.mult), reads=[banks[bc], cxh], writes=[uh] if m == 0 else (),
                 addwrites=[uh] if m else ())
            P.op("dve", TT(uh.ap[:, m, 0:nh], uh.ap[:, m, 0:nh], jd["hmask_s"].ap[:, 0:nh], ALU.mult), reads=[uh, jd["hmask_s"]], addwrites=[uh])
        cur = load3(jd, 0)
        hT_n3 = norm_group(nb3, jd, cur[0], pre=(jd["rstd_q"], 0))
        for g in range(ng3):
            xg, oi = cur
            hT = hT_n3
            more3 = g + 1 < ng3
            if more3:
                nxt = load3(jd, g + 1)
            P.op("dve", TT(osq.ap, oi.ap, oi.ap, ALU.mult), reads=[oi], writes=[osq])
            P.group("pe", [MM(bank(6 + hh // 2, G3, (hh % 2) * G3), ones_b.ap, osq.ap[:, hh, :], True, True) for hh in range(4)],
                    reads=[osq, ones_b], writes=[banks[6], banks[7]])
            P.op("act", ACT(rsa.ap, ps[:, 6 * 512: 8 * 512], AF.Sqrt, scale=1.0 / 128, bias=eps_t.ap[:, 0:1]), reads=[banks[6], banks[7], eps_t], writes=[rsa])
            P.op("dve", RCP(rsa.ap, rsa.ap), reads=[rsa], writes=[rsa])
            for m in range(4):
                bc = proj(512 + m * 128, hT, hT.ap, G3)
                bx = proj(1024 + m * 128, hT, hT.ap, G3)
                cxs, ue, cv = tmpf.get(), uext.get(), tmpf.get()
                P.op("act", ACT(cxs.ap, bank(bx, G3), AF.Copy), reads=[banks[bx]], writes=[cxs])
                P.op("dve", TT(ue.ap[:, 1:G3 + 1], bank(bc, G3), cxs.ap, ALU.mult), reads=[banks[bc], cxs], writes=[ue])
                P.op("dve", CP(ue.ap[:, 0:1], uh.ap[:, m, 2 * g:2 * g + 1]), reads=[uh], addwrites=[ue])
                P.op("dve", CP(ue.ap[:, G3 + 1:G3 + 2], uh.ap[:, m, 2 * g + 1:2 * g + 2]), reads=[uh], addwrites=[ue])
                P.op("dve", TS(cv.ap, ue.ap[:, 0:G3], conv_l.ap[:, m * 3:m * 3 + 1], ALU.mult), reads=[ue, conv_l], writes=[cv])
                P.op("dve", STT(cv.ap, ue.ap[:, 1:G3 + 1], conv_l.ap[:, m * 3 + 1:m * 3 + 2], cv.ap, ALU.mult, ALU.add), reads=[ue, cv], writes=[cv])
                P.op("dve", STT(cv.ap, ue.ap[:, 2:G3 + 2], conv_l.ap[:, m * 3 + 2:m * 3 + 3], cv.ap, ALU.mult, ALU.add), reads=[ue, cv], writes=[cv])
                bb_ = proj(0 + m * 128, hT, hT.ap, G3)
                bz = proj(1536 + m * 128, hT, hT.ap, G3)
                th, y1 = tmpf.get(), tmpf.get()
                tanh_half(bz, G3, th)
                P.op("dve", STT(th.ap, th.ap, 1.0, bank(bz, G3), ALU.add, ALU.mult), reads=[banks[bz], th], writes=[th])
                P.op("dve", TT(y1.ap, bank(bb_, G3), cv.ap, ALU.mult), reads=[banks[bb_], cv], writes=[y1])
                P.op("dve", TT(yT.ap[:, m, :], y1.ap, th.ap, ALU.mult), reads=[y1, th], writes=[yT] if m == 0 else (), addwrites=[yT] if m else ())
            for h in range(4):
                ba = proj(2048 + h * 128, hT, hT.ap, G3)
                th, on = tmpf.get(), tmpf.get()
                tanh_half(ba, G3, th)
                P.op("dve", STT(th.ap, th.ap, 1.0, bank(ba, G3), ALU.add, ALU.mult), reads=[banks[ba], th], writes=[th])
                P.op("dve", STT(on.ap, oi.ap[:, h, :], sgs.ap[:, 0:1], rsa.ap[:, h * G3:(h + 1) * G3], ALU.mult, ALU.mult), reads=[oi, sgs, rsa], writes=[on])
                P.op("dve", TT(ogT.ap[:, h, :], on.ap, th.ap, ALU.mult), reads=[on, th], writes=[ogT] if h == 0 else (), addwrites=[ogT] if h else ())
            if more3:
                xn_n3 = norm_a(nb3, jd, nxt[0], pre=(jd["rstd_q"], (g + 1) * NT3))
            for f in range(KC):
                bga = proj(2560 + f * 128, hT, hT.ap, G3)
                bgb = proj(3584 + f * 128, hT, hT.ap, G3)
                bA = pr2.get()
                P.group("pe", [MM(bank(bA, G3), wco.ap[:, m, f * 128:(f + 1) * 128], yT.ap[:, m, :], m == 0, m == 3) for m in range(4)],
                        reads=[yT, wco], writes=[banks[bA]])
                bB = pr2.get()
                P.group("pe", [MM(bank(bB, G3), wao.ap[:, h, f * 128:(f + 1) * 128], ogT.ap[:, h, :], h == 0, h == 3) for h in range(4)],
                        reads=[ogT, wao], writes=[banks[bB]])
                sa, sb_, ma, mb = tmpf.get(), tmpf.get(), tmpf.get(), tmpf.get()
                tanh_half(bga, G3, sa)
                tanh_half(bgb, G3, sb_)
                P.op("dve", STT(ma.ap, sa.ap, 1.0, bank(bA, G3), ALU.add, ALU.mult), reads=[banks[bA], sa], writes=[ma])
                P.op("dve", STT(mb.ap, sb_.ap, 1.0, bank(bB, G3), ALU.add, ALU.mult), reads=[banks[bB], sb_], writes=[mb])
                P.op("dve", TT(mT.ap[:, f, :], ma.ap, mb.ap, ALU.add), reads=[ma, mb], writes=[mT] if f == 0 else (), addwrites=[mT] if f else ())
            if more3:
                hT_n3 = norm_b(nb3, jd, xn_n3)
                cur = nxt
            for i in range(NT3):
                yo_ = yo.get()
                for hf in range(2):
                    bo = pr2.get()
                    P.group("pe", [MM(bank(bo), mT.ap[:, f, i * 128:(i + 1) * 128], wo.ap[:, f, hf * 512:(hf + 1) * 512], f == 0, f == KC - 1) for f in range(KC)],
                            reads=[mT, wo], writes=[banks[bo]])
                    tg = tmpg.get()
                    P.op("dve", TT(tg.ap, bank(bo), jd["gate_b"].ap[:, hf * 512:(hf + 1) * 512], ALU.mult), reads=[banks[bo], jd["gate_b"]], writes=[tg])
                    P.op("dve", TT(yo_.ap[:, hf * 512:(hf + 1) * 512], tg.ap, xg.ap[:, i, hf * 512:(hf + 1) * 512], ALU.add),
                         reads=[tg, xg], writes=[yo_] if hf == 0 else (), addwrites=[yo_] if hf else ())
                P.dma(jd["y"][g * G3 + i * 128: g * G3 + (i + 1) * 128, :], yo_.ap, yosem.get(), reads=[yo_])
    P.barrier()
    P.replay(nc)
    return nc


ROT_HALF = 8
ROPE_THETA = 500000.0


def _rope_table(pos):
    inv = (np.float32(ROPE_THETA) ** (-np.arange(ROT_HALF, dtype=np.float32) / np.float32(ROT_HALF))).astype(np.float32)
    ang = pos.astype(np.float32)[:, None] * inv[None, :]
    return np.ascontiguousarray(np.concatenate([np.cos(ang), np.sin(ang)], axis=1).astype(np.float32))


def _job_inputs(j, x_seq, q0, NQ, cvec):
    S = x_seq.shape[0]
    ng3 = NQ // G3
    halo = np.zeros((2 * ng3, D), np.float32)
    hmask = np.zeros((2 * ng3,), np.float32)
    for g in range(ng3):
        for side, idx in ((0, q0 + g * G3 - 1), (1, q0 + (g + 1) * G3)):
            if 0 <= idx < S:
                halo[2 * g + side] = x_seq[idx]
                hmask[2 * g + side] = 1.0
    rope = _rope_table(np.arange(S))
    return {
        f"xf{j}": np.ascontiguousarray(x_seq),
        f"xq{j}": np.ascontiguousarray(x_seq[q0:q0 + NQ]),
        f"halo{j}": halo,
        f"hmask{j}": np.ascontiguousarray(np.broadcast_to(hmask[None, :], (128, 2 * ng3))),
        f"ropek{j}": rope,
        f"ropeq{j}": np.ascontiguousarray(rope[q0:q0 + NQ]),
        f"cl{j}": np.ascontiguousarray(cvec.reshape(8, 128).T),
    }


def _shared_inputs(norm_g, w_ada, b_ada, w_in, conv_w, q_norm_g, k_norm_g, lam_q1, lam_k1, lam_q2, lam_k2, subln_g,
                   w_conv_out, w_attn_out, w_out):
    f = lambda a: np.ascontiguousarray(np.asarray(a, dtype=np.float32))
    return {
        "w_in": f(w_in[0]), "w_ada": f(w_ada[0]), "b_ada": f(b_ada[0]),
        "norm_g_l": f(np.asarray(norm_g[0]).reshape(8, 128).T),
        "conv_l": f(np.asarray(conv_w[0]).reshape(3, 4, 128).transpose(2, 1, 0).reshape(128, 12)),
        "qg": f(q_norm_g[0]), "kg": f(k_norm_g[0]),
        "lamv": f(np.concatenate([np.asarray(lam_q1[0]), np.asarray(lam_k1[0]), np.asarray(lam_q2[0]), np.asarray(lam_k2[0])])),
        "subln": f(np.asarray(subln_g[0]).reshape(128, 1)),
        "w_conv_out": f(w_conv_out[0]), "w_attn_out": f(w_attn_out[0]), "w_out": f(w_out[0]),
        "ident": np.eye(128, dtype=np.float32),
    }


def kernel(x_prompt, x_sample, c_prompt, c_sample, norm_g, w_ada, b_ada, w_in, conv_w, q_norm_g, k_norm_g,
           lam_q1, lam_k1, lam_q2, lam_k2, subln_g, w_conv_out, w_attn_out, w_out):
    x_prompt = np.asarray(x_prompt, dtype=np.float32)
    x_sample = np.asarray(x_sample, dtype=np.float32)
    c_prompt = np.asarray(c_prompt, dtype=np.float32)
    c_sample = np.asarray(c_sample, dtype=np.float32)
    n = 8
    B, S0, _ = x_prompt.shape
    B1, S1, _ = x_sample.shape
    NQ1 = S1 * B1 // n
    per_seq = S1 // NQ1
    nc = build([dict(S=S0, NQ=S0), dict(S=S1, NQ=NQ1)])
    shared = _shared_inputs(norm_g, w_ada, b_ada, w_in, conv_w, q_norm_g, k_norm_g, lam_q1, lam_k1, lam_q2, lam_k2, subln_g,
                            w_conv_out, w_attn_out, w_out)
    in_maps = []
    for c in range(n):
        m = dict(shared)
        m.update(_job_inputs(0, x_prompt[c], 0, S0, c_prompt[c]))
        b, jq = c // per_seq, c % per_seq
        m.update(_job_inputs(1, x_sample[b], jq * NQ1, NQ1, c_sample[b]))
        in_maps.append(m)
    res = run_bass_kernel_spmd(nc, in_maps, core_ids=list(range(n)))
    y_prompt = np.stack([np.asarray(res.results[c]["y0"], dtype=np.float32) for c in range(n)], axis=0)
    y_sample = np.zeros((B1, S1, D), np.float32)
    for c in range(n):
        b, jq = c // per_seq, c % per_seq
        y_sample[b, jq * NQ1:(jq + 1) * NQ1] = np.asarray(res.results[c]["y1"], dtype=np.float32)
    return (y_prompt, y_sample)
```

```python
import math
import contextlib
import numpy as np
import concourse.bass as bass
import concourse.mybir as mybir
from concourse.bass_utils import run_bass_kernel_spmd

F32 = mybir.dt.float32
BF16 = mybir.dt.bfloat16
AF = mybir.ActivationFunctionType
ALU = mybir.AluOpType
AX = mybir.AxisListType

D = 1024
KC = 8
EPS = 1e-6
LAM_INIT = 0.8 - 0.6 * math.exp(-0.3 * 0)
SBUF_LO = 16512
SBUF_HI = 229344
G1 = 512
G3 = 256
QT = 512


class Buf:
    def __init__(self, ap):
        self.ap = ap
        self.wr = {}
        self.rd = {}


class Prog:
    def __init__(self):
        self.q = {e: [] for e in ("sync", "pe", "act", "dve", "pool")}
        self.cnt = {}
        self.waited = {e: {} for e in self.q}

    def wait(self, eng, tok):
        if tok is None:
            return
        s, v = tok
        if self.waited[eng].get(s, 0) >= v:
            return
        self.waited[eng][s] = v
        self.q[eng].append(("w", s, v))

    def _hz(self, reads, writes, extra):
        waits = list(extra)
        for b in reads:
            waits += list(b.wr.items())
        for b in writes:
            waits += list(b.rd.items())
            if not b.rd:
                waits += list(b.wr.items())
        return waits

    def _reg(self, tok, reads, writes, addwrites):
        s, v = tok
        for b in reads:
            b.rd[s] = max(b.rd.get(s, 0), v)
        for b in writes:
            b.wr = {s: v}
            b.rd = {}
        for b in addwrites:
            b.wr[s] = max(b.wr.get(s, 0), v)

    def op(self, eng, fn, reads=(), writes=(), addwrites=(), extra=()):
        for t in self._hz(reads, tuple(writes) + tuple(addwrites), extra):
            self.wait(eng, t)
        s = "s_" + eng
        self.cnt[s] = self.cnt.get(s, 0) + 1
        tok = (s, self.cnt[s])
        self.q[eng].append(("o", fn, tok))
        self._reg(tok, reads, writes, addwrites)
        return tok

    def group(self, eng, fns, reads=(), writes=(), addwrites=(), extra=()):
        for t in self._hz(reads, tuple(writes) + tuple(addwrites), extra):
            self.wait(eng, t)
        for fn in fns[:-1]:
            self.q[eng].append(("o", fn, None))
        s = "s_" + eng
        self.cnt[s] = self.cnt.get(s, 0) + 1
        tok = (s, self.cnt[s])
        self.q[eng].append(("o", fns[-1], tok))
        self._reg(tok, reads, writes, addwrites)
        return tok

    def dma(self, out, in_, sem, reads=(), writes=(), addwrites=(), extra=(), eng="sync"):
        for t in self._hz(reads, tuple(writes) + tuple(addwrites), extra):
            self.wait(eng, t)
        self.cnt[sem] = self.cnt.get(sem, 0) + 16
        tok = (sem, self.cnt[sem])
        self.q[eng].append(("d", out, in_, tok))
        self._reg(tok, reads, writes, addwrites)
        return tok

    def barrier(self):
        toks = [(s, v) for s, v in self.cnt.items()]
        for e in self.q:
            for t in toks:
                self.wait(e, t)

    def replay(self, nc):
        for s, v in self.cnt.items():
            assert v < 60000, (s, v)
        with contextlib.ExitStack() as st:
            sems = {name: st.enter_context(nc.semaphore(name)) for name in sorted(self.cnt)}
            block = st.enter_context(nc.Block())

            def run(items):
                def f(eng):
                    n = len(items)
                    i = 0
                    while i < n:
                        it = items[i]
                        if it[0] == "w":
                            nx = items[i + 1] if i + 1 < n else None
                            if nx is not None and nx[0] == "o" and getattr(nx[1], "embed_ok", False):
                                ins = nx[1](eng)
                                ins._wait_ge(sems[it[1]], it[2])
                                if nx[2] is not None:
                                    ins.then_inc(sems[nx[2][0]], 1)
                                i += 2
                                continue
                            eng.wait_ge(sems[it[1]], it[2])
                        elif it[0] == "o":
                            ins = it[1](eng)
                            if it[2] is not None:
                                ins.then_inc(sems[it[2][0]], 1)
                        else:
                            eng.dma_start(out=it[1], in_=it[2]).then_inc(sems[it[3][0]], 16)
                        i += 1
                return f

            block.sync(run(self.q["sync"]))
            block.tensor(run(self.q["pe"]))
            block.scalar(run(self.q["act"]))
            block.vector(run(self.q["dve"]))
            block.gpsimd(run(self.q["pool"]))


class Ring:
    def __init__(self, items):
        self.items = items
        self.i = 0

    def get(self):
        b = self.items[self.i % len(self.items)]
        self.i += 1
        return b


def build(jobs):
    nc = bass.Bass("TRN2", target_bir_lowering=False)
    P = Prog()
    uid = [0]

    def din(name, shape, dt=F32):
        return nc.dram_tensor(name, list(shape), dt, kind="ExternalInput").ap()

    w_in = din("w_in", [D, 6144]).rearrange("(k p) c -> p k c", p=128)
    w_ada = din("w_ada", [D, 3072]).rearrange("(k p) c -> p k c", p=128)
    b_ada = din("b_ada", [3072])
    norm_g_d = din("norm_g_l", [128, 8])
    conv_d = din("conv_l", [128, 12])
    qg_d = din("qg", [64])
    kg_d = din("kg", [64])
    lam_d = din("lamv", [256])
    subln_d = din("subln", [128, 1])
    wco_d = din("w_conv_out", [512, D]).rearrange("(k p) c -> p k c", p=128)
    wao_d = din("w_attn_out", [512, D]).rearrange("(k p) c -> p k c", p=128)
    wo_d = din("w_out", [D, D]).rearrange("(k p) c -> p k c", p=128)
    ident_d = din("ident", [128, 128])
    J = []
    for j, cfg in enumerate(jobs):
        S, NQ = cfg["S"], cfg["NQ"]
        ng3 = NQ // G3
        jd = dict(S=S, NQ=NQ, ng3=ng3)
        jd["xf"] = din(f"xf{j}", [S, D])
        jd["xq"] = din(f"xq{j}", [NQ, D])
        jd["halo"] = din(f"halo{j}", [2 * ng3, D])
        jd["hmask"] = din(f"hmask{j}", [128, 2 * ng3])
        jd["ropek"] = din(f"ropek{j}", [S, 16]).rearrange("(t p) c -> p t c", p=128)
        jd["ropeq"] = din(f"ropeq{j}", [NQ, 16]).rearrange("(t p) c -> p t c", p=128)
        jd["cl"] = din(f"cl{j}", [128, 8])
        jd["y"] = nc.dram_tensor(f"y{j}", [NQ, D], F32, kind="ExternalOutput").ap()
        jd["kt_d"] = nc.dram_tensor(f"kt_d{j}", [4, 128, S], BF16).ap()
        jd["v_d"] = nc.dram_tensor(f"v_d{j}", [4, 128, S], BF16).ap()
        jd["qt_d"] = nc.dram_tensor(f"qt_d{j}", [4, 128, NQ], BF16).ap()
        jd["o_d"] = nc.dram_tensor(f"o_d{j}", [4, 128, NQ], BF16).ap()
        J.append(jd)

    class Arena:
        def __init__(self, lo, hi):
            self.lo, self.hi, self.off = lo, hi, lo

        def reset(self):
            self.off = self.lo

        def alloc(self, shape, dt):
            uid[0] += 1
            nb = int(np.prod(shape[1:])) * (4 if dt == F32 else 2)
            off = (self.off + 63) // 64 * 64
            assert off + nb <= self.hi, ("SBUF overflow", shape, off, nb, self.hi)
            self.off = off + nb
            h = nc.alloc_sbuf_tensor_at(f"t{uid[0]}", list(shape), dt, offset=off)
            return Buf(h[:])

    pers = Arena(SBUF_LO, SBUF_LO + 16 * 1024)
    ph = Arena(SBUF_LO + 16 * 1024, SBUF_HI)

    ps_h = nc.alloc_psum_tensor("ps", [128, 4096], F32)
    ps = ps_h[:]
    ps_bf = ps.bitcast(BF16)

    def bank(b, n=512, off=0):
        return ps[:, b * 512 + off: b * 512 + off + n]

    def bank_bf(b, n=1024, off=0):
        return ps_bf[:, b * 1024 + off: b * 1024 + off + n]

    banks = [Buf(bank(b)) for b in range(8)]

    def MM(out, lhsT, rhs, start, stop, tp=None):
        if tp is None:
            return lambda e: e.matmul(out, lhsT=lhsT, rhs=rhs, start=start, stop=stop)
        return lambda e: e.matmul(out, lhsT=lhsT, rhs=rhs, start=start, stop=stop, tile_position=tp)

    def TR(out, in_, ident):
        return lambda e: e.transpose(out, in_, ident)

    def EMB(fn):
        fn.embed_ok = True
        return fn

    def ACT(out, in_, func, scale=1.0, accum=None, bias=None):
        if bias is not None:
            return EMB(lambda e: e.activation(out=out, in_=in_, func=func, scale=scale, bias=bias))
        if accum is None:
            return EMB(lambda e: e.activation(out=out, in_=in_, func=func, scale=scale))
        return lambda e: e.activation(out=out, in_=in_, func=func, scale=scale, accum_out=accum)

    def TT(out, in0, in1, op):
        return EMB(lambda e: e.tensor_tensor(out=out, in0=in0, in1=in1, op=op))

    def TS(out, in0, s1, op0, s2=None, op1=None):
        if op1 is None:
            return EMB(lambda e: e.tensor_scalar(out=out, in0=in0, scalar1=s1, scalar2=None, op0=op0))
        return EMB(lambda e: e.tensor_scalar(out=out, in0=in0, scalar1=s1, scalar2=s2, op0=op0, op1=op1))

    def STT(out, in0, scalar, in1, op0, op1):
        return EMB(lambda e: e.scalar_tensor_tensor(out=out, in0=in0, scalar=scalar, in1=in1, op0=op0, op1=op1))

    def CP(out, in_):
        return EMB(lambda e: e.tensor_copy(out=out, in_=in_))

    def RED(out, in_):
        return lambda e: e.tensor_reduce(out=out, in_=in_, axis=AX.X, op=ALU.add)

    def RCP(out, in_):
        return lambda e: e.reciprocal(out=out, in_=in_)

    def MS(ap, val):
        return lambda e: e.memset(ap, val)

    def g3(ap, b):
        return ap.rearrange("p (a b) -> p a b", b=b)

    ident_f = pers.alloc([128, 128], F32)
    ident_b = pers.alloc([128, 128], BF16)
    ones_b = pers.alloc([128, 128], BF16)
    neghalf = pers.alloc([128, 512], F32)
    norm_g = pers.alloc([128, 8], F32)
    conv_l = pers.alloc([128, 12], F32)
    qg_b = pers.alloc([128, 64], F32)
    kg_b = pers.alloc([128, 64], F32)
    lamv = pers.alloc([128, 256], F32)
    subln = pers.alloc([128, 1], F32)
    sgs = pers.alloc([128, 1], F32)
    neg_lam = pers.alloc([128, 1], F32)
    lamt = pers.alloc([128, 128], F32)
    lams = pers.alloc([128, 4], F32)
    ones_f = pers.alloc([128, 128], F32)
    eps_t = pers.alloc([128, 1], F32)

    P.dma(ident_f.ap, ident_d, "cld", writes=[ident_f])
    P.dma(norm_g.ap, norm_g_d, "cld", writes=[norm_g])
    P.dma(conv_l.ap, conv_d, "cld", writes=[conv_l])
    P.dma(qg_b.ap, qg_d.partition_broadcast(128), "cld", writes=[qg_b])
    P.dma(kg_b.ap, kg_d.partition_broadcast(128), "cld", writes=[kg_b])
    P.dma(lamv.ap, lam_d.partition_broadcast(128), "cld", writes=[lamv])
    P.dma(subln.ap, subln_d, "cld", writes=[subln])
    for jd in J:
        jd["Gp"] = pers.alloc([128, 8], F32)
        jd["shiftT"] = pers.alloc([128, 8], F32)
        jd["gate_b"] = pers.alloc([128, D], F32)
        jd["hmask_s"] = pers.alloc([128, 2 * jd["ng3"]], F32)
        jd["rstd_q"] = pers.alloc([128, jd["NQ"] // 128], F32)
        P.dma(jd["hmask_s"].ap, jd["hmask"], "cld", writes=[jd["hmask_s"]])
    P.barrier()
    P.op("pool", MS(ones_b.ap, 1.0), writes=[ones_b])
    P.op("pool", MS(neghalf.ap, -0.5), writes=[neghalf])
    P.op("pool", MS(ones_f.ap, 1.0), writes=[ones_f])
    P.op("pool", MS(eps_t.ap, EPS), writes=[eps_t])
    P.op("dve", TS(conv_l.ap, conv_l.ap, 0.5, ALU.mult), reads=[conv_l], writes=[conv_l])
    P.op("dve", CP(ident_b.ap, ident_f.ap), reads=[ident_f], writes=[ident_b])
    P.op("dve", TT(lamt.ap[:, 0:64], lamv.ap[:, 0:64], lamv.ap[:, 64:128], ALU.mult), reads=[lamv], writes=[lamt])
    P.op("dve", TT(lamt.ap[:, 64:128], lamv.ap[:, 128:192], lamv.ap[:, 192:256], ALU.mult), reads=[lamv], addwrites=[lamt])
    P.op("dve", RED(lams.ap[:, 0:2], g3(lamt.ap, 64)), reads=[lamt], writes=[lams])
    P.op("act", ACT(lams.ap[:, 2:4], lams.ap[:, 0:2], AF.Exp), reads=[lams], addwrites=[lams])
    P.op("dve", STT(neg_lam.ap, lams.ap[:, 3:4], -LAM_INIT, lams.ap[:, 2:3], ALU.add, ALU.subtract), reads=[lams], writes=[neg_lam])
    P.op("dve", TS(sgs.ap, subln.ap, 0.5 * (1.0 - LAM_INIT), ALU.mult), reads=[subln], writes=[sgs])

    ph.reset()
    wada = ph.alloc([128, KC, 3072], F32)
    bada = ph.alloc([128, 3072], F32)
    for kc in range(KC):
        P.dma(wada.ap[:, kc, :], w_ada[:, kc, :], "wld", addwrites=[wada])
    P.dma(bada.ap, b_ada.partition_broadcast(128), "bld", writes=[bada])
    for j, jd in enumerate(J):
        cl = ph.alloc([128, 8], F32)
        ce = ph.alloc([128, 8], F32)
        sc = ph.alloc([128, 8], F32)
        screp = ph.alloc([128, KC, 128], F32)
        modb = ph.alloc([128, 3072], F32)
        scl = ph.alloc([128, 8], F32)
        P.dma(cl.ap, jd["cl"], f"cl{j}", writes=[cl])
        P.op("act", ACT(ce.ap, cl.ap, AF.Exp, scale=-1.0), reads=[cl], writes=[ce])
        P.op("dve", TS(ce.ap, ce.ap, 1.0, ALU.add), reads=[ce], writes=[ce])
        P.op("dve", RCP(ce.ap, ce.ap), reads=[ce], writes=[ce])
        P.op("dve", TT(sc.ap, cl.ap, ce.ap, ALU.mult), reads=[cl, ce], writes=[sc])
        P.op("dve", CP(screp.ap, sc.ap.unsqueeze(2).to_broadcast([128, KC, 128])), reads=[sc], writes=[screp])
        for cg in range(6):
            bk = banks[cg]
            fns = [MM(bank(cg), screp.ap[:, kc, :], wada.ap[:, kc, cg * 512:(cg + 1) * 512], kc == 0, kc == KC - 1) for kc in range(KC)]
            P.group("pe", fns, reads=[screp, wada], writes=[bk])
            P.op("dve", TT(modb.ap[:, cg * 512:(cg + 1) * 512], bank(cg), bada.ap[:, cg * 512:(cg + 1) * 512], ALU.add),
                 reads=[bk, bada], addwrites=[modb])
        P.op("dve", TS(jd["gate_b"].ap, modb.ap[:, 2048:3072], 0.5, ALU.mult), reads=[modb], writes=[jd["gate_b"]])
        for half in range(2):
            fns = [TR(ps[:, 6 * 512 + blk * 128: 6 * 512 + (blk + 1) * 128], modb.ap[:, half * 1024 + blk * 128: half * 1024 + (blk + 1) * 128], ident_f.ap)
                   for blk in range(8)]
            P.group("pe", fns, reads=[modb, ident_f], writes=[banks[6], banks[7]])
            src = g3(ps[:, 6 * 512: 8 * 512], 128)[:, :, 0]
            dst = jd["shiftT"] if half == 0 else scl
            P.op("dve", CP(dst.ap, src), reads=[banks[6], banks[7]], writes=[dst])
        P.op("dve", STT(jd["Gp"].ap, scl.ap, 1.0, norm_g.ap, ALU.add, ALU.mult), reads=[scl, norm_g], writes=[jd["Gp"]])
    P.barrier()

    def make_norm_bufs(G, nxn=2):
        nt = G // 128
        return dict(
            G=G, nt=nt,
            xg=Ring([ph.alloc([128, nt, D], F32) for _ in range(2)]),
            xn=Ring([ph.alloc([128, nt, D], BF16) for _ in range(nxn)]),
            hT=Ring([ph.alloc([128, KC, G], BF16) for _ in range(2)]),
            ss=Ring([ph.alloc([128, 8], F32) for _ in range(2)]),
            rs=Ring([ph.alloc([128, 8], F32) for _ in range(2)]),
            junk=ph.alloc([128, D], BF16),
            xsem=Ring(["xld0", "xld1"]),
            tb=Ring([0, 1]),
        )

    def load_group(nb, x_ap, g):
        G = nb["G"]
        xg = nb["xg"].get()
        P.dma(xg.ap, x_ap[g * G:(g + 1) * G, :].rearrange("(t p) d -> p t d", p=128), nb["xsem"].get(), writes=[xg])
        return xg

    def norm_group(nb, jd, xg, keep=None, pre=None):
        return norm_b(nb, jd, norm_a(nb, jd, xg, keep=keep, pre=pre))

    def norm_a(nb, jd, xg, keep=None, pre=None):
        G, nt = nb["G"], nb["nt"]
        xn, ss = nb["xn"].get(), nb["ss"].get()
        junk = nb["junk"]
        if pre is not None:
            rs, c0 = pre
        else:
            for i in range(nt):
                P.op("act", ACT(junk.ap, xg.ap[:, i, :], AF.Square, accum=ss.ap[:, i:i + 1]),
                     reads=[xg], addwrites=[ss] if i else (), writes=() if i else [ss])
            if keep is not None:
                rs, c0 = keep
                P.op("act", ACT(rs.ap[:, c0:c0 + nt], ss.ap[:, 0:nt], AF.Sqrt, scale=1.0 / D, bias=eps_t.ap[:, 0:1]), reads=[ss, eps_t], addwrites=[rs])
                P.op("dve", RCP(rs.ap[:, c0:c0 + nt], rs.ap[:, c0:c0 + nt]), reads=[rs], addwrites=[rs])
            else:
                rs, c0 = nb["rs"].get(), 0
                P.op("act", ACT(rs.ap[:, 0:nt], ss.ap[:, 0:nt], AF.Sqrt, scale=1.0 / D, bias=eps_t.ap[:, 0:1]), reads=[ss, eps_t], writes=[rs])
                P.op("dve", RCP(rs.ap[:, 0:nt], rs.ap[:, 0:nt]), reads=[rs], writes=[rs])
        for i in range(nt):
            P.op("dve", TS(xn.ap[:, i, :], xg.ap[:, i, :], rs.ap[:, c0 + i:c0 + i + 1], ALU.mult),
                 reads=[xg, rs], addwrites=[xn] if i else (), writes=() if i else [xn])
        return xn

    def norm_b(nb, jd, xn):
        G, nt = nb["G"], nb["nt"]
        hT = nb["hT"].get()
        for kp in range(KC // 2):
            b = nb["tb"].get()
            fns = []
            for k2 in range(2):
                kc = kp * 2 + k2
                for i in range(nt):
                    fns.append(TR(bank_bf(b, 128, k2 * G + i * 128), xn.ap[:, i, kc * 128:(kc + 1) * 128], ident_b.ap))
            P.group("pe", fns, reads=[xn, ident_b], writes=[banks[b]])
            for k2 in range(2):
                kc = kp * 2 + k2
                first = (kc == 0)
                if k2 == 0:
                    P.op("dve", TS(hT.ap[:, kc, :], bank_bf(b, G, k2 * G), jd["Gp"].ap[:, kc:kc + 1], ALU.mult, jd["shiftT"].ap[:, kc:kc + 1], ALU.add),
                         reads=[banks[b], jd["Gp"], jd["shiftT"]], writes=[hT] if first else (), addwrites=() if first else [hT])
                else:
                    P.op("act", ACT(hT.ap[:, kc, :], bank_bf(b, G, k2 * G), AF.Identity, scale=jd["Gp"].ap[:, kc:kc + 1], bias=jd["shiftT"].ap[:, kc:kc + 1]),
                         reads=[banks[b], jd["Gp"], jd["shiftT"]], addwrites=[hT])
        return hT

    ph.reset()
    wqkv = ph.alloc([128, KC, 1536], BF16)
    wst = Ring([ph.alloc([128, 1536], F32) for _ in range(2)])
    wsem = Ring(["wst0", "wst1"])
    for kc in range(KC):
        st_ = wst.get()
        P.dma(st_.ap, w_in[:, kc, 2048:3584], wsem.get(), writes=[st_])
        if kc % 2:
            P.op("act", ACT(wqkv.ap[:, kc, :], st_.ap, AF.Copy), reads=[st_], addwrites=[wqkv])
        else:
            P.op("dve", CP(wqkv.ap[:, kc, :], st_.ap), reads=[st_], addwrites=[wqkv])
    for j, jd in enumerate(J):
        jd["ropek_s"] = ph.alloc([128, jd["S"] // 128, 16], F32)
        jd["ropeq_s"] = ph.alloc([128, jd["NQ"] // 128, 16], F32)
        P.dma(jd["ropek_s"].ap, jd["ropek"], f"rk{j}", writes=[jd["ropek_s"]])
        P.dma(jd["ropeq_s"].ap, jd["ropeq"], f"rq{j}", writes=[jd["ropeq_s"]])
    nb1 = make_norm_bufs(G1)
    sqb = Ring([ph.alloc([128, 512], F32) for _ in range(4)])
    ss8 = Ring([ph.alloc([128, 8], F32) for _ in range(4)])
    r8 = Ring([ph.alloc([128, 8], F32) for _ in range(4)])
    knb = Ring([ph.alloc([128, 512], F32) for _ in range(4)])
    kn2 = Ring([ph.alloc([128, 512], F32) for _ in range(4)])
    kbb = Ring([ph.alloc([128, 512], BF16) for _ in range(8)])
    rt = Ring([ph.alloc([128, 4, 64], F32) for _ in range(4)])
    kts = Ring([ph.alloc([128, 4, G1], BF16) for _ in range(2)])
    vts = Ring([ph.alloc([128, 4, G1], BF16) for _ in range(2)])
    ksem = Ring(["kst0", "kst1"])
    vsem = Ring(["vst0", "vst1"])
    pjk = Ring([2, 4])
    pjv = Ring([3, 5])
    ptr = Ring([6, 7])

    def qk_gen(hT, i, col0, gvec, rope_s, T, stage, out):
        b = pjk.get()
        bk = banks[b]
        fns = [MM(bank(b), hT.ap[:, kc, i * 128:(i + 1) * 128], wqkv.ap[:, kc, col0:col0 + 512], kc == 0, kc == KC - 1) for kc in range(KC)]
        P.group("pe", fns, reads=[hT, wqkv], writes=[bk])
        yield
        sq, s8, rr, kn, k2, kb, r_ = sqb.get(), ss8.get(), r8.get(), knb.get(), kn2.get(), kbb.get(), rt.get()
        P.op("act", ACT(sq.ap, bank(b), AF.Square), reads=[bk], writes=[sq])
        yield
        P.op("dve", RED(s8.ap, g3(sq.ap, 64)), reads=[sq], writes=[s8])
        yield
        P.op("act", ACT(rr.ap, s8.ap, AF.Sqrt, scale=1.0 / 64, bias=eps_t.ap[:, 0:1]), reads=[s8, eps_t], writes=[rr])
        yield
        P.op("dve", RCP(rr.ap, rr.ap), reads=[rr], writes=[rr])
        kn3, k23, kb3 = g3(kn.ap, 64), g3(k2.ap, 64), g3(kb.ap, 64)
        P.op("dve", TT(kn3, g3(bank(b), 64), rr.ap.unsqueeze(2).to_broadcast([128, 8, 64]), ALU.mult), reads=[bk, rr], writes=[kn])
        yield
        P.op("dve", TT(k23, kn3, gvec.ap.unsqueeze(1).to_broadcast([128, 8, 64]), ALU.mult), reads=[kn, gvec], writes=[k2])
        yield
        P.op("act", ACT(kb3[:, :, 16:64], k23[:, :, 16:64], AF.Copy), reads=[k2], writes=[kb])
        cosb = rope_s.ap[:, T, 0:8].unsqueeze(1).to_broadcast([128, 8, 8])
        sinb = rope_s.ap[:, T, 8:16].unsqueeze(1).to_broadcast([128, 8, 8])
        x1, x2 = k23[:, :, 0:8], k23[:, :, 8:16]
        rq = [g3(r_.ap[:, q, :], 8) for q in range(4)]
        P.op("pool", TT(rq[0], x1, cosb, ALU.mult), reads=[k2, rope_s], writes=[r_])
        P.op("pool", TT(rq[1], x2, sinb, ALU.mult), reads=[k2], addwrites=[r_])
        yield
        P.op("pool", TT(rq[2], x2, cosb, ALU.mult), reads=[k2], addwrites=[r_])
        P.op("pool", TT(rq[3], x1, sinb, ALU.mult), reads=[k2], addwrites=[r_])
        yield
        P.op("pool", TT(kb3[:, :, 0:8], rq[0], rq[1], ALU.subtract), reads=[r_], addwrites=[kb])
        P.op("pool", TT(kb3[:, :, 8:16], rq[2], rq[3], ALU.add), reads=[r_], addwrites=[kb])

        def part_b():
            tb = ptr.get()
            fns = [TR(bank_bf(tb, 128, h * 128), kb.ap[:, h * 128:(h + 1) * 128], ident_b.ap) for h in range(4)]
            P.group("pe", fns, reads=[kb, ident_b], writes=[banks[tb]])
            P.op("act", ACT(stage.ap[:, :, i * 128:(i + 1) * 128], g3(bank_bf(tb, 512), 128), AF.Copy), reads=[banks[tb]], addwrites=[stage])
        out.append(part_b)

    def run_pair(gens, after_first=None):
        live = list(gens)
        first = True
        while live:
            nxt_live = []
            for gn in live:
                try:
                    next(gn)
                    nxt_live.append(gn)
                except StopIteration:
                    pass
            live = nxt_live
            if first and after_first is not None:
                after_first()
            first = False

    def v_post(hT, i, stage):
        b = pjv.get()
        bk = banks[b]
        fns = [MM(bank(b), hT.ap[:, kc, i * 128:(i + 1) * 128], wqkv.ap[:, kc, 1024:1536], kc == 0, kc == KC - 1) for kc in range(KC)]
        P.group("pe", fns, reads=[hT, wqkv], writes=[bk])
        P.op("dve", CP(stage.ap[:, :, i * 128:(i + 1) * 128], g3(bank(b), 128)), reads=[bk], addwrites=[stage])

    pending = []
    xn_nxt = None
    hT_next = None
    for jd in J:
        S, NQ = jd["S"], jd["NQ"]
        work = [("kv", g) for g in range(S // G1)] + [("q", g) for g in range(NQ // G1)]
        xsrc = {"kv": jd["xf"], "q": jd["xq"]}
        def keep_of(w):
            return (jd["rstd_q"], w[1] * (G1 // 128)) if w[0] == "q" else None

        xg_cur = load_group(nb1, xsrc[work[0][0]], work[0][1])
        hT_next = norm_b(nb1, jd, norm_a(nb1, jd, xg_cur, keep=keep_of(work[0])))
        for wi, (kind, g) in enumerate(work):
            hT = hT_next
            more = wi + 1 < len(work)
            if more:
                xg_nxt = load_group(nb1, xsrc[work[wi + 1][0]], work[wi + 1][1])

            def hook(i, wi=wi, more=more):
                nonlocal hT_next, xn_nxt
                if not more:
                    return
                if i == 0:
                    xn_nxt = norm_a(nb1, jd, xg_nxt, keep=keep_of(work[wi + 1]))
                if i == 2:
                    hT_next = norm_b(nb1, jd, xn_nxt)
            ks = kts.get()
            P.op("pool", MS(ks.ap[:, 0, 0:2], 0.0), writes=[ks])
            if kind == "kv":
                vs = vts.get()
                P.op("pool", MS(vs.ap[:, 0, 0:2], 0.0), writes=[vs])
                for i0 in range(0, G1 // 128, 2):
                    outs = []
                    gens = [qk_gen(hT, i, 512, kg_b, jd["ropek_s"], g * (G1 // 128) + i, ks, outs) for i in (i0, i0 + 1)]

                    def vproj(i0=i0, hT=hT, vs=vs):
                        v_post(hT, i0, vs)
                        v_post(hT, i0 + 1, vs)
                        if i0 == 0:
                            hook(0)
                    run_pair(gens, after_first=vproj)
                    while pending:
                        pending.pop(0)()
                    pending.extend(outs)
                    if i0 == 0:
                        hook(2)

                def stores(ks=ks, vs=vs, g=g, jd=jd):
                    P.dma(jd["kt_d"][:, :, g * G1:(g + 1) * G1].rearrange("h p s -> p h s"), ks.ap, ksem.get(), reads=[ks])
                    P.dma(jd["v_d"][:, :, g * G1:(g + 1) * G1].rearrange("h p s -> p h s"), vs.ap, vsem.get(), reads=[vs])
                pending.append(stores)
            else:
                for i0 in range(0, G1 // 128, 2):
                    outs = []
                    gens = [qk_gen(hT, i, 0, qg_b, jd["ropeq_s"], g * (G1 // 128) + i, ks, outs) for i in (i0, i0 + 1)]
                    run_pair(gens, after_first=(lambda: hook(0)) if i0 == 0 else None)
                    while pending:
                        pending.pop(0)()
                    pending.extend(outs)
                    if i0 == 0:
                        hook(2)

                def stores(ks=ks, g=g, jd=jd):
                    P.dma(jd["qt_d"][:, :, g * G1:(g + 1) * G1].rearrange("h p s -> p h s"), ks.ap, ksem.get(), reads=[ks])
                pending.append(stores)
    while pending:
        pending.pop(0)()
    P.barrier()

    ph.reset()
    SMAX = max(jd["S"] for jd in J)
    NQMAX = max(jd["NQ"] for jd in J)
    ktb = Ring([ph.alloc([128, SMAX], BF16) for _ in range(2)])
    vtb = Ring([ph.alloc([128, SMAX], BF16) for _ in range(2)])
    qtb = Ring([ph.alloc([128, NQMAX], BF16) for _ in range(2)])
    kvsem = Ring(["kvl0", "kvl1"])
    Pb = Ring([ph.alloc([128, 1024], BF16) for _ in range(3)])
    Sb = Ring([0, 2])
    rinv = ph.alloc([128, 1024], F32)
    DW = 512
    raccs = Ring([(ph.alloc([128, DW], F32), None) for _ in range(2)])
    osb0 = ph.alloc([128, 512], F32)
    osb1 = ph.alloc([128, 512], F32)
    rsb = ph.alloc([128, 1024], F32)
    deferred = []
    t0b = ph.alloc([128, 512], F32)
    t1b = ph.alloc([128, 512], F32)
    ost = Ring([ph.alloc([128, QT], BF16) for _ in range(2)])
    osem = Ring(["ost0", "ost1"])

    def load_head(jd, h):
        S, NQ = jd["S"], jd["NQ"]
        kt, vt, qt_, sem = ktb.get(), vtb.get(), qtb.get(), kvsem.get()
        nsp = max(1, S // 4096)
        w = S // nsp
        for sp in range(nsp):
            P.dma(kt.ap[:, sp * w:(sp + 1) * w], jd["kt_d"][h, :, sp * w:(sp + 1) * w], sem, writes=[kt] if sp == 0 else (), addwrites=[kt] if sp else ())
        for sp in range(nsp):
            P.dma(vt.ap[:, sp * w:(sp + 1) * w], jd["v_d"][h, :, sp * w:(sp + 1) * w], sem, writes=[vt] if sp == 0 else (), addwrites=[vt] if sp else ())
        tq = P.dma(qt_.ap[:, 0:NQ], jd["qt_d"][h, :, :], sem, writes=[qt_])
        kt.wr = {tq[0]: tq[1]}
        vt.wr = {tq[0]: tq[1]}
        return kt, vt, qt_

    heads = [(jd, h) for jd in J for h in range(4)]
    nxt = load_head(*heads[0])
    for hi, (jd, h) in enumerate(heads):
        kt, vt, qt_ = nxt
        if hi + 1 < len(heads):
            nxt = load_head(*heads[hi + 1])
        S, NQ = jd["S"], jd["NQ"]
        nkt = S // 128
        for qi in range(NQ // QT):
            racc, raccp = raccs.get()

            def qk(u):
                sb = Sb.get()
                fns = [MM(bank(sb), kt.ap[0:64, u * 128:(u + 1) * 128], qt_.ap[0:64, qi * QT:(qi + 1) * QT], True, True, (0, 0)),
                       MM(bank(sb + 1), kt.ap[64:128, u * 128:(u + 1) * 128], qt_.ap[64:128, qi * QT:(qi + 1) * QT], True, True, (64, 0))]
                if u >= 2:
                    fns[0].embed_ok = True
                P.group("pe", fns, reads=[kt, qt_], writes=[banks[sb], banks[sb + 1]])
                pb = Pb.get()
                P.op("act", ACT(pb.ap, ps[:, sb * 512: sb * 512 + 1024], AF.Exp, scale=0.125), reads=[banks[sb], banks[sb + 1]], writes=[pb])
                return pb

            def pv(u, pb):
                vcol = u * 128
                first, last = (u == 0), (u == nkt - 1)
                fns = [MM(bank(4), vt.ap[:, vcol:vcol + 128], pb.ap[:, 0:512], first, last),
                       MM(bank(5), vt.ap[:, vcol:vcol + 128], pb.ap[:, 512:1024], first, last),
                       MM(bank(7, 1024 - DW, DW - 512), ones_b.ap, pb.ap[:, DW:1024], first, last)]
                acc = [banks[4], banks[5], banks[7]]
                if first:
                    P.group("pe", fns, reads=[vt, pb, ones_b], writes=acc)
                    P.op("dve", CP(racc.ap, pb.ap[:, 0:DW]), reads=[pb], writes=[racc])
                else:
                    fns[0].embed_ok = True
                    P.group("pe", fns, reads=[vt, pb, ones_b], addwrites=acc)
                    P.op("dve", TT(racc.ap, racc.ap, pb.ap[:, 0:DW], ALU.add), reads=[pb, racc], writes=[racc])

            pbs = {0: qk(0)}
            if nkt > 1:
                pbs[1] = qk(1)
            for u in range(nkt):
                if u + 2 < nkt:
                    pbs[u + 2] = qk(u + 2)
                pv(u, pbs.pop(u))
                if deferred:
                    deferred.pop(0)()
            while deferred:
                deferred.pop(0)()
            P.op("dve", CP(osb0.ap, bank(4)), reads=[banks[4]], writes=[osb0])
            P.op("dve", CP(osb1.ap, bank(5)), reads=[banks[5]], writes=[osb1])
            P.op("dve", CP(rsb.ap[:, DW:1024], bank(7, 1024 - DW, DW - 512)), reads=[banks[7]], writes=[rsb])

            def tot(racc=racc, raccp=raccp):
                P.group("pe", [MM(bank(6), ones_f.ap, racc.ap[:, 0:512], True, True)], reads=[racc, ones_f], writes=[banks[6]])
                P.op("dve", CP(rsb.ap[:, 0:512], bank(6)), reads=[banks[6]], addwrites=[rsb])
                if DW > 512:
                    P.group("pe", [MM(bank(6, DW - 512), ones_f.ap, racc.ap[:, 512:DW], True, True)], reads=[racc, ones_f], writes=[banks[6]])
                    P.op("dve", CP(rsb.ap[:, 512:DW], bank(6, DW - 512)), reads=[banks[6]], addwrites=[rsb])
            nch = 16 if nkt >= 32 else 8
            cw = 1024 // nch

            def mk_rcp(c):
                def f():
                    P.op("dve", RCP(rinv.ap[:, c * cw:(c + 1) * cw], rsb.ap[:, c * cw:(c + 1) * cw]), reads=[rsb],
                         writes=[rinv] if c == 0 else (), addwrites=[rinv] if c else ())
                return f

            def fin(jd=jd, h=h, qi=qi):
                os_ = ost.get()
                P.op("dve", TT(t0b.ap, osb0.ap, rinv.ap[:, 0:512], ALU.mult), reads=[osb0, rinv], writes=[t0b])
                P.op("dve", TT(t1b.ap, osb1.ap, rinv.ap[:, 512:1024], ALU.mult), reads=[osb1, rinv], writes=[t1b])
                P.op("dve", STT(os_.ap, t1b.ap, neg_lam.ap[:, 0:1], t0b.ap, ALU.mult, ALU.add), reads=[t0b, t1b, neg_lam], writes=[os_])
                P.dma(jd["o_d"][h, :, qi * QT:(qi + 1) * QT], os_.ap, osem.get(), reads=[os_])
            deferred.extend([tot] + [mk_rcp(c) for c in range(nch)] + [fin])
    while deferred:
        deferred.pop(0)()
    P.barrier()

    ph.reset()
    w3 = ph.alloc([128, KC, 4608], BF16)
    wco = ph.alloc([128, 4, D], BF16)
    wao = ph.alloc([128, 4, D], BF16)
    wo = ph.alloc([128, KC, D], BF16)
    wst3 = Ring([ph.alloc([128, 1024], F32) for _ in range(2)])
    wsem3 = Ring(["wst0", "wst1"])
    cast_eng = Ring(["act", "dve"])

    def load_cast(dst_ap, src_ap, n, dstbuf):
        st_ = wst3.get()
        P.dma(st_.ap[:, 0:n], src_ap, wsem3.get(), writes=[st_])
        ce_ = cast_eng.get()
        if ce_ == "act":
            P.op("act", ACT(dst_ap, st_.ap[:, 0:n], AF.Copy), reads=[st_], addwrites=[dstbuf])
        else:
            P.op("dve", CP(dst_ap, st_.ap[:, 0:n]), reads=[st_], addwrites=[dstbuf])

    for kc in range(KC):
        for c0, s0, n in ((0, 0, 1024), (1024, 1024, 1024), (2048, 3584, 1024), (3072, 4608, 1024), (4096, 5632, 512)):
            load_cast(w3.ap[:, kc, c0:c0 + n], w_in[:, kc, s0:s0 + n], n, w3)
        load_cast(wo.ap[:, kc, :], wo_d[:, kc, :], 1024, wo)
    for m in range(4):
        load_cast(wco.ap[:, m, :], wco_d[:, m, :], 1024, wco)
        load_cast(wao.ap[:, m, :], wao_d[:, m, :], 1024, wao)
    nb3 = make_norm_bufs(G3, 1)
    NT3 = G3 // 128
    oin = Ring([ph.alloc([128, 4, G3], BF16) for _ in range(2)])
    oisem = Ring(["oin0", "oin1"])
    yT = ph.alloc([128, 4, G3], BF16)
    ogT = ph.alloc([128, 4, G3], BF16)
    mT = ph.alloc([128, KC, G3], BF16)
    yo = Ring([ph.alloc([128, D], F32) for _ in range(2)])
    yosem = Ring(["yo0", "yo1"])
    tmpf = Ring([ph.alloc([128, G3], F32) for _ in range(10)])
    tmpg = Ring([ph.alloc([128, 512], F32) for _ in range(2)])
    osq = ph.alloc([128, 4, G3], BF16)
    rsa = ph.alloc([128, 4 * G3], F32)
    uext = Ring([ph.alloc([128, G3 + 2], F32) for _ in range(2)])
    pr = Ring([2, 3, 4, 5])
    pr2 = Ring([6, 7])
    xh = yo.items[0]
    xhn = ph.alloc([32, D], BF16)
    hTh = ph.alloc([128, KC, 32], BF16)
    ssh = ph.alloc([32, 2], F32)
    uh = ph.alloc([128, 4, 32], F32)
    cxh = ph.alloc([128, 32], F32)

    pe_embed = [False]

    def proj(col, hT_buf, hT_ap, n):
        b = pr.get()
        fns = [MM(bank(b, n), w3.ap[:, kc, col:col + 128], hT_ap[:, kc, :], kc == 0, kc == KC - 1) for kc in range(KC)]
        if pe_embed[0]:
            fns[0].embed_ok = True
        P.group("pe", fns, reads=[hT_buf, w3], writes=[banks[b]])
        return b

    def tanh_half(b, n, dst):
        P.op("act", ACT(dst.ap[:, 0:n], bank(b, n), AF.Tanh, scale=0.5), reads=[banks[b]], writes=[dst])

    def load3(jd, g):
        xg = load_group(nb3, jd["xq"], g)
        oi = oin.get()
        P.dma(oi.ap, jd["o_d"][:, :, g * G3:(g + 1) * G3].rearrange("h p s -> p h s"), oisem.get(), writes=[oi])
        return xg, oi

    for j, jd in enumerate(J):
        NQ, ng3 = jd["NQ"], jd["ng3"]
        nh = 2 * ng3
        P.dma(xh.ap[0:nh, :], jd["halo"], f"hl{j}", writes=[xh])
        P.op("act", ACT(nb3["junk"].ap[0:nh, :], xh.ap[0:nh, :], AF.Square, accum=ssh.ap[0:nh, 0:1]), reads=[xh], writes=[ssh])
        P.op("act", ACT(ssh.ap[0:nh, 1:2], ssh.ap[0:nh, 0:1], AF.Sqrt, scale=1.0 / D, bias=eps_t.ap[0:nh, 0:1]), reads=[ssh, eps_t], writes=[ssh])
        P.op("dve", RCP(ssh.ap[0:nh, 1:2], ssh.ap[0:nh, 1:2]), reads=[ssh], writes=[ssh])
        P.op("dve", TS(xhn.ap[0:nh, :], xh.ap[0:nh, :], ssh.ap[0:nh, 1:2], ALU.mult), reads=[xh, ssh], writes=[xhn])
        fns = [TR(bank_bf(0, nh, kc * 32), xhn.ap[0:nh, kc * 128:(kc + 1) * 128], ident_b.ap[0:nh, 0:nh]) for kc in range(KC)]
        P.group("pe", fns, reads=[xhn, ident_b], writes=[banks[0]])
        for kc in range(KC):
            P.op("dve", TS(hTh.ap[:, kc, 0:nh], bank_bf(0, nh, kc * 32), jd["Gp"].ap[:, kc:kc + 1], ALU.mult, jd["shiftT"].ap[:, kc:kc + 1], ALU.add),
                 reads=[banks[0], jd["Gp"], jd["shiftT"]], writes=[hTh] if kc == 0 else (), addwrites=[hTh] if kc else ())
        for m in range(4):
            bc = proj(512 + m * 128, hTh, hTh.ap[:, :, 0:nh], nh)
            bx = proj(1024 + m * 128, hTh, hTh.ap[:, :, 0:nh], nh)
            P.op("act", ACT(cxh.ap[:, 0:nh], bank(bx, nh), AF.Copy), reads=[banks[bx]], writes=[cxh])
            P.op("dve", TT(uh.ap[:, m, 0:nh], bank(bc, nh), cxh.ap[:, 0:nh], ALU.mult), reads=[banks[bc], cxh], writes=[uh] if m == 0 else (),
                 addwrites=[uh] if m else ())
            P.op("dve", TT(uh.ap[:, m, 0:nh], uh.ap[:, m, 0:nh], jd["hmask_s"].ap[:, 0:nh], ALU.mult), reads=[uh, jd["hmask_s"]], addwrites=[uh])
        cur = load3(jd, 0)
        hT_n3 = norm_group(nb3, jd, cur[0], pre=(jd["rstd_q"], 0))
        for g in range(ng3):
            xg, oi = cur
            hT = hT_n3
            pe_embed[0] = g >= 1
            more3 = g + 1 < ng3
            if more3:
                nxt = load3(jd, g + 1)
            P.op("dve", TT(osq.ap, oi.ap, oi.ap, ALU.mult), reads=[oi], writes=[osq])
            P.group("pe", [MM(bank(6 + hh // 2, G3, (hh % 2) * G3), ones_b.ap, osq.ap[:, hh, :], True, True) for hh in range(4)],
                    reads=[osq, ones_b], writes=[banks[6], banks[7]])
            P.op("act", ACT(rsa.ap, ps[:, 6 * 512: 8 * 512], AF.Sqrt, scale=1.0 / 128, bias=eps_t.ap[:, 0:1]), reads=[banks[6], banks[7], eps_t], writes=[rsa])
            P.op("dve", RCP(rsa.ap, rsa.ap), reads=[rsa], writes=[rsa])
            for m in range(4):
                bc = proj(512 + m * 128, hT, hT.ap, G3)
                bx = proj(1024 + m * 128, hT, hT.ap, G3)
                cxs, ue, cv = tmpf.get(), uext.get(), tmpf.get()
                P.op("act", ACT(cxs.ap, bank(bx, G3), AF.Copy), reads=[banks[bx]], writes=[cxs])
                P.op("dve", TT(ue.ap[:, 1:G3 + 1], bank(bc, G3), cxs.ap, ALU.mult), reads=[banks[bc], cxs], writes=[ue])
                P.op("dve", CP(ue.ap[:, 0:1], uh.ap[:, m, 2 * g:2 * g + 1]), reads=[uh], addwrites=[ue])
                P.op("dve", CP(ue.ap[:, G3 + 1:G3 + 2], uh.ap[:, m, 2 * g + 1:2 * g + 2]), reads=[uh], addwrites=[ue])
                P.op("dve", TS(cv.ap, ue.ap[:, 0:G3], conv_l.ap[:, m * 3:m * 3 + 1], ALU.mult), reads=[ue, conv_l], writes=[cv])
                P.op("dve", STT(cv.ap, ue.ap[:, 1:G3 + 1], conv_l.ap[:, m * 3 + 1:m * 3 + 2], cv.ap, ALU.mult, ALU.add), reads=[ue, cv], writes=[cv])
                P.op("dve", STT(cv.ap, ue.ap[:, 2:G3 + 2], conv_l.ap[:, m * 3 + 2:m * 3 + 3], cv.ap, ALU.mult, ALU.add), reads=[ue, cv], writes=[cv])
                bb_ = proj(0 + m * 128, hT, hT.ap, G3)
                bz = proj(1536 + m * 128, hT, hT.ap, G3)
                th, y1 = tmpf.get(), tmpf.get()
                tanh_half(bz, G3, th)
                P.op("dve", STT(th.ap, th.ap, 1.0, bank(bz, G3), ALU.add, ALU.mult), reads=[banks[bz], th], writes=[th])
                P.op("dve", TT(y1.ap, bank(bb_, G3), cv.ap, ALU.mult), reads=[banks[bb_], cv], writes=[y1])
                P.op("dve", TT(yT.ap[:, m, :], y1.ap, th.ap, ALU.mult), reads=[y1, th], writes=[yT] if m == 0 else (), addwrites=[yT] if m else ())
            for h in range(4):
                ba = proj(2048 + h * 128, hT, hT.ap, G3)
                th, on = tmpf.get(), tmpf.get()
                tanh_half(ba, G3, th)
                P.op("dve", STT(th.ap, th.ap, 1.0, bank(ba, G3), ALU.add, ALU.mult), reads=[banks[ba], th], writes=[th])
                P.op("dve", STT(on.ap, oi.ap[:, h, :], sgs.ap[:, 0:1], rsa.ap[:, h * G3:(h + 1) * G3], ALU.mult, ALU.mult), reads=[oi, sgs, rsa], writes=[on])
                P.op("dve", TT(ogT.ap[:, h, :], on.ap, th.ap, ALU.mult), reads=[on, th], writes=[ogT] if h == 0 else (), addwrites=[ogT] if h else ())
            if more3:
                xn_n3 = norm_a(nb3, jd, nxt[0], pre=(jd["rstd_q"], (g + 1) * NT3))
            for f in range(KC):
                bga = proj(2560 + f * 128, hT, hT.ap, G3)
                bgb = proj(3584 + f * 128, hT, hT.ap, G3)
                bA = pr2.get()
                P.group("pe", [MM(bank(bA, G3), wco.ap[:, m, f * 128:(f + 1) * 128], yT.ap[:, m, :], m == 0, m == 3) for m in range(4)],
                        reads=[yT, wco], writes=[banks[bA]])
                bB = pr2.get()
                P.group("pe", [MM(bank(bB, G3), wao.ap[:, h, f * 128:(f + 1) * 128], ogT.ap[:, h, :], h == 0, h == 3) for h in range(4)],
                        reads=[ogT, wao], writes=[banks[bB]])
                sa, sb_, ma, mb = tmpf.get(), tmpf.get(), tmpf.get(), tmpf.get()
                tanh_half(bga, G3, sa)
                tanh_half(bgb, G3, sb_)
                P.op("dve", STT(ma.ap, sa.ap, 1.0, bank(bA, G3), ALU.add, ALU.mult), reads=[banks[bA], sa], writes=[ma])
                P.op("dve", STT(mb.ap, sb_.ap, 1.0, bank(bB, G3), ALU.add, ALU.mult), reads=[banks[bB], sb_], writes=[mb])
                P.op("dve", TT(mT.ap[:, f, :], ma.ap, mb.ap, ALU.add), reads=[ma, mb], writes=[mT] if f == 0 else (), addwrites=[mT] if f else ())
            if more3:
                hT_n3 = norm_b(nb3, jd, xn_n3)
                cur = nxt
            for i in range(NT3):
                yo_ = yo.get()
                for hf in range(2):
                    bo = pr2.get()
                    P.group("pe", [MM(bank(bo), mT.ap[:, f, i * 128:(i + 1) * 128], wo.ap[:, f, hf * 512:(hf + 1) * 512], f == 0, f == KC - 1) for f in range(KC)],
                            reads=[mT, wo], writes=[banks[bo]])
                    tg = tmpg.get()
                    P.op("dve", TT(tg.ap, bank(bo), jd["gate_b"].ap[:, hf * 512:(hf + 1) * 512], ALU.mult), reads=[banks[bo], jd["gate_b"]], writes=[tg])
                    P.op("dve", TT(yo_.ap[:, hf * 512:(hf + 1) * 512], tg.ap, xg.ap[:, i, hf * 512:(hf + 1) * 512], ALU.add),
                         reads=[tg, xg], writes=[yo_] if hf == 0 else (), addwrites=[yo_] if hf else ())
                P.dma(jd["y"][g * G3 + i * 128: g * G3 + (i + 1) * 128, :], yo_.ap, yosem.get(), reads=[yo_])
    P.barrier()
    P.replay(nc)
    return nc


ROT_HALF = 8
ROPE_THETA = 500000.0


def _rope_table(pos):
    inv = (np.float32(ROPE_THETA) ** (-np.arange(ROT_HALF, dtype=np.float32) / np.float32(ROT_HALF))).astype(np.float32)
    ang = pos.astype(np.float32)[:, None] * inv[None, :]
    return np.ascontiguousarray(np.concatenate([np.cos(ang), np.sin(ang)], axis=1).astype(np.float32))


def _job_inputs(j, x_seq, q0, NQ, cvec):
    S = x_seq.shape[0]
    ng3 = NQ // G3
    halo = np.zeros((2 * ng3, D), np.float32)
    hmask = np.zeros((2 * ng3,), np.float32)
    for g in range(ng3):
        for side, idx in ((0, q0 + g * G3 - 1), (1, q0 + (g + 1) * G3)):
            if 0 <= idx < S:
                halo[2 * g + side] = x_seq[idx]
                hmask[2 * g + side] = 1.0
    rope = _rope_table(np.arange(S))
    return {
        f"xf{j}": np.ascontiguousarray(x_seq),
        f"xq{j}": np.ascontiguousarray(x_seq[q0:q0 + NQ]),
        f"halo{j}": halo,
        f"hmask{j}": np.ascontiguousarray(np.broadcast_to(hmask[None, :], (128, 2 * ng3))),
        f"ropek{j}": rope,
        f"ropeq{j}": np.ascontiguousarray(rope[q0:q0 + NQ]),
        f"cl{j}": np.ascontiguousarray(cvec.reshape(8, 128).T),
    }


def _shared_inputs(norm_g, w_ada, b_ada, w_in, conv_w, q_norm_g, k_norm_g, lam_q1, lam_k1, lam_q2, lam_k2, subln_g,
                   w_conv_out, w_attn_out, w_out):
    f = lambda a: np.ascontiguousarray(np.asarray(a, dtype=np.float32))
    return {
        "w_in": f(w_in[0]), "w_ada": f(w_ada[0]), "b_ada": f(b_ada[0]),
        "norm_g_l": f(np.asarray(norm_g[0]).reshape(8, 128).T),
        "conv_l": f(np.asarray(conv_w[0]).reshape(3, 4, 128).transpose(2, 1, 0).reshape(128, 12)),
        "qg": f(q_norm_g[0]), "kg": f(k_norm_g[0]),
        "lamv": f(np.concatenate([np.asarray(lam_q1[0]), np.asarray(lam_k1[0]), np.asarray(lam_q2[0]), np.asarray(lam_k2[0])])),
        "subln": f(np.asarray(subln_g[0]).reshape(128, 1)),
        "w_conv_out": f(w_conv_out[0]), "w_attn_out": f(w_attn_out[0]), "w_out": f(w_out[0]),
        "ident": np.eye(128, dtype=np.float32),
    }


def kernel(x_prompt, x_sample, c_prompt, c_sample, norm_g, w_ada, b_ada, w_in, conv_w, q_norm_g, k_norm_g,
           lam_q1, lam_k1, lam_q2, lam_k2, subln_g, w_conv_out, w_attn_out, w_out):
    x_prompt = np.asarray(x_prompt, dtype=np.float32)
    x_sample = np.asarray(x_sample, dtype=np.float32)
    c_prompt = np.asarray(c_prompt, dtype=np.float32)
    c_sample = np.asarray(c_sample, dtype=np.float32)
    n = 8
    B, S0, _ = x_prompt.shape
    B1, S1, _ = x_sample.shape
    NQ1 = S1 * B1 // n
    per_seq = S1 // NQ1
    nc = build([dict(S=S0, NQ=S0), dict(S=S1, NQ=NQ1)])
    shared = _shared_inputs(norm_g, w_ada, b_ada, w_in, conv_w, q_norm_g, k_norm_g, lam_q1, lam_k1, lam_q2, lam_k2, subln_g,
                            w_conv_out, w_attn_out, w_out)
    in_maps = []
    for c in range(n):
        m = dict(shared)
        m.update(_job_inputs(0, x_prompt[c], 0, S0, c_prompt[c]))
        b, jq = c // per_seq, c % per_seq
        m.update(_job_inputs(1, x_sample[b], jq * NQ1, NQ1, c_sample[b]))
        in_maps.append(m)
    res = run_bass_kernel_spmd(nc, in_maps, core_ids=list(range(n)))
    y_prompt = np.stack([np.asarray(res.results[c]["y0"], dtype=np.float32) for c in range(n)], axis=0)
    y_sample = np.zeros((B1, S1, D), np.float32)
    for c in range(n):
        b, jq = c // per_seq, c % per_seq
        y_sample[b, jq * NQ1:(jq + 1) * NQ1] = np.asarray(res.results[c]["y1"], dtype=np.float32)
    return (y_prompt, y_sample)
```
